# Optimizing a Trainium2 kernel written in Bass

```python
import math
import jax, jax.numpy as jnp
from jax import lax
import numpy as np

D_MODEL = 1024
BATCH = 2
SEQ = 8192
DEPTH = 1
DEC_BATCH = 128
DEC_SEQ = 4
PAST_LEN = 8192
PAGE_SIZE = 128

HEAD_DIM = 64
ATTN_WIDTH = D_MODEL // 2
N_Q_HEADS = ATTN_WIDTH // HEAD_DIM
N_KV_HEADS = N_Q_HEADS // 4
Q_PER_KV = N_Q_HEADS // N_KV_HEADS
KV_WIDTH = N_KV_HEADS * HEAD_DIM
WINDOW = 128
BLOCK = 128
N_BUCKETS = 32
MAX_DISTANCE = 128
SSM_WIDTH = D_MODEL - ATTN_WIDTH
SSM_GROUP = 16
N_SSM_GROUPS = SSM_WIDTH // SSM_GROUP
SSM_STATE = 64
D_FF = 4 * D_MODEL
D_IN_PROJ = ATTN_WIDTH + 2 * KV_WIDTH + SSM_WIDTH
DT_MIN = 0.001
DT_MAX = 0.1
EPS = 1e-6
NEG = -1e30

kernel_name = "hymba_swa_sink_s5_decode_step"


def rmsnorm(x, g):
    xf = x.astype(jnp.float32)
    y = xf * lax.rsqrt(jnp.mean(xf * xf, axis=-1, keepdims=True) + EPS)
    return (y * g.astype(jnp.float32)).astype(x.dtype)


def rel_bucket(dist):
    n = jnp.maximum(dist, 0)
    max_exact = N_BUCKETS // 2
    nf = jnp.maximum(n, max_exact).astype(jnp.float32)
    large = max_exact + (jnp.log(nf / max_exact) / math.log(MAX_DISTANCE / max_exact)
                         * (N_BUCKETS - max_exact)).astype(jnp.int32)
    large = jnp.minimum(large, N_BUCKETS - 1)
    return jnp.where(n < max_exact, n, large)


def band_attention(q, k, v, dist, valid, sinks, rel_table):
    bias = rel_table[rel_bucket(dist)]
    bias = jnp.transpose(bias, (2, 0, 1)).reshape(N_KV_HEADS, Q_PER_KV, *dist.shape).astype(jnp.float32)
    s = jnp.einsum('bnqgrd,bnkgd->bngrqk', q, k).astype(jnp.float32) * (HEAD_DIM ** -0.5) + bias
    s = jnp.where(valid[:, None, None], s, NEG)
    sink = sinks.astype(jnp.float32).reshape(N_KV_HEADS, Q_PER_KV, 1)
    m = jnp.maximum(jnp.max(s, axis=-1), sink)
    p = jnp.exp(s - m[..., None])
    denom = jnp.sum(p, axis=-1) + jnp.exp(sink - m)
    p = (p / denom[..., None]).astype(v.dtype)
    return jnp.einsum('bngrqk,bnkgd->bnqgrd', p, v)


def attn_prompt(q, k, v, sinks, rel_table):
    b, L = q.shape[:2]
    nb = L // BLOCK
    qb = q.reshape(b, nb, BLOCK, N_KV_HEADS, Q_PER_KV, HEAD_DIM)
    kb = k.reshape(b, nb, BLOCK, N_KV_HEADS, HEAD_DIM)
    vb = v.reshape(b, nb, BLOCK, N_KV_HEADS, HEAD_DIM)

    def band(t):
        prev = jnp.concatenate([jnp.zeros_like(t[:, :1]), t[:, :-1]], axis=1)
        return jnp.concatenate([prev, t], axis=2)

    i = jnp.arange(BLOCK)[:, None]
    j = jnp.arange(2 * BLOCK)[None, :]
    dist = i + BLOCK - j
    in_band = (dist >= 0) & (dist < WINDOW)
    first = (jnp.arange(nb) == 0)[:, None, None]
    valid = in_band[None] & ~(first & (j < BLOCK)[None])
    o = band_attention(qb, band(kb), band(vb), dist, valid, sinks, rel_table)
    return o.reshape(b, L, ATTN_WIDTH)


def attn_sample(q, k_new, v_new, k_cache, v_cache, sinks, rel_table):
    b, T = q.shape[:2]
    keys = jnp.concatenate([k_cache, k_new], axis=1)
    vals = jnp.concatenate([v_cache, v_new], axis=1)
    i = jnp.arange(T)[:, None]
    key_pos = jnp.arange(WINDOW + T)[None, :] - WINDOW
    dist = i - key_pos
    valid = ((dist >= 0) & (dist < WINDOW))[None]
    o = band_attention(q[:, None], keys[:, None], vals[:, None], dist, valid, sinks, rel_table)
    return o.reshape(b, T, ATTN_WIDTH), keys[:, T:], vals[:, T:]


def s5_mixer(u, h0, a_re, a_im, log_step, b_re, b_im, c_re, c_im, d, w_glu, b_glu):
    f32 = jnp.float32
    b, L = u.shape[:2]
    uf = u.astype(f32).reshape(b, L, N_SSM_GROUPS, SSM_GROUP)
    lam = lax.complex(a_re.astype(f32), a_im.astype(f32))
    step = jnp.exp(log_step.astype(f32))[:, None]
    lam_bar = jnp.exp(lam * step)
    b_bar = ((lam_bar - 1.0) / lam)[..., None] * lax.complex(b_re.astype(f32), b_im.astype(f32))
    bu = jnp.einsum('blgc,gpc->blgp', uf.astype(jnp.complex64), b_bar)
    if h0 is not None:
        bu = bu.at[:, 0].add(lam_bar * h0)
    a = jnp.broadcast_to(lam_bar, bu.shape)

    def combine(e1, e2):
        a1, b1 = e1
        a2, b2 = e2
        return a1 * a2, a2 * b1 + b2

    _, h = lax.associative_scan(combine, (a, bu), axis=1)
    c = lax.complex(c_re.astype(f32), c_im.astype(f32))
    y = jnp.real(jnp.einsum('blgp,gcp->blgc', h, c)) + d.astype(f32) * uf
    y = y.reshape(b, L, SSM_WIDTH)
    g = jax.nn.gelu(y)
    out = g * jax.nn.sigmoid(g @ w_glu.astype(f32) + b_glu.astype(f32))
    return out.astype(u.dtype), h[:, -1]


def layer_forward(x, lw, rel_table, kv_cache, ssm_h0):
    b, L, _ = x.shape
    h = rmsnorm(x, lw['norm_mix'])
    proj = h @ lw['w_in']
    q, k, v, u = jnp.split(proj, [ATTN_WIDTH, ATTN_WIDTH + KV_WIDTH, ATTN_WIDTH + 2 * KV_WIDTH], axis=-1)
    q = q.reshape(b, L, N_KV_HEADS, Q_PER_KV, HEAD_DIM)
    k = k.reshape(b, L, N_KV_HEADS, HEAD_DIM)
    v = v.reshape(b, L, N_KV_HEADS, HEAD_DIM)
    if kv_cache is None:
        o_attn = attn_prompt(q, k, v, lw['sinks'], rel_table)
        new_k, new_v = k[:, L - WINDOW:], v[:, L - WINDOW:]
    else:
        o_attn, new_k, new_v = attn_sample(q, k, v, kv_cache[0], kv_cache[1], lw['sinks'], rel_table)
    o_ssm, h_last = s5_mixer(u, ssm_h0, lw['a_re'], lw['a_im'], lw['log_step'], lw['b_re'], lw['b_im'],
                             lw['c_re'], lw['c_im'], lw['d'], lw['w_glu'], lw['b_glu'])
    merged = jnp.concatenate([rmsnorm(o_attn, lw['norm_attn_out']), rmsnorm(o_ssm, lw['norm_ssm_out'])], axis=-1)
    x = x + merged @ lw['w_out']
    hm = rmsnorm(x, lw['norm_mlp'])
    x = x + jnp.square(jax.nn.relu(hm @ lw['w_up'])) @ lw['w_down']
    return x, new_k, new_v, jnp.real(h_last), jnp.imag(h_last)


def setup_inputs(seed: int = 0) -> dict:
    key = jax.random.key(seed)
    ks = jax.random.split(key, 32)
    f32 = jnp.float32
    nrm = lambda k, shape, s: jax.random.normal(k, shape, f32) * s
    gain = lambda k, shape: 1.0 + 0.02 * jax.random.normal(k, shape, f32)
    a_im_base = math.pi * jnp.arange(SSM_STATE, dtype=f32)
    return {
        'x_prompt': nrm(ks[0], (BATCH, SEQ, D_MODEL), 1.0),
        'x_sample': nrm(ks[1], (DEC_BATCH, DEC_SEQ, D_MODEL), 1.0),
        'cache_k': nrm(ks[2], (DEPTH, DEC_BATCH, WINDOW, N_KV_HEADS, HEAD_DIM), 1.0),
        'cache_v': nrm(ks[3], (DEPTH, DEC_BATCH, WINDOW, N_KV_HEADS, HEAD_DIM), 1.0),
        'state_ssm_re': nrm(ks[4], (DEPTH, DEC_BATCH, N_SSM_GROUPS, SSM_STATE), 0.5),
        'state_ssm_im': nrm(ks[5], (DEPTH, DEC_BATCH, N_SSM_GROUPS, SSM_STATE), 0.5),
        'rel_bias': nrm(ks[6], (N_BUCKETS, N_Q_HEADS), 0.2),
        'norm_mix': gain(ks[7], (DEPTH, D_MODEL)),
        'w_in': nrm(ks[8], (DEPTH, D_MODEL, D_IN_PROJ), D_MODEL ** -0.5),
        'attn_sinks': nrm(ks[9], (DEPTH, N_Q_HEADS), 0.5),
        'ssm_a_re': -0.5 + nrm(ks[10], (DEPTH, N_SSM_GROUPS, SSM_STATE), 0.01),
        'ssm_a_im': a_im_base + nrm(ks[11], (DEPTH, N_SSM_GROUPS, SSM_STATE), 0.01),
        'ssm_log_step': jax.random.uniform(ks[12], (DEPTH, N_SSM_GROUPS), f32,
                                           minval=math.log(DT_MIN), maxval=math.log(DT_MAX)),
        'ssm_b_re': nrm(ks[13], (DEPTH, N_SSM_GROUPS, SSM_STATE, SSM_GROUP), (2 * SSM_GROUP) ** -0.5),
        'ssm_b_im': nrm(ks[14], (DEPTH, N_SSM_GROUPS, SSM_STATE, SSM_GROUP), (2 * SSM_GROUP) ** -0.5),
        'ssm_c_re': nrm(ks[15], (DEPTH, N_SSM_GROUPS, SSM_GROUP, SSM_STATE), SSM_STATE ** -0.5),
        'ssm_c_im': nrm(ks[16], (DEPTH, N_SSM_GROUPS, SSM_GROUP, SSM_STATE), SSM_STATE ** -0.5),
        'ssm_d': nrm(ks[17], (DEPTH, N_SSM_GROUPS, SSM_GROUP), 1.0),
        'w_glu': nrm(ks[18], (DEPTH, SSM_WIDTH, SSM_WIDTH), SSM_WIDTH ** -0.5),
        'b_glu': nrm(ks[19], (DEPTH, SSM_WIDTH), 0.02),
        'norm_attn_out': gain(ks[20], (DEPTH, ATTN_WIDTH)),
        'norm_ssm_out': gain(ks[21], (DEPTH, SSM_WIDTH)),
        'w_out': nrm(ks[22], (DEPTH, D_MODEL, D_MODEL), D_MODEL ** -0.5),
        'norm_mlp': gain(ks[23], (DEPTH, D_MODEL)),
        'w_up': nrm(ks[24], (DEPTH, D_MODEL, D_FF), D_MODEL ** -0.5),
        'w_down': nrm(ks[25], (DEPTH, D_FF, D_MODEL), D_FF ** -0.5),
        'norm_final': gain(ks[26], (D_MODEL,)),
    }


def reference(x_prompt, x_sample, cache_k, cache_v, state_ssm_re, state_ssm_im, rel_bias,
              norm_mix, w_in, attn_sinks, ssm_a_re, ssm_a_im, ssm_log_step, ssm_b_re, ssm_b_im,
              ssm_c_re, ssm_c_im, ssm_d, w_glu, b_glu, norm_attn_out, norm_ssm_out, w_out,
              norm_mlp, w_up, w_down, norm_final):
    xp, xs = x_prompt, x_sample
    kp, vp, rep, imp = [], [], [], []
    ksm, vsm, res, ims = [], [], [], []
    for l in range(DEPTH):
        lw = {'norm_mix': norm_mix[l], 'w_in': w_in[l], 'sinks': attn_sinks[l],
              'a_re': ssm_a_re[l], 'a_im': ssm_a_im[l], 'log_step': ssm_log_step[l],
              'b_re': ssm_b_re[l], 'b_im': ssm_b_im[l], 'c_re': ssm_c_re[l], 'c_im': ssm_c_im[l],
              'd': ssm_d[l], 'w_glu': w_glu[l], 'b_glu': b_glu[l],
              'norm_attn_out': norm_attn_out[l], 'norm_ssm_out': norm_ssm_out[l], 'w_out': w_out[l],
              'norm_mlp': norm_mlp[l], 'w_up': w_up[l], 'w_down': w_down[l]}
        xp, k1, v1, r1, i1 = layer_forward(xp, lw, rel_bias, None, None)
        h0 = lax.complex(state_ssm_re[l].astype(jnp.float32), state_ssm_im[l].astype(jnp.float32))
        xs, k2, v2, r2, i2 = layer_forward(xs, lw, rel_bias, (cache_k[l], cache_v[l]), h0)
        kp.append(k1); vp.append(v1); rep.append(r1); imp.append(i1)
        ksm.append(k2); vsm.append(v2); res.append(r2); ims.append(i2)
    y_prompt = rmsnorm(xp, norm_final)
    y_sample = rmsnorm(xs, norm_final)
    return (y_prompt, y_sample,
            jnp.stack(kp), jnp.stack(vp), jnp.stack(rep), jnp.stack(imp),
            jnp.stack(ksm), jnp.stack(vsm), jnp.stack(res), jnp.stack(ims))
```

```python
import math
import numpy as np
import concourse.bass as bass
import concourse.mybir as mybir
from concourse.bass_utils import run_bass_kernel_spmd

F32 = mybir.dt.float32
BF16 = mybir.dt.bfloat16
I32 = mybir.dt.int32
ALU = mybir.AluOpType
AF = mybir.ActivationFunctionType

NCORES = 8
D = 1024
NP = 2048
NSQ = 16
NS = 64
NT = NP + NS
NLITE = 6
ST = 1024
NEG = -30000.0
EPS = 1e-6
TWO_PI = 2.0 * math.pi
C1 = 6.28125
C2 = TWO_PI - C1


class _Op:
    __slots__ = ("eng", "fn", "deps", "dma", "key", "sig", "val", "idx", "sem", "where", "rw")

    def __init__(self, eng, fn, dma, key):
        self.eng = eng
        self.fn = fn
        self.dma = dma
        self.key = key
        self.deps = set()
        self.sig = False
        self.val = 0
        self.sem = None


class Prog:
    ENGS = ("pe", "act", "dve", "pool", "sp")

    def __init__(self, nc):
        self.nc = nc
        self.ops = []
        self.last_w = {}
        self.reads = {}
        self.pending = {}
        self.applied = set()

    def _key_deps(self, k, deps):
        rn = k[0] if isinstance(k, tuple) else k
        pend = self.pending.get(rn)
        if pend and k not in self.applied:
            deps |= pend
            self.applied.add(k)

    def op(self, eng, fn, reads=(), writes=(), dma=False, key=None):
        o = _Op(eng, fn, dma, key)
        o.idx = len(self.ops)
        deps = o.deps
        ispsum = lambda k: isinstance(k, tuple) and k[0] == "ps"
        pk = [k[:2] for k in list(reads) + list(writes) if ispsum(k)]
        reads = [k for k in reads if not ispsum(k)]
        writes = [k for k in writes if not ispsum(k)]
        for k in dict.fromkeys(pk):
            j = self.last_w.get(k)
            if j is not None and (self.ops[j].eng != eng or self.ops[j].dma or dma):
                deps.add(j)
            self.last_w[k] = o.idx
        for r in reads:
            self._key_deps(r, deps)
            j = self.last_w.get(r)
            if j is not None:
                deps.add(j)
        for w in writes:
            self._key_deps(w, deps)
            j = self.last_w.get(w)
            if j is not None:
                deps.add(j)
            for j in self.reads.get(w, ()):
                deps.add(j)
        for r in reads:
            self.reads.setdefault(r, []).append(o.idx)
        for w in writes:
            self.last_w[w] = o.idx
            self.reads[w] = []
        if dma and key is None:
            o.key = (writes[0] if writes else reads[0])
        o.where = None
        o.rw = (list(reads), list(writes))
        self.ops.append(o)
        return o

    def users_of_region(self, rn):
        s = set()
        for k, j in self.last_w.items():
            if (k[0] if isinstance(k, tuple) else k) == rn:
                s.add(j)
        for k, js in self.reads.items():
            if (k[0] if isinstance(k, tuple) else k) == rn:
                s.update(js)
        return s

    def pe(self, fn, reads=(), writes=()):
        return self.op("pe", fn, reads, writes)

    def act(self, fn, reads=(), writes=()):
        return self.op("act", fn, reads, writes)

    def dve(self, fn, reads=(), writes=()):
        return self.op("dve", fn, reads, writes)

    def pool(self, fn, reads=(), writes=()):
        return self.op("pool", fn, reads, writes)

    def dma(self, out, in_, reads=(), writes=(), eng="sp", key=None, **kw):
        return self.op(eng, lambda e: e.dma_start(out=out, in_=in_, **kw), reads, writes, dma=True, key=key)

    def emit(self):
        nc = self.nc
        ops = self.ops
        for o in ops:
            if o.eng == "pe" and not o.dma:
                o.deps = {j for j in o.deps if not (ops[j].eng == "pe" and not ops[j].dma)}
        for o in ops:
            for j in o.deps:
                ops[j].sig = True
        for o in ops:
            if o.dma:
                o.sig = True
        eng_cnt = {e: 0 for e in self.ENGS}
        dma_cnt = {}
        dma_keys = []
        for o in ops:
            if not o.sig:
                continue
            if o.dma:
                if o.key not in dma_cnt:
                    dma_cnt[o.key] = 0
                    dma_keys.append(o.key)
                dma_cnt[o.key] += 16
                o.val = dma_cnt[o.key]
            else:
                eng_cnt[o.eng] += 1
                o.val = eng_cnt[o.eng]
        sems = {}
        for e in ("pe", "act", "dve", "pool"):
            sems[("eng", e)] = nc.alloc_semaphore("s_" + e)
        for i, k in enumerate(dma_keys):
            sems[("dma", k)] = nc.alloc_semaphore("d%d" % i)
        self.n_sems = len(sems)
        for o in ops:
            if o.sig:
                o.sem = sems[("dma", o.key)] if o.dma else sems[("eng", o.eng)]
        dma_hist = {}
        for o in ops:
            if o.dma:
                dma_hist.setdefault(o.key, []).append((o.idx, o.val))
        by_eng = {e: [o for o in ops if o.eng == e] for e in self.ENGS}

        def emit_engine(ename, eng):
            waited = {}
            for o in by_eng[ename]:
                need = {}
                for j in o.deps:
                    p = ops[j]
                    if p.dma:
                        v = p.val
                        for (ii, vv) in dma_hist[p.key]:
                            if ii < o.idx and vv > v:
                                v = vv
                        sk = ("dma", p.key)
                    else:
                        v = p.val
                        sk = ("eng", p.eng)
                    if need.get(sk, 0) < v:
                        need[sk] = v
                for sk, v in need.items():
                    if waited.get(sk, 0) >= v:
                        continue
                    eng.wait_ge(sems[sk], v)
                    waited[sk] = v
                ins = o.fn(eng)
                if o.sig:
                    ins.then_inc(o.sem, 16 if o.dma else 1)
            if ename == "sp":
                for k, v in dma_cnt.items():
                    eng.wait_ge(sems[("dma", k)], v)

        with nc.Block() as block:
            @block.tensor
            def _(e):
                emit_engine("pe", e)

            @block.scalar
            def _(e):
                emit_engine("act", e)

            @block.vector
            def _(e):
                emit_engine("dve", e)

            @block.gpsimd
            def _(e):
                emit_engine("pool", e)

            @block.sync
            def _(e):
                emit_engine("sp", e)


class Arena:
    def __init__(self, nc, P, words):
        self.P = P
        self.words = words
        self.t = nc.alloc_sbuf_tensor("arena", [128, words], F32)
        self.A = self.t.ap()
        self.live = {}
        self.dead = []
        self.peak = 0

    def alloc(self, name, n):
        n = (n + 7) // 8 * 8
        spans = sorted(self.live.values())
        pos = 0
        off = None
        for (o, m) in spans:
            if o - pos >= n:
                off = pos
                break
            pos = max(pos, o + m)
        if off is None:
            if self.words - pos >= n:
                off = pos
            else:
                raise RuntimeError("arena OOM for %s (%d words); live=%s" % (name, n, sorted((v, k) for k, v in self.live.items())))
        assert name not in self.live and name not in self.P.pending
        self.live[name] = (off, n)
        self.peak = max(self.peak, off + n)
        pend = set()
        nd = []
        for (o, m, users) in self.dead:
            if o < off + n and off < o + m:
                pend |= users
            nd.append((o, m, users))
        self.P.pending[name] = pend
        return Region(self, name, off, n)

    def free(self, reg):
        off, n = self.live.pop(reg.name)
        self.dead.append((off, n, self.P.users_of_region(reg.name) | self.P.pending.get(reg.name, set())))


class Region:
    def __init__(self, ar, name, off, n):
        self.ar = ar
        self.name = name
        self.off = off
        self.n = n

    def k(self, *sub):
        return (self.name,) + sub if sub else self.name

    def f32(self, lo=0, n=None):
        n = self.n - lo if n is None else n
        return self.ar.A[:, self.off + lo:self.off + lo + n]

    def bf(self, lo=0, n=None):
        n = self.n - lo if n is None else n
        return self.ar.A[:, self.off + lo:self.off + lo + n].bitcast(BF16)

    def i32(self, lo=0, n=None):
        n = self.n - lo if n is None else n
        return self.ar.A[:, self.off + lo:self.off + lo + n].bitcast(I32)


_DBG_P = [None]


def dram_ap(t_ap, offset, pattern):
    return bass.AP(t_ap.tensor, offset, [list(p) for p in pattern])


def build_program(debug=False):
    nc = bass.Bass("TRN2", target_bir_lowering=False)
    P = Prog(nc)
    _DBG_P[0] = P

    def din(name, shape):
        return nc.dram_tensor(name, list(shape), F32, kind="ExternalInput").ap()

    def dout(name, shape):
        return nc.dram_tensor(name, list(shape), F32, kind="ExternalOutput").ap()

    xo = din("xo", [NT, D])
    xp = din("xp", [NLITE * ST, D])
    ck_d = din("cache_k", [NSQ, 128, 128])
    cv_d = din("cache_v", [NSQ, 128, 128])
    sre_d = din("st_re", [NSQ, 2048])
    sim_d = din("st_im", [NSQ, 2048])
    relb_d = din("rel_bias", [32, 8])
    gmix_d = din("norm_mix", [1, D])
    win_d = din("w_in", [D, 1280])
    sink_d = din("sinks", [1, 8])
    are_d = din("a_re", [32, 64])
    aim_d = din("a_im", [32, 64])
    ls_d = din("log_step", [1, 32])
    bre_d = din("b_re", [32, 64, 16])
    bim_d = din("b_im", [32, 64, 16])
    cre_d = din("c_re", [32, 16, 64])
    cim_d = din("c_im", [32, 16, 64])
    dd_d = din("ssm_d", [32, 16])
    wglu_d = din("w_glu", [512, 512])
    bglu_d = din("b_glu", [1, 512])
    gatt_d = din("norm_attn", [1, 512])
    gssm_d = din("norm_ssm", [1, 512])
    wout_d = din("w_out", [D, D])
    gmlp_d = din("norm_mlp", [1, D])
    wup_d = din("w_up", [D, 4096])
    wdn_d = din("w_down", [4096, D])
    gfin_d = din("norm_final", [1, D])
    cst_d = din("consts", [128, 160])
    oh_d = din("onehot", [33, 384])
    hm_d = din("halo_mask", [128, 1])
    mB_d = din("maskB", [64, 64])
    jm_d = din("jmat", [128, 192])

    y_o = dout("y", [NT, D])
    nkp_o = dout("nk_p", [128, 128])
    nvp_o = dout("nv_p", [128, 128])
    spre_o = dout("sp_re", [32, 64])
    spim_o = dout("sp_im", [32, 64])
    nks_o = dout("nk_s", [NSQ, 128, 128])
    nvs_o = dout("nv_s", [NSQ, 128, 128])
    ssre_o = dout("ss_re", [NSQ, 2048])
    ssim_o = dout("ss_im", [NSQ, 2048])
    fext_d = nc.dram_tensor("fext", [8, 384], F32, kind="Internal").ap()
    dbg = {}

    AR = Arena(nc, P, 53000)
    psb = [nc.alloc_psum_tensor("ps%d" % i, [128, 512], F32).ap() for i in range(8)]

    def PS(i):
        return psb[i], ("ps", i)

    pers = AR.alloc("pers", 128 + 64 + 160 + 2048 + 64 + 520 + 64)
    ident_f = pers.f32(0, 128)
    ident_b = pers.bf(128, 64)
    cst = pers.f32(192, 160)
    biasT = pers.f32(352, 2048).rearrange("p (s h q) -> p s h q", s=2, h=8)
    misc = pers.f32(2400, 64)
    biasA = pers.f32(2464, 32).rearrange("p (h i) -> p h i", h=8)
    biasB = pers.f32(2496, 512).rearrange("p (h q) -> p h q", h=8)
    gains = AR.alloc("gains", 1024 + 512 + 24)
    g_mix = gains.f32(0, 1024)
    g_att = gains.f32(1024, 512)
    sinkexp = gains.f32(1536, 8)
    gsb = gains.f32(1544, 8)

    P.pool(lambda e: e.memset(ident_f, 0.0), writes=[pers.k("idf")])
    P.pool(lambda e: e.affine_select(out=ident_f, in_=ident_f, pattern=[[-1, 128]], compare_op=ALU.not_equal,
                                     fill=1.0, base=0, channel_multiplier=1), reads=[pers.k("idf")], writes=[pers.k("idf")])
    P.dve(lambda e: e.tensor_copy(out=ident_b, in_=ident_f), reads=[pers.k("idf")], writes=[pers.k("idb")])
    P.dma(cst, cst_d, writes=[pers.k("cst")])
    P.pool(lambda e: e.memset(misc[:, 0:1], -0.5), writes=[pers.k("mh")])
    P.dma(misc[:, 1:2], hm_d, writes=[pers.k("hm")])

    def bc_row(row_ap, n):
        return dram_ap(row_ap, 0, [[0, 128], [1, n]])

    P.dma(g_mix, bc_row(gmix_d, 1024), writes=[gains.k("mix")])
    P.dma(g_att, bc_row(gatt_d, 512), writes=[gains.k("att")])
    P.dma(sinkexp, bc_row(sink_d, 8), writes=[gains.k("sink")])
    P.act(lambda e: e.activation(out=sinkexp, in_=sinkexp, func=AF.Exp), reads=[gains.k("sink")], writes=[gains.k("sink")])
    P.dma(gsb[:, 0:4], dram_ap(gssm_d, 0, [[1, 128], [128, 4]]), writes=[gains.k("gs")], allow_slow_non_contiguous=True)
    P.dma(gsb[:, 4:8], dram_ap(bglu_d, 0, [[1, 128], [128, 4]]), writes=[gains.k("bg")], allow_slow_non_contiguous=True)
    mhalf = misc[:, 0:1]
    gcol = misc[:, 16:24]
    P.dma(gcol, dram_ap(gmix_d, 0, [[1, 128], [128, 8]]), writes=[pers.k("gcol")], allow_slow_non_contiguous=True)

    def reduce_angle(x_ap, r_ap, n_i32, n_f32, kx, kr, ktmp, add=0.0):
        ki_, kf_ = ktmp
        if add != 0.0:
            P.dve(lambda e: e.tensor_scalar(out=r_ap, in0=x_ap, scalar1=float(add), scalar2=None, op0=ALU.add),
                  reads=[kx], writes=[kr])
            src, ksrc = r_ap, kr
        else:
            src, ksrc = x_ap, kx
        P.dve(lambda e: e.tensor_scalar(out=n_i32, in0=src, scalar1=1.0 / TWO_PI, scalar2=None, op0=ALU.mult),
              reads=[ksrc], writes=[ki_])
        P.dve(lambda e: e.tensor_copy(out=n_f32, in_=n_i32), reads=[ki_], writes=[kf_])
        P.dve(lambda e: e.scalar_tensor_tensor(out=r_ap, in0=n_f32, scalar=-C1, in1=src, op0=ALU.mult, op1=ALU.add),
              reads=[kf_, ksrc], writes=[kr])
        P.dve(lambda e: e.scalar_tensor_tensor(out=r_ap, in0=n_f32, scalar=-C2, in1=r_ap, op0=ALU.mult, op1=ALU.add),
              reads=[kf_, kr], writes=[kr])

    def tt(eng, out, a, b, op, reads, writes):
        P.op(eng, lambda e: e.tensor_tensor(out=out, in0=a, in1=b, op=op), reads, writes)


    sm = AR.alloc("sm", 2816)
    _o = [0]

    def smv(n):
        v = sm.f32(_o[0], n)
        _o[0] += n
        return v
    ARE = smv(16); AIM = smv(16); LS = smv(16); DEL = smv(16); ARs = smv(16); AIs = smv(16)
    LRE = smv(144).rearrange("p (g j) -> p g j", g=16); LIM = smv(144).rearrange("p (g j) -> p g j", g=16)
    CRE = smv(16); CIM = smv(16); T8R = smv(16); R8 = smv(16); R128 = smv(16)
    BRE = smv(256).rearrange("p (g c) -> p g c", g=16); BIM = smv(256).rearrange("p (g c) -> p g c", g=16)
    CTRE = smv(256).rearrange("p (g c) -> p g c", g=16); CTIM = smv(256).rearrange("p (g c) -> p g c", g=16)
    HRE = smv(16); HIM = smv(16)
    t1 = smv(256); t2 = smv(256); t3 = smv(256); t4 = smv(256)
    tI = sm.i32(_o[0], 256); _o[0] += 256
    ksm = lambda s: sm.k(s)

    for gh in range(2):
        P.dma(ARE[64 * gh:64 * gh + 64, :], dram_ap(are_d, 1024 * gh, [[1, 64], [64, 16]]), writes=[ksm("are")], allow_slow_non_contiguous=True)
        P.dma(AIM[64 * gh:64 * gh + 64, :], dram_ap(aim_d, 1024 * gh, [[1, 64], [64, 16]]), writes=[ksm("aim")], allow_slow_non_contiguous=True)
        P.dma(LS[64 * gh:64 * gh + 64, :], dram_ap(ls_d, 16 * gh, [[0, 64], [1, 16]]), writes=[ksm("ls")])
        P.dma(BRE[64 * gh:64 * gh + 64], dram_ap(bre_d, 16 * 1024 * gh, [[16, 64], [1024, 16], [1, 16]]), writes=[ksm("bre")])
        P.dma(BIM[64 * gh:64 * gh + 64], dram_ap(bim_d, 16 * 1024 * gh, [[16, 64], [1024, 16], [1, 16]]), writes=[ksm("bim")])
    wqkv = AR.alloc("wqkv", 8 * 1280 // 2)
    W_QKV = wqkv.bf().rearrange("p (k c) -> p k c", k=8)
    wu = AR.alloc("wu", 8 * 512 // 2)
    W_U = wu.bf().rearrange("p (k c) -> p k c", k=8)
    wst = AR.alloc("wstage", 8 * 1280)
    P.pool(lambda e: e.memset(wqkv.f32(), 0.0), writes=[wqkv.k()])
    _wc = {"n": 0}

    def wcast(dst, src, kt, reads, writes):
        _wc["n"] += 1
        if True:
            P.pool(lambda e: e.tensor_scalar(out=dst, in0=src, scalar1=gcol[:, kt:kt + 1], scalar2=None, op0=ALU.mult), reads=reads + [pers.k("gcol")], writes=writes)
        else:
            P.dve(lambda e: e.tensor_scalar(out=dst, in0=src, scalar1=gcol[:, kt:kt + 1], scalar2=None, op0=ALU.mult), reads=reads + [pers.k("gcol")], writes=writes)

    for kt in range(8):
        stg = wst.f32(1280 * kt, 1280)
        sk_ = wst.k(kt)
        P.dma(stg, win_d[kt * 128:(kt + 1) * 128, :], writes=[sk_])
        for (dst, src) in ((W_QKV[:, kt, 0:768], stg[:, 0:768]),
                           (W_QKV[:, kt, 768:832], stg[:, 512:576]), (W_QKV[:, kt, 960:1024], stg[:, 512:576]),
                           (W_QKV[:, kt, 1024:1088], stg[:, 576:640]), (W_QKV[:, kt, 1216:1280], stg[:, 576:640])):
            wcast(dst, src, kt, [sk_], [wqkv.k()])
        wcast(W_U[:, kt, :], stg[:, 768:1280], kt, [sk_], [wu.k()])
    AR.free(wst)
    P.act(lambda e: e.activation(out=DEL, in_=LS, func=AF.Exp), reads=[ksm("ls")], writes=[ksm("del")])
    tt("dve", ARs, ARE, DEL, ALU.mult, [ksm("are"), ksm("del")], [ksm("ars")])
    tt("dve", AIs, AIM, DEL, ALU.mult, [ksm("aim"), ksm("del")], [ksm("ais")])
    JV = cst[:, 0:9]
    b3 = lambda a: a.unsqueeze(2).to_broadcast([128, 16, 9])
    jb = JV.unsqueeze(1).to_broadcast([128, 16, 9])
    v144 = lambda t: t[:, 0:144].rearrange("p (g j) -> p g j", g=16)
    tt("dve", v144(t1), b3(ARs), jb, ALU.mult, [ksm("ars"), pers.k("cst")], [ksm("t1")])
    P.act(lambda e: e.activation(out=v144(t1), in_=v144(t1), func=AF.Exp), reads=[ksm("t1")], writes=[ksm("t1")])
    tt("dve", v144(t2), b3(AIs), jb, ALU.mult, [ksm("ais"), pers.k("cst")], [ksm("t2")])
    reduce_angle(t2[:, 0:144], t3[:, 0:144], tI[:, 0:144], t4[:, 0:144], ksm("t2"), ksm("t3"), (ksm("tI"), ksm("t4")))
    P.act(lambda e: e.activation(out=t3[:, 0:144], in_=t3[:, 0:144], func=AF.Sin), reads=[ksm("t3")], writes=[ksm("t3")])
    tt("dve", LIM, v144(t1), v144(t3), ALU.mult, [ksm("t1"), ksm("t3")], [ksm("lim")])
    reduce_angle(t2[:, 0:144], t3[:, 0:144], tI[:, 0:144], t4[:, 0:144], ksm("t2"), ksm("t3"), (ksm("tI"), ksm("t4")), add=math.pi / 2)
    P.act(lambda e: e.activation(out=t3[:, 0:144], in_=t3[:, 0:144], func=AF.Sin), reads=[ksm("t3")], writes=[ksm("t3")])
    tt("dve", LRE, v144(t1), v144(t3), ALU.mult, [ksm("t1"), ksm("t3")], [ksm("lre")])
    L1r = LRE[:, :, 1]; L1i = LIM[:, :, 1]
    a16 = lambda t, i=0: t[:, 16 * i:16 * i + 16]
    P.dve(lambda e: e.tensor_scalar(out=a16(t1, 0), in0=L1r, scalar1=-1.0, scalar2=None, op0=ALU.add), reads=[ksm("lre"), ksm("t1")], writes=[ksm("t1")])
    tt("dve", a16(t1, 1), ARE, ARE, ALU.mult, [ksm("are")], [ksm("t1")])
    tt("dve", a16(t1, 2), AIM, AIM, ALU.mult, [ksm("aim")], [ksm("t1")])
    tt("dve", a16(t1, 1), a16(t1, 1), a16(t1, 2), ALU.add, [ksm("t1"), ksm("t1")], [ksm("t1")])
    P.dve(lambda e: e.reciprocal(out=a16(t1, 1), in_=a16(t1, 1)), reads=[ksm("t1")], writes=[ksm("t1")])
    tt("dve", a16(t1, 3), a16(t1, 0), ARE, ALU.mult, [ksm("t1"), ksm("are")], [ksm("t1")])
    tt("dve", a16(t1, 4), L1i, AIM, ALU.mult, [ksm("lim"), ksm("aim")], [ksm("t1")])
    tt("dve", a16(t1, 3), a16(t1, 3), a16(t1, 4), ALU.add, [ksm("t1"), ksm("t1")], [ksm("t1")])
    tt("dve", CRE, a16(t1, 3), a16(t1, 1), ALU.mult, [ksm("t1"), ksm("t1")], [ksm("cre")])
    tt("dve", a16(t1, 5), L1i, ARE, ALU.mult, [ksm("lim"), ksm("are")], [ksm("t1")])
    tt("dve", a16(t1, 6), a16(t1, 0), AIM, ALU.mult, [ksm("t1"), ksm("aim")], [ksm("t1")])
    tt("dve", a16(t1, 5), a16(t1, 5), a16(t1, 6), ALU.subtract, [ksm("t1"), ksm("t1")], [ksm("t1")])
    tt("dve", CIM, a16(t1, 5), a16(t1, 1), ALU.mult, [ksm("t1"), ksm("t1")], [ksm("cim")])
    cb = lambda a: a.unsqueeze(2).to_broadcast([128, 16, 16])
    v256 = lambda t: t.rearrange("p (g c) -> p g c", g=16)
    tt("dve", v256(t2), cb(CRE), BRE, ALU.mult, [ksm("cre"), ksm("bre")], [ksm("t2")])
    tt("dve", v256(t3), cb(CIM), BIM, ALU.mult, [ksm("cim"), ksm("bim")], [ksm("t3")])
    tt("dve", v256(t4), cb(CRE), BIM, ALU.mult, [ksm("cre"), ksm("bim")], [ksm("t4")])
    tt("dve", v256(t1), cb(CIM), BRE, ALU.mult, [ksm("cim"), ksm("bre"), ksm("t1"), ksm("t1"), ksm("t1"), ksm("t1")], [ksm("t1")])
    tt("dve", BRE, v256(t2), v256(t3), ALU.subtract, [ksm("t2"), ksm("t3")], [ksm("bre")])
    tt("dve", BIM, v256(t4), v256(t1), ALU.add, [ksm("t4"), ksm("t1")], [ksm("bim")])
    P.dve(lambda e: e.tensor_scalar(out=a16(t2, 0), in0=AIs, scalar1=8.0, scalar2=None, op0=ALU.mult), reads=[ksm("ais"), ksm("t2")], writes=[ksm("t2")])
    reduce_angle(a16(t2, 0), T8R, tI[:, 0:16], a16(t4, 0), ksm("t2"), ksm("t8r"), (ksm("tI"), ksm("t4")))
    P.act(lambda e: e.activation(out=R8, in_=ARs, func=AF.Exp, scale=8.0), reads=[ksm("ars")], writes=[ksm("r8")])
    P.act(lambda e: e.activation(out=R128, in_=ARs, func=AF.Exp, scale=1024.0), reads=[ksm("ars")], writes=[ksm("r128")])
    P.pool(lambda e: e.memset(HRE, 0.0), writes=[ksm("hre")])
    P.pool(lambda e: e.memset(HIM, 0.0), writes=[ksm("him")])

    tabs = AR.alloc("tabs", 2 * 16 * 129)
    CK = tabs.f32(0, 2064).rearrange("p (g k) -> p g k", g=16)
    SK = tabs.f32(2064, 2064).rearrange("p (g k) -> p g k", g=16)
    tw = AR.alloc("tabwork", 4 * 2064)
    xk = tw.f32(0, 2064); rk = tw.f32(2064, 2064); nki = tw.i32(4128, 2064); nkf = tw.f32(6192, 2064)
    KK = cst[:, 16:145]
    tt("dve", xk.rearrange("p (g k) -> p g k", g=16), T8R.unsqueeze(2).to_broadcast([128, 16, 129]),
       KK.unsqueeze(1).to_broadcast([128, 16, 129]), ALU.mult, [ksm("t8r"), pers.k("cst")], [tw.k("x")])
    reduce_angle(xk, rk, nki, nkf, tw.k("x"), tw.k("r"), (tw.k("ni"), tw.k("nf")))
    P.act(lambda e: e.activation(out=SK.rearrange("p g k -> p (g k)"), in_=rk, func=AF.Sin), reads=[tw.k("r")], writes=[tabs.k("sk")])
    reduce_angle(xk, rk, nki, nkf, tw.k("x"), tw.k("r"), (tw.k("ni"), tw.k("nf")), add=math.pi / 2)
    P.act(lambda e: e.activation(out=CK.rearrange("p g k -> p (g k)"), in_=rk, func=AF.Sin), reads=[tw.k("r")], writes=[tabs.k("ck")])
    AR.free(tw)

    wp = AR.alloc("wp", 32 * 2 * 128 // 2)
    W_P = wp.bf().rearrange("p (g r m) -> p g r m", g=32, r=2)
    P.pool(lambda e: e.memset(wp.f32(), 0.0), writes=[wp.k()])
    s2 = AR.alloc("setup2", 2048 * 6 + 32 + 4096)
    adr = s2.f32(0, 2048); adi = s2.f32(2048, 2048); ltr = s2.f32(4096, 2048); lti = s2.f32(6144, 2048)
    w1 = s2.f32(8192, 2048); w2 = s2.f32(10240, 2048); lsb = s2.f32(12288, 32)
    wI = s2.i32(8192, 2048)
    brep = s2.bf(12320, 2048).rearrange("p (g r m) -> p g r m", g=16, r=2)
    P.dma(adr, dram_ap(are_d, 0, [[0, 128], [1, 2048]]), writes=[s2.k("adr")])
    P.dma(adi, dram_ap(aim_d, 0, [[0, 128], [1, 2048]]), writes=[s2.k("adi")])
    P.dma(lsb, dram_ap(ls_d, 0, [[0, 128], [1, 32]]), writes=[s2.k("lsb")])
    P.act(lambda e: e.activation(out=lsb, in_=lsb, func=AF.Exp), reads=[s2.k("lsb")], writes=[s2.k("lsb")])
    g64 = lambda t: t.rearrange("p (g q) -> p g q", g=32)
    lb = lsb.unsqueeze(2).to_broadcast([128, 32, 64])
    tt("dve", g64(adr), g64(adr), lb, ALU.mult, [s2.k("adr"), s2.k("lsb")], [s2.k("adr")])
    tt("dve", g64(adi), g64(adi), lb, ALU.mult, [s2.k("adi"), s2.k("lsb")], [s2.k("adi")])
    JC = cst[:, 146:147]
    P.act(lambda e: e.activation(out=w1, in_=adr, func=AF.Exp, scale=JC), reads=[s2.k("adr"), pers.k("cst")], writes=[s2.k("w1")])
    P.dve(lambda e: e.tensor_scalar(out=adi, in0=adi, scalar1=JC, scalar2=None, op0=ALU.mult), reads=[s2.k("adi"), pers.k("cst")], writes=[s2.k("adi")])
    reduce_angle(adi, ltr, s2.i32(10240, 2048), lti, s2.k("adi"), s2.k("ltr"), (s2.k("w2"), s2.k("lti")))
    P.act(lambda e: e.activation(out=lti, in_=ltr, func=AF.Sin), reads=[s2.k("ltr")], writes=[s2.k("lti")])
    tt("dve", lti, lti, w1, ALU.mult, [s2.k("lti"), s2.k("w1")], [s2.k("lti")])
    reduce_angle(adi, ltr, s2.i32(10240, 2048), adr, s2.k("adi"), s2.k("ltr"), (s2.k("w2"), s2.k("adr")), add=math.pi / 2)
    P.act(lambda e: e.activation(out=ltr, in_=ltr, func=AF.Sin), reads=[s2.k("ltr")], writes=[s2.k("ltr")])
    tt("dve", ltr, ltr, w1, ALU.mult, [s2.k("ltr"), s2.k("w1")], [s2.k("ltr")])
    sb = lambda a: a.unsqueeze(2).to_broadcast([128, 16, 8, 16])
    P.dve(lambda e: e.tensor_copy(out=brep[:, :, 0, :].rearrange("p g (s c) -> p g s c", s=8), in_=sb(BRE)), reads=[ksm("bre")], writes=[s2.k("brep0")])
    P.dve(lambda e: e.tensor_copy(out=brep[:, :, 1, :].rearrange("p g (s c) -> p g s c", s=8), in_=sb(BIM)), reads=[ksm("bim")], writes=[s2.k("brep1")])
    LTR = g64(ltr); LTI = g64(lti)
    for gh in range(2):
        for gq in range(16):
            bk, bkk = PS(gq // 4)
            o_ = bk[:, (gq % 4) * 128:(gq % 4) * 128 + 128]

            def f(e, o_=o_, gq=gq, gh=gh):
                e.matmul(o_[:, 0:64], lhsT=brep[:, gq, 0, :], rhs=ident_b[:, 64 * gh:64 * gh + 64], start=True, stop=True)
                return e.matmul(o_[:, 64:128], lhsT=brep[:, gq, 1, :], rhs=ident_b[:, 64 * gh:64 * gh + 64], start=True, stop=True)
            P.pe(f, reads=[s2.k("brep0"), s2.k("brep1"), pers.k("idb")], writes=[bkk + (gq % 4,)])
        for q4 in range(4):
            bk, bkk = PS(q4)
            btv = bk.rearrange("p (g r m) -> p g r m", g=4, r=2)
            gs = slice(16 * gh + 4 * q4, 16 * gh + 4 * q4 + 4)
            rd = [bkk + (i,) for i in range(4)]
            wv1 = w1[:, 0:256].rearrange("p (g m) -> p g m", g=4); wv2 = w2[:, 0:256].rearrange("p (g m) -> p g m", g=4)
            kw1 = s2.k("w1"); kw2 = s2.k("w2")
            tt("dve", wv1, LTR[:, gs, :], btv[:, :, 0, :], ALU.mult, [s2.k("ltr")] + rd, [kw1])
            tt("dve", wv2, LTI[:, gs, :], btv[:, :, 1, :], ALU.mult, [s2.k("lti")] + rd, [kw2])
            tt("dve", W_P[:, gs, 0, 64 * gh:64 * gh + 64], wv1, wv2, ALU.subtract, [kw1, kw2], [wp.k()])
            tt("dve", wv1, LTR[:, gs, :], btv[:, :, 1, :], ALU.mult, [s2.k("ltr")] + rd, [kw1])
            tt("dve", wv2, LTI[:, gs, :], btv[:, :, 0, :], ALU.mult, [s2.k("lti")] + rd, [kw2])
            tt("dve", W_P[:, gs, 1, 64 * gh:64 * gh + 64], wv1, wv2, ALU.add, [kw1, kw2] + rd, [wp.k()] + rd)
    AR.free(s2)
    if debug:
        dbg["W_P"] = (wp, [128, 32 * 2 * 128], BF16)
        dbg["tabs"] = (tabs, [128, 2 * 2064], F32)
        dbg["sm"] = (sm, [128, 2816], F32)


    _rr = {"n": 0}

    def evac(out, in_, reads, writes, eng=None):
        if eng is None:
            eng = "act" if (_rr["n"] % 3 != 2) else "dve"
            _rr["n"] += 1
        if eng == "act":
            P.act(lambda e: e.copy(out=out, in_=in_), reads=reads, writes=writes)
        elif eng == "pool":
            P.pool(lambda e: e.tensor_copy(out=out, in_=in_), reads=reads, writes=writes)
        else:
            P.dve(lambda e: e.tensor_copy(out=out, in_=in_), reads=reads, writes=writes)

    NTK = 128 + NT
    bh_r = AR.alloc("biasH", 1024)
    biasH = bh_r.f32().rearrange("p (h q) -> p h q", h=8)
    ab_r = AR.alloc("attbias_tmp", 8 + 384 + 384 + 32 + 64)
    rb = ab_r.f32(0, 8); ohs = ab_r.f32(8, 384); fsb = ab_r.f32(392, 384); TB = ab_r.f32(776, 32).rearrange("p (h i) -> p h i", h=8)
    mBs = ab_r.f32(808, 64)
    P.pool(lambda e: e.memset(rb[0:64, :], 1.0), writes=[ab_r.k("rb")])
    P.dma(rb[0:32, :], relb_d, writes=[ab_r.k("rb")])
    P.pool(lambda e: e.memset(ohs[0:64, :], 0.0), writes=[ab_r.k("oh")])
    P.dma(ohs[0:33, :], oh_d, writes=[ab_r.k("oh")])
    P.dma(mBs[0:64, :], mB_d, writes=[ab_r.k("mb")])
    ab2 = AR.alloc("attbias_bf", 16 + 8 + 256)
    rbh = ab2.bf(0, 4); rbl = ab2.bf(4, 4); rbr = ab2.f32(8, 8); rbf = ab2.f32(16, 8); ohb = ab2.bf(24, 192)
    P.dve(lambda e: e.tensor_copy(out=rbh[0:64], in_=rb[0:64]), reads=[ab_r.k("rb")], writes=[ab2.k("h")])
    P.dve(lambda e: e.tensor_copy(out=rbf[0:64], in_=rbh[0:64]), reads=[ab2.k("h")], writes=[ab2.k("hf")])
    tt("dve", rbr[0:64], rb[0:64], rbf[0:64], ALU.subtract, [ab_r.k("rb"), ab2.k("hf")], [ab2.k("r")])
    P.dve(lambda e: e.tensor_copy(out=rbl[0:64], in_=rbr[0:64]), reads=[ab2.k("r")], writes=[ab2.k("l")])
    P.dve(lambda e: e.tensor_copy(out=ohb[0:64], in_=ohs[0:64]), reads=[ab_r.k("oh")], writes=[ab2.k("o")])
    bk, bkk = PS(7)

    def ffx(e, bk=bk):
        e.matmul(bk[0:8, 0:384], lhsT=rbh[0:64, :], rhs=ohb[0:64, :], start=True, stop=False)
        return e.matmul(bk[0:8, 0:384], lhsT=rbl[0:64, :], rhs=ohb[0:64, :], start=False, stop=True)
    P.pe(ffx, reads=[ab2.k("h"), ab2.k("l"), ab2.k("o")], writes=[bkk])
    evac(fsb[0:8, :], bk[0:8, 0:384], [bkk], [ab_r.k("f")], eng="act")
    P.dma(fext_d, fsb[0:8, :], reads=[ab_r.k("f")], writes=["fext"])
    hk_r = AR.alloc("hankel", 2048 + 192 + 32 + 32)
    HK = hk_r.f32(0, 2048).rearrange("p (s h q) -> p s h q", s=2, h=8)
    JM = hk_r.f32(2048, 192)
    HA = hk_r.f32(2240, 32).rearrange("p (h i) -> p h i", h=8)
    HB = hk_r.f32(2272, 32).rearrange("p (h i) -> p h i", h=8)
    P.dma(JM, jm_d, writes=[hk_r.k("jm")])
    def bias_reads():
        for slot in range(2):
            for h in range(8):
                P.dma(HK[:, slot, h, :], dram_ap(fext_d, h * 384 + (128 if slot == 0 else 0), [[1, 128], [1, 128]]),
                      reads=["fext"], writes=[hk_r.k("hk")])
        P.dma(HA, dram_ap(fext_d, 128, [[1, 128], [384, 8], [1, 4]]), reads=["fext"], writes=[hk_r.k("ha")])
        P.dma(HB[0:4], dram_ap(fext_d, 124, [[1, 4], [384, 8], [1, 4]]), reads=["fext"], writes=[hk_r.k("hb")])

    qT_r = AR.alloc("qT", 4 * NT // 2)
    qT = qT_r.bf().rearrange("p (t n) -> p t n", t=4)
    kT_r = AR.alloc("kT", 4 * NTK // 2)
    kT = kT_r.bf().rearrange("p (t n) -> p t n", t=4)
    va_r = AR.alloc("vaug", 17 * 2 * 72 // 2)
    v_aug = va_r.bf().rearrange("p (b g d) -> p b g d", b=17, g=2)
    vs_r = AR.alloc("vsaug", 2 * 72 // 2)
    vs_aug = vs_r.bf().rearrange("p (g d) -> p g d", g=2)
    uown_r = AR.alloc("Uown", 2 * 32 * 128 // 2)
    U_own = uown_r.bf().rearrange("p (s g k) -> p s g k", s=2, g=32)
    us_r = AR.alloc("Usamp", 2 * 32 * 16 // 2)
    xs_r = AR.alloc("xs", 2 * 1024)
    xn_rs = [AR.alloc("xn_a", 512), AR.alloc("xn_b", 512)]
    hn_r = AR.alloc("hnT", 8 * 1024 // 2)
    hnT = hn_r.bf().rearrange("p (k n) -> p k n", k=8)
    utm_r = AR.alloc("utm8", 32 * 12 * 16 // 2)
    u_tm8 = utm_r.bf(0, 2048).rearrange("p (g s c) -> p g s c", g=32, s=8)
    u_s12 = utm_r.bf().rearrange("p (g s c) -> p g s c", g=32, s=12)
    stat_r = AR.alloc("stats", 256)
    kvf_r = AR.alloc("kvf", 256)

    P.pool(lambda e: e.memset(va_r.bf(), 1.0), writes=[va_r.k()])
    P.pool(lambda e: e.memset(vs_r.bf(), 1.0), writes=[vs_r.k()])
    def bias_stage3():
        HKf = hk_r.f32(0, 2048)
        bTf = biasT.rearrange("p s h q -> p (s h q)")
        for c4 in range(4):
            bk, bkk = PS(4 + c4)
            P.pe(lambda e, bk=bk, c4=c4: e.matmul(bk, lhsT=JM[:, 0:128], rhs=HKf[:, 512 * c4:512 * c4 + 512], start=True, stop=True),
                 reads=[hk_r.k("jm"), hk_r.k("hk")], writes=[bkk])
            evac(bTf[:, 512 * c4:512 * c4 + 512], bk, [bkk], [pers.k("biasT")])
        bk, bkk = PS(7)

        def fja(e, bk=bk):
            e.matmul(bk[:, 0:32], lhsT=JM[:, 0:128], rhs=hk_r.f32(2240, 32), start=True, stop=True)
            return e.matmul(bk[0:64, 32:64], lhsT=JM[0:4, 128:192], rhs=hk_r.f32(2272, 32)[0:4], start=True, stop=True)
        P.pe(fja, reads=[hk_r.k("jm"), hk_r.k("ha"), hk_r.k("hb")], writes=[bkk])
        evac(biasA.rearrange("p h i -> p (h i)"), bk[:, 0:32], [bkk], [pers.k("biasA")], eng="act")
        evac(TB[0:64].rearrange("p h i -> p (h i)"), bk[0:64, 32:64], [bkk], [ab_r.k("tb")], eng="act")
        tt("dve", biasB[0:64].rearrange("p h (b i) -> p h b i", b=16), TB[0:64].unsqueeze(2).to_broadcast([64, 8, 16, 4]),
           mBs[0:64].rearrange("p (b i) -> p b i", b=16).unsqueeze(1).to_broadcast([64, 8, 16, 4]), ALU.add,
           [ab_r.k("tb"), ab_r.k("mb")], [pers.k("biasB")])
        P.dve(lambda e: e.tensor_scalar(out=biasH, in0=biasT[:, 0], scalar1=misc[:, 1:2], scalar2=None, op0=ALU.add),
              reads=[pers.k("biasT"), pers.k("hm")], writes=[bh_r.k()])
        if debug:
            _o1 = nc.dram_tensor("dbg_abr", [128, 872], F32, kind="ExternalOutput").ap()
            P.dma(_o1, ab_r.f32(0, 872), reads=[ab_r.k(x) for x in ("rb", "oh", "f", "tb", "mb")])
            _o2 = nc.dram_tensor("dbg_hkr", [128, 2304], F32, kind="ExternalOutput").ap()
            P.dma(_o2, hk_r.f32(0, 2304), reads=[hk_r.k(x) for x in ("jm", "hk", "ha", "hb")])
        AR.free(ab_r)
        AR.free(hk_r)
        AR.free(ab2)

    tile_ctr = {"n": 0}

    def norm_front(src_rows, nrows, gain, wname):
        i = tile_ctr["n"]
        tile_ctr["n"] += 1
        b = i % (xs_r.n // 1024)
        xs = xs_r.f32(1024 * b, 1024)
        bn = i % len(xn_rs)
        xn = xn_rs[bn].bf(0, 512)
        kx = xs_r.k(b)
        kn = xn_rs[bn].k()
        sc = stat_r.f32(4 * (i % 64), 4)
        ksc = stat_r.k(i % 64)
        pr = slice(0, nrows)
        P.dma(xs[pr], src_rows, writes=[kx])
        P.act(lambda e: e.activation(out=xn[pr], in_=xs[pr], func=AF.Square, accum_out=sc[pr, 0:1]), reads=[kx], writes=[kn, ksc])
        P.dve(lambda e: e.tensor_scalar(out=sc[pr, 1:2], in0=sc[pr, 0:1], scalar1=1.0 / 1024, scalar2=EPS, op0=ALU.mult, op1=ALU.add),
              reads=[ksc], writes=[ksc + ("a",)])
        P.pool(lambda e: e.tensor_tensor(out=sc[pr, 2:3], in0=sc[pr, 1:2], in1=mhalf[pr], op=ALU.pow), reads=[ksc + ("a",), pers.k("mh")], writes=[ksc + ("b",)])
        P.act(lambda e: e.activation(out=xn[pr], in_=xs[pr], func=AF.Copy, scale=sc[pr, 2:3]), reads=[kx, ksc + ("b",)], writes=[kn])
        return (xn, kn, pr, nrows)

    def norm_back(ctx, hn_dst, hn_key, eng=None):
        xn, kn, pr, nrows = ctx
        for half in range(2):
            bk, bkk = PS(half)

            def f(e, bk=bk, half=half):
                last = None
                for j in range(4):
                    kt = 4 * half + j
                    last = e.matmul(bk[:, j * 128:j * 128 + nrows], lhsT=xn[pr, kt * 128:(kt + 1) * 128], rhs=ident_b[pr, pr], start=True, stop=True)
                return last
            P.pe(f, reads=[kn, pers.k("idb")], writes=[bkk])
            evac(hn_dst[:, 4 * half:4 * half + 4, :], bk.rearrange("p (j n) -> p j n", j=4)[:, :, 0:nrows], [bkk], [hn_key], eng=eng)

    def norm_tile(src_rows, nrows, hn_dst, hn_key, gain, wname):
        norm_back(norm_front(src_rows, nrows, gain, wname), hn_dst, hn_key)

    def run_tiles(specs):
        ctxs = [None] * len(specs)
        ctxs[0] = norm_front(specs[0][0], 128, g_mix, "mix")
        for i in range(len(specs)):
            if i + 1 < len(specs):
                ctxs[i + 1] = norm_front(specs[i + 1][0], 128, g_mix, "mix")
            norm_back(ctxs[i], specs[i][1], specs[i][2])
            if specs[i][3] is not None:
                specs[i][3]()

    def proj_fm(hn_src, hn_key, ncols, wtile, dst, dst_key, bank):
        bk, bkk = PS(bank)

        def f(e):
            last = None
            for kt in range(8):
                last = e.matmul(bk[:, 0:ncols], lhsT=wtile(kt), rhs=hn_src[:, kt, :], start=(kt == 0), stop=(kt == 7))
            return last
        P.pe(f, reads=[hn_key, wqkv.k(), wu.k()], writes=[bkk])
        evac(dst, bk[:, 0:ncols], [bkk], [dst_key])

    def proj_tm(hn_src, hn_key, nrows, wsl, ncols, bank):
        bk, bkk = PS(bank)

        def f(e):
            last = None
            for kt in range(8):
                last = e.matmul(bk[0:nrows, 0:ncols], lhsT=hn_src[:, kt, :], rhs=wsl(kt), start=(kt == 0), stop=(kt == 7))
            return last
        P.pe(f, reads=[hn_key, wqkv.k(), wu.k()], writes=[bkk])
        return bk, bkk

    pbank = {"n": 0}

    def nb():
        pbank["n"] += 1
        return 2 + (pbank["n"] % 2)

    def kv_tile(hn_src, hn_key, nrows, blk, last=False, sample=False):
        bk, bkk = proj_tm(hn_src, hn_key, nrows, lambda kt: W_QKV[:, kt, 512:768], 256, 4)
        if sample:
            evac(vs_aug[0:nrows, :, 0:64], bk[0:nrows, 128:256].rearrange("p (g d) -> p g d", g=2), [bkk], [vs_r.k()], eng="dve")
        else:
            evac(v_aug[:, blk, :, 0:64], bk[:, 128:256].rearrange("p (g d) -> p g d", g=2), [bkk], [va_r.k()], eng="dve")
        if last or sample:
            kvf = kvf_r.f32()
            evac(kvf[0:nrows], bk[0:nrows, 0:256], [bkk], [kvf_r.k()], eng="act")
            if sample:
                for b_ in range(NSQ):
                    P.dma(nks_o[b_, 124:128, :], kvf[4 * b_:4 * b_ + 4, 0:128], reads=[kvf_r.k()])
                    P.dma(nvs_o[b_, 124:128, :], kvf[4 * b_:4 * b_ + 4, 128:256], reads=[kvf_r.k()])
            else:
                P.dma(nkp_o, kvf[:, 0:128], reads=[kvf_r.k()])
                P.dma(nvp_o, kvf[:, 128:256], reads=[kvf_r.k()])

    def k_cols(hn_src, hn_key, ncols, tok0):
        for t in range(4):
            proj_fm(hn_src, hn_key, ncols, lambda kt, t=t: W_QKV[:, kt, 768 + 128 * t:896 + 128 * t],
                    kT[:, t, tok0:tok0 + ncols], kT_r.k(), nb())

    def q_cols(hn_src, hn_key, ncols, tok0):
        for t in range(4):
            proj_fm(hn_src, hn_key, ncols, lambda kt, t=t: W_QKV[:, kt, 128 * t:128 * t + 128],
                    qT[:, t, tok0:tok0 + ncols], qT_r.k(), nb())

    def u_proj_st(hn_key, hnT_, utm8_, utm_k):
        for s_ in range(8):
            bk, bkk = PS(nb())

            def f(e, bk=bk, s_=s_, hnT_=hnT_):
                last = None
                for kt in range(8):
                    last = e.matmul(bk[:, 0:512], lhsT=hnT_[:, kt, s_::8], rhs=W_U[:, kt, :], start=(kt == 0), stop=(kt == 7))
                return last
            P.pe(f, reads=[hn_key, wu.k()], writes=[bkk])
            evac(utm8_[:, :, s_, :], bk.rearrange("p (g c) -> p g c", g=32), [bkk], [utm_k])

    def U_transposes(U_dst, U_key, utm8_, utm_k):
        for q4 in range(8):
            bk, bkk = PS(5 + q4 % 2)

            def f(e, bk=bk, q4=q4, utm8_=utm8_):
                last = None
                for j in range(4):
                    g = 4 * q4 + j
                    last = e.matmul(bk[:, j * 128:(j + 1) * 128], lhsT=utm8_[:, g, :, :].rearrange("p s c -> p (s c)"), rhs=ident_b, start=True, stop=True)
                return last
            P.pe(f, reads=[utm_k, pers.k("idb")], writes=[bkk])
            evac(U_dst[:, 4 * q4:4 * q4 + 4, :], bk.rearrange("p (j k) -> p j k", j=4), [bkk], [U_key])

    P.dma(nks_o[:, 0:124, :], ck_d[:, 4:128, :], key="c2o_k")
    P.dma(nvs_o[:, 0:124, :], cv_d[:, 4:128, :], key="c2o_v")
    hkey = hn_r.k()
    norm_tile(xp[NLITE * ST - 128:NLITE * ST, :], 128, hnT[:, :, 0:128], hkey, g_mix, "mix")
    k_cols(hnT[:, :, 0:128], hkey, 128, 0)
    kv_tile(hnT[:, :, 0:128], hkey, 128, 0)
    for st in range(2):
        specs = []
        for tl in range(8):
            r0 = st * ST + tl * 128
            hsl = hnT[:, :, tl * 128:(tl + 1) * 128]
            specs.append((xo[r0:r0 + 128, :], hsl, hkey,
                          (lambda hsl=hsl, st=st, tl=tl: kv_tile(hsl, hkey, 128, 1 + st * 8 + tl, last=(st == 1 and tl == 7)))))
        run_tiles(specs)
        if st == 0:
            bias_reads()
        for hf in range(2):
            c0 = hf * 512
            q_cols(hnT[:, :, c0:c0 + 512], hkey, 512, st * ST + c0)
            k_cols(hnT[:, :, c0:c0 + 512], hkey, 512, 128 + st * ST + c0)
        u_proj_st(hkey, hnT, u_tm8, utm_r.k())
        U_transposes(U_own[:, st], uown_r.k(st), u_tm8, utm_r.k())
        if st == 0:
            bias_stage3()
    norm_tile(xo[NP:NT, :], 64, hnT[:, :, 0:64], hkey, g_mix, "mix")
    q_cols(hnT[:, :, 0:64], hkey, 64, NP)
    k_cols(hnT[:, :, 0:64], hkey, 64, 128 + NP)
    kv_tile(hnT[:, :, 0:64], hkey, 64, 0, sample=True)
    P.pool(lambda e: e.memset(utm_r.f32(), 0.0), reads=[uown_r.k(0), uown_r.k(1)], writes=[utm_r.k()])
    for t in range(4):
        bk, bkk = PS(nb())

        def f(e, bk=bk, t=t, hnT=hnT):
            last = None
            for kt in range(8):
                last = e.matmul(bk[0:16, 0:512], lhsT=hnT[:, kt, t:64:4], rhs=W_U[:, kt, :], start=(kt == 0), stop=(kt == 7))
            return last
        P.pe(f, reads=[hkey, wu.k()], writes=[bkk])
        evac(u_s12[0:16, :, 4 + t, :], bk[0:16, :].rearrange("p (g c) -> p g c", g=32), [bkk], [utm_r.k()])
    U_s = us_r.bf().rearrange("p (w g b) -> p w g b", w=2, g=32)
    for w in range(2):
        bk, bkk = PS(5 + w)

        def f(e, bk=bk, w=w, u_s12=u_s12):
            last = None
            for g in range(32):
                last = e.matmul(bk[:, g * 16:(g + 1) * 16], lhsT=u_s12[0:16, g, 4 * w:4 * w + 8, :].rearrange("p s c -> p (s c)"),
                                rhs=ident_b[0:16, 0:16], start=True, stop=True)
            return last
        P.pe(f, reads=[utm_r.k(), pers.k("idb")], writes=[bkk])
        evac(U_s[:, w], bk.rearrange("p (g b) -> p g b", g=32), [bkk], [us_r.k()])
    AR.free(wqkv)
    for _r in [xs_r, hn_r, utm_r, stat_r, kvf_r] + xn_rs:
        AR.free(_r)
    if debug:
        dbg["qT"] = (qT_r, [128, 4 * NT], BF16)
        dbg["kT"] = (kT_r, [128, 4 * NTK], BF16)
        dbg["vaug"] = (va_r, [128, 17 * 2 * 72], BF16)
        dbg["Uown"] = (uown_r, [128, 2 * 32 * 128], BF16)
        dbg["Usamp"] = (us_r, [128, 2 * 32 * 16], BF16)

    mT_r = AR.alloc("mTattn", 4 * NT // 2)
    mT_att = mT_r.bf().rearrange("p (t n) -> p t n", t=4)
    aw = AR.alloc("attwork", 2 * 512 + 2 * 512 + 512 + 256 + 64 + 1024)
    tS = [aw.f32(0, 512), aw.f32(512, 512)]
    PT = [aw.bf(1024, 256).rearrange("p (r q) -> p r q", r=4), aw.bf(1280, 256).rearrange("p (r q) -> p r q", r=4),
          aw.bf(1536, 256).rearrange("p (r q) -> p r q", r=4), aw.bf(1792, 256).rearrange("p (r q) -> p r q", r=4)]
    PT += [aw.bf(2880 + 256 * i_, 256).rearrange("p (r q) -> p r q", r=4) for i_ in range(4)]
    o_sb = aw.f32(2048, 512)
    on_b = aw.bf(2560, 256)
    ast = aw.f32(2816, 64)
    ep_ctr = {"n": 0}

    def attn_epilogue(bO, nrows, tok0):
        i = ep_ctr["n"]
        ep_ctr["n"] += 1
        pr = slice(0, nrows)
        sc = ast[:, 16 * (i % 4):16 * (i % 4) + 16]
        ks = aw.k("st", i % 4)
        for g in range(2):
            bk, bkk = bO[g]
            Ov = bk[:, 0:288].rearrange("p (r d) -> p r d", r=4)
            tt("dve", sc[pr, 4 * g:4 * g + 4], Ov[pr, :, 64], sinkexp[pr, 4 * g:4 * g + 4], ALU.add, [bkk, gains.k("sink")], [ks])
        P.dve(lambda e: e.reciprocal(out=sc[pr, 0:8], in_=sc[pr, 0:8]), reads=[ks], writes=[ks])
        for g in range(2):
            bk, bkk = bO[g]
            Ov = bk[:, 0:288].rearrange("p (r d) -> p r d", r=4)
            tt("dve", o_sb[pr, 256 * g:256 * g + 256].rearrange("p (r d) -> p r d", r=4), Ov[pr, :, 0:64],
               sc[pr, 4 * g:4 * g + 4].unsqueeze(2).to_broadcast([nrows, 4, 64]), ALU.mult, [bkk, ks], [aw.k("o")])
        P.act(lambda e: e.activation(out=on_b[pr], in_=o_sb[pr], func=AF.Square, accum_out=sc[pr, 8:9]), reads=[aw.k("o")], writes=[aw.k("on"), ks + ("s",)])
        P.dve(lambda e: e.tensor_scalar(out=sc[pr, 9:10], in0=sc[pr, 8:9], scalar1=1.0 / 512, scalar2=EPS, op0=ALU.mult, op1=ALU.add),
              reads=[ks + ("s",)], writes=[ks + ("a",)])
        P.pool(lambda e: e.tensor_tensor(out=sc[pr, 10:11], in0=sc[pr, 9:10], in1=mhalf[pr], op=ALU.pow), reads=[ks + ("a",), pers.k("mh")], writes=[ks + ("b",)])
        P.dve(lambda e: e.scalar_tensor_tensor(out=on_b[pr], in0=o_sb[pr], scalar=sc[pr, 10:11], in1=g_att[pr], op0=ALU.mult, op1=ALU.mult),
              reads=[aw.k("o"), ks + ("b",), gains.k("att")], writes=[aw.k("on")])
        bk, bkk = PS(6)

        def f(e):
            last = None
            for t in range(4):
                last = e.matmul(bk[:, t * 128:t * 128 + nrows], lhsT=on_b[pr, t * 128:(t + 1) * 128], rhs=ident_b[pr, pr], start=True, stop=True)
            return last
        P.pe(f, reads=[aw.k("on"), pers.k("idb")], writes=[bkk])
        evac(mT_att[:, :, tok0:tok0 + nrows], bk.rearrange("p (t n) -> p t n", t=4)[:, :, 0:nrows], [bkk], [mT_r.k()], eng="act")

    def att_stage_a(b):
        j = b + 1
        ps_ = 4 * (b % 2)
        for g in range(2):
            for slot in range(2):
                kb = j - 1 + slot
                bS, bSk = PS(2 + slot)

                def f(e, bS=bS, g=g, kb=kb, b=b):
                    last = None
                    for r in range(4):
                        last = e.matmul(bS[:, r * 128:(r + 1) * 128], lhsT=kT[:, 2 * g + (r % 2), 128 * kb:128 * kb + 128],
                                        rhs=qT[:, 2 * g + r // 2, 128 * b:128 * b + 128], start=True, stop=True)
                    return last
                P.pe(f, reads=[kT_r.k(), qT_r.k()], writes=[bSk])
                bias = (biasH if b == 0 else biasT[:, 0]) if slot == 0 else biasT[:, 1]
                bkey = (bh_r.k() if b == 0 else pers.k("biasT")) if slot == 0 else pers.k("biasT")
                P.dve(lambda e, bS=bS, bias=bias, slot=slot, g=g: e.scalar_tensor_tensor(
                    out=tS[slot], in0=bS, scalar=0.125, in1=bias[:, 4 * g:4 * g + 4, :].rearrange("p r q -> p (r q)"), op0=ALU.mult, op1=ALU.add),
                    reads=[bSk, bkey], writes=[aw.k("tS", slot)])
                pt = PT[ps_ + 2 * g + slot]
                P.act(lambda e, pt=pt, slot=slot: e.activation(out=pt.rearrange("p r q -> p (r q)"), in_=tS[slot], func=AF.Exp),
                      reads=[aw.k("tS", slot)], writes=[aw.k("PT", ps_ + 2 * g + slot)])

    def att_stage_b(b):
        j = b + 1
        ps_ = 4 * (b % 2)
        bO = [PS(4), PS(5)]
        for g in range(2):
            bk, bkk = bO[g]

            def fpv(e, bk=bk, g=g, j=j, ps_=ps_):
                last = None
                for r in range(4):
                    for slot in range(2):
                        last = e.matmul(bk[:, r * 72:r * 72 + 65], lhsT=PT[ps_ + 2 * g + slot][:, r, :], rhs=v_aug[:, j - 1 + slot, g, 0:65],
                                        start=(slot == 0), stop=(slot == 1))
                return last
            P.pe(fpv, reads=[aw.k("PT", ps_ + 2 * g), aw.k("PT", ps_ + 2 * g + 1), va_r.k()], writes=[bkk])
        attn_epilogue(bO, 128, 128 * b)

    att_stage_a(0)
    for b in range(16):
        if b + 1 < 16:
            att_stage_a(b + 1)
        att_stage_b(b)

    sa = AR.alloc("sa_st", 2048)
    sa2 = AR.alloc("sa_kc", 2048)
    sa3 = AR.alloc("sa_vc", 1152)

    class _SK:
        def k(self, s_):
            return ("sa", s_)
    cst32 = sa.f32(0, 2048).rearrange("p (b d) -> p b d", b=16)
    kc_n = sa2.bf(0, 1024).rearrange("p (b d) -> p b d", b=16)
    kc_s = sa2.bf(1024, 1024).rearrange("p (b d) -> p b d", b=16)
    Vc = sa3.bf(0, 1152).rearrange("p (b g d) -> p b g d", b=16, g=2)
    Kz = [None] * 4
    _unused = [lambda i: sa4.bf(1024 * i, 1024).rearrange("p (b k) -> p b k", b=16) for i in range(4)]
    _sa0 = sa
    sa = type("X", (), {"k": staticmethod(lambda s_: ("sa_" + {"st": "st", "kcn": "kc", "kcs": "kc", "vc": "vc", "kz": "kz", "pfa": "pfa", "tsa": "ts", "tsb": "ts", "pb": "ts"}[s_], s_))})
    P.dma(cst32, ck_d.rearrange("b k d -> k b d"), writes=[sa.k("st")])
    P.pool(lambda e: e.tensor_copy(out=kc_n, in_=cst32), reads=[sa.k("st")], writes=[sa.k("kcn")])
    P.pool(lambda e: e.tensor_copy(out=kc_s[:, :, 0:64], in_=cst32[:, :, 64:128]), reads=[sa.k("st")], writes=[sa.k("kcs")])
    P.pool(lambda e: e.tensor_copy(out=kc_s[:, :, 64:128], in_=cst32[:, :, 0:64]), reads=[sa.k("st")], writes=[sa.k("kcs")])
    P.pool(lambda e: e.memset(Vc, 1.0), writes=[sa.k("vc")])
    P.dma(cst32, cv_d.rearrange("b k d -> k b d"), writes=[sa.k("st")])
    P.pool(lambda e: e.tensor_copy(out=Vc[:, :, :, 0:64], in_=cst32.rearrange("p b (g d) -> p b g d", g=2)), reads=[sa.k("st")], writes=[sa.k("vc")])
    AR.free(_sa0)
    sa4 = AR.alloc("sa_kz", 4096)
    Kz = [sa4.bf(1024 * i, 1024).rearrange("p (b k) -> p b k", b=16) for i in range(4)]
    P.pool(lambda e: e.memset(sa4.f32(), 0.0), writes=[sa.k("kz")])
    for typ, src, (lo_i, hi_i) in ((0, kc_n, (0, 1)), (1, kc_s, (2, 3))):
        for b4 in range(4):
            bk, bkk = PS(2 + b4 % 2)

            def f(e, bk=bk, src=src, b4=b4):
                last = None
                for jj in range(4):
                    last = e.matmul(bk[:, jj * 128:(jj + 1) * 128], lhsT=src[:, 4 * b4 + jj, :], rhs=ident_b, start=True, stop=True)
                return last
            P.pe(f, reads=[sa.k("kcn"), sa.k("kcs"), pers.k("idb")], writes=[bkk])
            bv = bk.rearrange("p (j k) -> p j k", j=4)
            evac(Kz[lo_i][0:64, 4 * b4:4 * b4 + 4, :], bv[0:64], [bkk], [sa.k("kz")], eng="act")
            evac(Kz[hi_i][64:128, 4 * b4:4 * b4 + 4, :], bv[64:128], [bkk], [sa.k("kz")], eng="dve")
    AR.free(sa2)
    sa5 = AR.alloc("sa_ts", 512 + 512 + 256)
    sa6 = AR.alloc("sa_pfa", 2048)
    tSA = sa5.f32(0, 512)
    PfA = sa6.bf(0, 2048)
    tSB = sa5.f32(512, 512)
    PBs = sa5.bf(1024, 256).rearrange("p (h q) -> p h q", h=8)
    P.pool(lambda e: e.memset(sa6.f32(), 0.0), writes=[sa.k("pfa")])
    _sa_regs = [sa3, sa4, sa5, sa6]
    KV = {(0, 0): Kz[0], (0, 1): Kz[3], (1, 0): Kz[2], (1, 1): Kz[1]}
    bA, bAk = PS(2)

    def fqa(e, bA=bA):
        last = None
        for b_ in range(NSQ):
            for g in range(2):
                for r in range(4):
                    c0 = ((b_ * 2 + g) * 4 + r) * 4
                    last = e.matmul(bA[:, c0:c0 + 4], lhsT=KV[(g, r % 2)][:, b_, :], rhs=qT[:, 2 * g + r // 2, NP + 4 * b_:NP + 4 * b_ + 4],
                                    start=True, stop=True)
        return last
    P.pe(fqa, reads=[sa.k("kz"), qT_r.k()], writes=[bAk])
    P.dve(lambda e: e.scalar_tensor_tensor(out=tSA.rearrange("p (b h i) -> p b h i", b=16, h=8), in0=bA.rearrange("p (b h i) -> p b h i", b=16, h=8),
                                           scalar=0.125, in1=biasA.unsqueeze(1).to_broadcast([128, 16, 8, 4]), op0=ALU.mult, op1=ALU.add),
          reads=[bAk, pers.k("biasA")], writes=[sa.k("tsa")])
    bB, bBk = PS(3)

    def fqb(e, bB=bB):
        last = None
        for h in range(8):
            g, r = h // 4, h % 4
            last = e.matmul(bB[0:64, h * 64:(h + 1) * 64], lhsT=kT[:, 2 * g + (r % 2), 128 + NP:128 + NP + 64],
                            rhs=qT[:, 2 * g + r // 2, NP:NP + 64], start=True, stop=True)
        return last
    P.pe(fqb, reads=[kT_r.k(), qT_r.k()], writes=[bBk])
    P.dve(lambda e: e.scalar_tensor_tensor(out=tSB[0:64], in0=bB[0:64], scalar=0.125, in1=biasB[0:64].rearrange("p h q -> p (h q)"),
                                           op0=ALU.mult, op1=ALU.add), reads=[bBk, pers.k("biasB")], writes=[sa.k("tsb")])
    P.act(lambda e: e.activation(out=PBs[0:64].rearrange("p h q -> p (h q)"), in_=tSB[0:64], func=AF.Exp), reads=[sa.k("tsb")], writes=[sa.k("pb")])
    PfAv = PfA.rearrange("p (r b q) -> p r b q", r=4, b=16)
    pfa_out = bass.AP(PfA.tensor, PfA.offset, [list(PfA.ap[0]), [68, 16], [1024, 4], [1, 4]])
    bO = [PS(4), PS(5)]
    for g in range(2):
        bk, bkk = bO[g]
        P.act(lambda e, g=g: e.activation(out=pfa_out, in_=tSA.rearrange("p (b h i) -> p b h i", b=16, h=8)[:, :, 4 * g:4 * g + 4, :], func=AF.Exp),
              reads=[sa.k("tsa"), sa.k("pfa")], writes=[sa.k("pfa")])

        def fpvs(e, bk=bk, g=g):
            last = None
            for r in range(4):
                h = 4 * g + r
                for b_ in range(NSQ):
                    e.matmul(bk[0:64, r * 72:r * 72 + 65], lhsT=PfAv[:, r, b_, :], rhs=Vc[:, b_, g, 0:65], start=(b_ == 0), stop=False)
                last = e.matmul(bk[0:64, r * 72:r * 72 + 65], lhsT=PBs[0:64, h, :], rhs=vs_aug[0:64, g, 0:65], start=False, stop=True)
            return last
        P.pe(fpvs, reads=[sa.k("pfa"), sa.k("pb"), sa.k("vc"), vs_r.k()], writes=[bkk])
    attn_epilogue(bO, 64, NP)
    for _r in _sa_regs:
        AR.free(_r)
    AR.free(aw)
    AR.free(qT_r)
    AR.free(kT_r)
    AR.free(va_r)
    AR.free(vs_r)
    AR.free(bh_r)
    if debug:
        dbg["mTattn"] = (mT_r, [128, 4 * NT], BF16)
        dbg["pers"] = (pers, [128, 3048], F32)

    hn2_rs = [AR.alloc("hnT2a", 8 * 1024 // 2), AR.alloc("hnT2b", 8 * 1024 // 2)]
    hnT2s = [r_.bf().rearrange("p (k n) -> p k n", k=8) for r_ in hn2_rs]
    gb_r = AR.alloc("gbuf", 16 * 2 * 128)
    G = gb_r.f32().rearrange("p (g r k) -> p g r k", g=16, r=2)
    xs_r = AR.alloc("xs2", 3 * 1024)
    utm2_r = AR.alloc("utm8b", 32 * 8 * 16 // 2)
    u_tm8b = utm2_r.bf().rearrange("p (g s c) -> p g s c", g=32, s=8)
    ul_rs = [AR.alloc("Ulitea", 32 * 128 // 2), AR.alloc("Uliteb", 32 * 128 // 2)]
    U_ls = [r_.bf().rearrange("p (g k) -> p g k", g=32) for r_ in ul_rs]
    rt_r = AR.alloc("rottmp", 2048)
    xn_rs = [AR.alloc("xn2a", 512), AR.alloc("xn2b", 512)]
    stat_r = AR.alloc("stats2", 256)
    ss_r = AR.alloc("sscr", 16 * 12)
    SS = [ss_r.f32(16 * i, 16) for i in range(12)]
    kH = ksm("H")
    P.pool(lambda e: e.memset(SS[11], 0.0), reads=[ksm("hre"), ksm("him")], writes=[kH])

    def S_rotate(Uv, Ukey, eng="dve"):
        for q4 in range(4):
            (bre, brek), (bim, bimk) = (PS(4), PS(5)) if q4 % 2 == 0 else (PS(6), PS(7))

            def f(e, bre=bre, bim=bim, q4=q4, Uv=Uv):
                last = None
                for j in range(4):
                    gq = 4 * q4 + j
                    for ri, bank in ((0, bre), (1, bim)):
                        e.matmul(bank[:, j * 128:(j + 1) * 128], lhsT=W_P[:, gq, ri, :], rhs=Uv[:, gq, :], start=True, stop=False)
                        last = e.matmul(bank[:, j * 128:(j + 1) * 128], lhsT=W_P[:, 16 + gq, ri, :], rhs=Uv[:, 16 + gq, :], start=False, stop=True)
                return last
            P.pe(f, reads=[wp.k(), Ukey], writes=[brek, bimk])
            gs = slice(4 * q4, 4 * q4 + 4)
            ck = CK[:, gs, 0:128]
            sk = SK[:, gs, 0:128]
            tk = [tabs.k("ck"), tabs.k("sk")]
            if eng == "dve":
                Sre = bre.rearrange("p (g k) -> p g k", g=4)
                Sim = bim.rearrange("p (g k) -> p g k", g=4)
                tA = rt_r.f32(0, 512).rearrange("p (g k) -> p g k", g=4)
                tB = rt_r.f32(512, 512).rearrange("p (g k) -> p g k", g=4)
                kA = rt_r.k("a"); kB = rt_r.k("b")
                tt("dve", tA, ck, Sre, ALU.mult, tk + [brek], [kA])
                tt("dve", tB, sk, Sim, ALU.mult, tk + [bimk], [kB])
                tt("dve", G[:, gs, 0, :], tA, tB, ALU.add, [kA, kB], [gb_r.k(q4)])
                tt("dve", tA, ck, Sim, ALU.mult, tk + [bimk], [kA])
                tt("dve", tB, sk, Sre, ALU.mult, tk + [brek], [kB])
                tt("dve", G[:, gs, 1, :], tA, tB, ALU.subtract, [kA, kB], [gb_r.k(q4)])
            else:
                gk = gb_r.k(q4)
                evac(G[:, gs, 0, :], bre.rearrange("p (g k) -> p g k", g=4), [brek], [gk], eng="act")
                evac(G[:, gs, 1, :], bim.rearrange("p (g k) -> p g k", g=4), [bimk], [gk], eng="act")
                tmps = [rt_r.f32(512 * i, 512).rearrange("p (g k) -> p g k", g=4) for i in range(4)]
                kt_ = [rt_r.k("t", i) for i in range(4)]
                tt(eng, tmps[0], ck, G[:, gs, 0, :], ALU.mult, tk + [gk], [kt_[0]])
                tt(eng, tmps[1], sk, G[:, gs, 1, :], ALU.mult, tk + [gk], [kt_[1]])
                tt(eng, tmps[2], ck, G[:, gs, 1, :], ALU.mult, tk + [gk], [kt_[2]])
                tt(eng, tmps[3], sk, G[:, gs, 0, :], ALU.mult, tk + [gk], [kt_[3]])
                tt(eng, G[:, gs, 0, :], tmps[0], tmps[1], ALU.add, [kt_[0], kt_[1]], [gk])
                tt(eng, G[:, gs, 1, :], tmps[2], tmps[3], ALU.subtract, [kt_[2], kt_[3]], [gk])

    def scan_all(init, eng="dve"):
        for gq in range(16):
            for ri in range(2):
                ini = 0.0 if init is None else init[ri][:, gq:gq + 1]
                rd = [gb_r.k(gq // 4), ksm("r8")] + ([] if init is None else [ss_r.k("ini")])
                P.op(eng, lambda e, gq=gq, ri=ri, ini=ini: e.tensor_tensor_scan(
                    out=G[:, gq, ri, :], data0=R8[:, gq:gq + 1].to_broadcast([128, 128]), data1=G[:, gq, ri, :],
                    initial=ini, op0=ALU.mult, op1=ALU.add), reads=rd, writes=[gb_r.k(gq // 4)])

    def rot_small(ore, oim, cc, sn, xre, xim, rd, wr, eng="dve"):
        tt(eng, SS[0], cc, xre, ALU.mult, rd, [ss_r.k(0)])
        tt(eng, SS[1], sn, xim, ALU.mult, rd, [ss_r.k(1)])
        tt(eng, SS[2], cc, xim, ALU.mult, rd, [ss_r.k(2)])
        tt(eng, SS[3], sn, xre, ALU.mult, rd, [ss_r.k(3)])
        tt(eng, ore, SS[0], SS[1], ALU.subtract, [ss_r.k(0), ss_r.k(1)], wr)
        tt(eng, oim, SS[2], SS[3], ALU.add, [ss_r.k(2), ss_r.k(3)], wr)

    tabk = [tabs.k("ck"), tabs.k("sk")]
    GK = [gb_r.k(i_) for i_ in range(4)]

    def final_state(ore, oim, wr, eng="dve"):
        rot_small(ore, oim, CK[:, :, 127], SK[:, :, 127], G[:, :, 0, 127], G[:, :, 1, 127], tabk + GK, wr, eng=eng)

    def lite_T_gen(lt):
        pb = lt % 2
        hk2 = hn2_rs[pb].k()
        hnT_ = hnT2s[pb]
        specs = [(xp[lt * ST + tl * 128:lt * ST + tl * 128 + 128, :], hnT_[:, :, tl * 128:(tl + 1) * 128]) for tl in range(8)]
        ctxs = [None] * 8
        ctxs[0] = norm_front(specs[0][0], 128, g_mix, "mix")
        for i in range(8):
            if i + 1 < 8:
                ctxs[i + 1] = norm_front(specs[i + 1][0], 128, g_mix, "mix")
            norm_back(ctxs[i], specs[i][1], hk2)
            yield

    def lite_U_gen(lt):
        pb = lt % 2
        hk2 = hn2_rs[pb].k()
        hnT_ = hnT2s[pb]
        for s_ in range(8):
            bk, bkk = PS(nb())

            def f(e, bk=bk, s_=s_, hnT_=hnT_):
                last = None
                for kt in range(8):
                    last = e.matmul(bk[:, 0:512], lhsT=hnT_[:, kt, s_::8], rhs=W_U[:, kt, :], start=(kt == 0), stop=(kt == 7))
                return last
            P.pe(f, reads=[hk2, wu.k()], writes=[bkk])
            evac(u_tm8b[:, :, s_, :], bk.rearrange("p (g c) -> p g c", g=32), [bkk], [utm2_r.k()])
            yield
        U_dst = U_ls[pb]
        for q4 in range(8):
            bk, bkk = PS(5 + q4 % 2)

            def f(e, bk=bk, q4=q4):
                last = None
                for j in range(4):
                    g = 4 * q4 + j
                    last = e.matmul(bk[:, j * 128:(j + 1) * 128], lhsT=u_tm8b[:, g, :, :].rearrange("p s c -> p (s c)"), rhs=ident_b, start=True, stop=True)
                return last
            P.pe(f, reads=[utm2_r.k(), pers.k("idb")], writes=[bkk])
            evac(U_dst[:, 4 * q4:4 * q4 + 4, :], bk.rearrange("p (j k) -> p j k", j=4), [bkk], [ul_rs[pb].k()])
            yield

    def scan_gen():
        for gq in range(16):
            for ri in range(2):
                P.dve(lambda e, gq=gq, ri=ri: e.tensor_tensor_scan(
                    out=G[:, gq, ri, :], data0=R8[:, gq:gq + 1].to_broadcast([128, 128]), data1=G[:, gq, ri, :],
                    initial=0.0, op0=ALU.mult, op1=ALU.add), reads=[gb_r.k(gq // 4), ksm("r8")], writes=[gb_r.k(gq // 4)])
            yield

    def drain(g):
        for _ in g:
            pass

    def interleave(gens):
        gens = [g for g in gens if g is not None]
        while gens:
            for g in list(gens):
                try:
                    next(g)
                except StopIteration:
                    gens.remove(g)

    drain(lite_T_gen(0))
    interleave([lite_T_gen(1), lite_U_gen(0)])
    interleave([lite_T_gen(2), lite_U_gen(1)])
    for lt in range(NLITE):
        S_rotate(U_ls[lt % 2], ul_rs[lt % 2].k(), eng=("pool" if lt >= 3 else "dve"))
        interleave([lite_T_gen(lt + 3) if lt + 3 < NLITE else None,
                    lite_U_gen(lt + 2) if lt + 2 < NLITE else None,
                    scan_gen()])
        final_state(SS[4], SS[5], [ss_r.k("F")])
        rot_small(SS[6], SS[7], CK[:, :, 128], SK[:, :, 128], HRE, HIM, tabk + [kH], [ss_r.k("LH")])
        tt("dve", HRE, SS[6], R128, ALU.mult, [ss_r.k("LH"), ksm("r128")], [kH])
        tt("dve", HIM, SS[7], R128, ALU.mult, [ss_r.k("LH"), ksm("r128")], [kH])
        tt("dve", HRE, HRE, SS[4], ALU.add, [kH, ss_r.k("F")], [kH])
        tt("dve", HIM, HIM, SS[5], ALU.add, [kH, ss_r.k("F")], [kH])
    for _r in [xs_r, stat_r, utm2_r, rt_r] + xn_rs + hn2_rs + ul_rs:
        AR.free(_r)
    rt_r = AR.alloc("rottmp2", 2 * 2032)
    rt2_r = AR.alloc("rottmp3", 2 * 2032)
    xp_r = AR.alloc("Xprev", 2 * 16 * 2 * 128 // 2)
    Xprev = xp_r.bf().rearrange("p (s g r k) -> p s g r k", s=2, g=16, r=2)

    for st in range(2):
        S_rotate(U_own[:, st], uown_r.k(st))
        rot_small(SS[8], SS[9], CK[:, :, 1], SK[:, :, 1], HRE, HIM, tabk + [kH], [ss_r.k("ini")])
        scan_all((SS[8], SS[9]))
        P.dve(lambda e, st=st: e.tensor_copy(out=Xprev[:, st, :, 0, 0], in_=HRE), reads=[kH], writes=[xp_r.k(st)])
        P.dve(lambda e, st=st: e.tensor_copy(out=Xprev[:, st, :, 1, 0], in_=HIM), reads=[kH], writes=[xp_r.k(st)])
        tA = rt_r.f32(0, 2032).rearrange("p (g k) -> p g k", g=16)
        tB = rt_r.f32(2032, 2032).rearrange("p (g k) -> p g k", g=16)
        kA = rt_r.k("a"); kB = rt_r.k("b")
        ck = CK[:, :, 0:127]; sk = SK[:, :, 0:127]
        tt("dve", tA, ck, G[:, :, 0, 0:127], ALU.mult, tabk + GK, [kA])
        tt("dve", tB, sk, G[:, :, 1, 0:127], ALU.mult, tabk + GK, [kB])
        tt("dve", Xprev[:, st, :, 0, 1:128], tA, tB, ALU.subtract, [kA, kB], [xp_r.k(st)])
        tC = rt2_r.f32(0, 2032).rearrange("p (g k) -> p g k", g=16)
        tD = rt2_r.f32(2032, 2032).rearrange("p (g k) -> p g k", g=16)
        kC = rt2_r.k("c"); kD = rt2_r.k("d")
        tt("pool", tC, ck, G[:, :, 1, 0:127], ALU.mult, tabk + GK, [kC])
        tt("pool", tD, sk, G[:, :, 0, 0:127], ALU.mult, tabk + GK, [kD])
        tt("pool", Xprev[:, st, :, 1, 1:128], tC, tD, ALU.add, [kC, kD], [xp_r.k(st, "im")])
        final_state(HRE, HIM, [kH])
    AR.free(gb_r)
    AR.free(rt_r)
    AR.free(rt2_r)
    AR.free(tabs)
    AR.free(wu)

    h0_r = AR.alloc("h0", 2 * 256 + 256 + 2 * 256 + 512)
    h0n_r = AR.alloc("h0n", 2 * 2048)
    h0n = [h0n_r.f32(0, 2048), h0n_r.f32(2048, 2048)]
    h0T = [h0_r.f32(0, 256).rearrange("p (g b) -> p g b", g=16), h0_r.f32(256, 256).rearrange("p (g b) -> p g b", g=16)]
    h0b = h0_r.bf(512, 256).rearrange("p (g r b) -> p g r b", g=16, r=2)
    XN = [h0_r.f32(768, 256).rearrange("p (g b) -> p g b", g=16), h0_r.f32(1024, 256).rearrange("p (g b) -> p g b", g=16)]
    for pl, src in ((0, sre_d), (1, sim_d)):
        for gh in range(2):
            P.dma(h0n[pl][0:16].rearrange("p (g h q) -> p g h q", g=16, h=2)[:, :, gh, :], dram_ap(src, 1024 * gh, [[2048, 16], [64, 16], [1, 64]]),
                  writes=[h0n_r.k(pl)])
        bk, bkk = PS(4 + pl)

        def f(e, bk=bk, pl=pl):
            last = None
            for gq in range(16):
                last = e.matmul(bk[:, gq * 16:(gq + 1) * 16], lhsT=h0n[pl][0:16, gq * 128:(gq + 1) * 128], rhs=ident_f[0:16, 0:16], start=True, stop=True)
            return last
        P.pe(f, reads=[h0n_r.k(pl), pers.k("idf")], writes=[bkk])
        evac(h0T[pl].rearrange("p g b -> p (g b)"), bk[:, 0:256], [bkk], [h0_r.k("T", pl)], eng="act")
        P.dve(lambda e, pl=pl: e.tensor_copy(out=h0b[:, :, pl, :], in_=h0T[pl]), reads=[h0_r.k("T", pl)], writes=[h0_r.k("b")])
    AR.free(h0n_r)
    xo_r = AR.alloc("xosb", 2048)
    xo_sb = xo_r.f32()
    bS, bSk = PS(6)

    def fss(e, bS=bS):
        last = None
        for gq in range(16):
            for ri in range(2):
                c0 = (gq * 2 + ri) * 16
                e.matmul(bS[:, c0:c0 + 16], lhsT=W_P[:, gq, ri, :], rhs=U_s[:, 0, gq, :], start=True, stop=False)
                last = e.matmul(bS[:, c0:c0 + 16], lhsT=W_P[:, 16 + gq, ri, :], rhs=U_s[:, 0, 16 + gq, :], start=False, stop=True)
        return last
    P.pe(fss, reads=[wp.k(), us_r.k()], writes=[bSk])
    Sv = bS.rearrange("p (g r b) -> p g r b", g=16, r=2)
    l4r = LRE[:, :, 4].unsqueeze(2).to_broadcast([128, 16, 16])
    l4i = LIM[:, :, 4].unsqueeze(2).to_broadcast([128, 16, 16])
    vv1 = h0_r.f32(1280, 256).rearrange("p (g b) -> p g b", g=16)
    vv2 = h0_r.f32(1536, 256).rearrange("p (g b) -> p g b", g=16)
    kv1_ = h0_r.k("w1"); kv2_ = h0_r.k("w2")
    lk = [ksm("lre"), ksm("lim")]
    tt("dve", vv1, l4r, h0T[0], ALU.mult, lk + [h0_r.k("T", 0)], [kv1_])
    tt("dve", vv2, l4i, h0T[1], ALU.mult, lk + [h0_r.k("T", 1)], [kv2_])
    tt("dve", vv1, vv1, vv2, ALU.subtract, [kv1_, kv2_], [kv1_])
    tt("dve", XN[0], vv1, Sv[:, :, 0, :], ALU.add, [kv1_, bSk], [h0_r.k("X", 0)])
    tt("dve", vv1, l4r, h0T[1], ALU.mult, lk + [h0_r.k("T", 1)], [kv1_])
    tt("dve", vv2, l4i, h0T[0], ALU.mult, lk + [h0_r.k("T", 0)], [kv2_])
    tt("dve", vv1, vv1, vv2, ALU.add, [kv1_, kv2_], [kv1_])
    tt("dve", XN[1], vv1, Sv[:, :, 1, :], ALU.add, [kv1_, bSk], [h0_r.k("X", 1)])
    for pl, dst in ((0, ssre_o), (1, ssim_o)):
        for q4 in range(4):
            bk, bkk = PS(4 + q4 % 2)

            def f(e, bk=bk, pl=pl, q4=q4):
                last = None
                for j in range(4):
                    last = e.matmul(bk[0:16, j * 128:(j + 1) * 128], lhsT=XN[pl][:, 4 * q4 + j, :], rhs=ident_f, start=True, stop=True)
                return last
            P.pe(f, reads=[h0_r.k("X", pl), pers.k("idf")], writes=[bkk])
            evac(xo_sb[0:16].rearrange("p (h g q) -> p g h q", h=2, g=16)[:, 4 * q4:4 * q4 + 4, :, :],
                 bk[0:16, :].rearrange("p (g h q) -> p g h q", g=4, h=2), [bkk], [xo_r.k()], eng="act")
        P.dma(dst, xo_sb[0:16], reads=[xo_r.k()])
    AR.free(xo_r)
    AR.free(wp)
    if debug:
        dbg["Xprev"] = (xp_r, [128, 2 * 16 * 2 * 128], BF16)
        dbg["sm2"] = (sm, [128, 2816], F32)

    _s3 = {"cn": AR.alloc("s3cn", 512), "g9r": AR.alloc("s3g9r", 2304), "g9i": AR.alloc("s3g9i", 2304), "u1": AR.alloc("s3u1", 2304), "u2": AR.alloc("s3u2", 2304)}

    class _S3:
        @staticmethod
        def k(x):
            return (_s3["cn" if x.startswith("cn") else x].name, x)
    s3 = _S3
    Cn = _s3["cn"].f32(0, 512).rearrange("p (r s q) -> p r s q", r=2, s=2)
    G9R = _s3["g9r"].f32().rearrange("p (g j c) -> p g j c", g=16, j=9)
    G9I = _s3["g9i"].f32().rearrange("p (g j c) -> p g j c", g=16, j=9)
    u1 = _s3["u1"].f32().rearrange("p (g j c) -> p g j c", g=16, j=9)
    u2 = _s3["u2"].f32().rearrange("p (g j c) -> p g j c", g=16, j=9)
    for gs in range(2):
        P.dma(Cn[:, 0, gs, :].rearrange("p (h q) -> p h q", h=2), dram_ap(cre_d, 8192 * gs, [[64, 128], [16384, 2], [1, 64]]), writes=[s3.k("cn0")])
        P.dma(Cn[:, 1, gs, :].rearrange("p (h q) -> p h q", h=2), dram_ap(cim_d, 8192 * gs, [[64, 128], [16384, 2], [1, 64]]), writes=[s3.k("cn1")])
    for ri, CT in ((0, CTRE), (1, CTIM)):
        bk, bkk = PS(4 + ri)
        for gs in range(2):
            P.pe(lambda e, bk=bk, gs=gs, ri=ri: e.matmul(bk[:, gs * 128:(gs + 1) * 128], lhsT=Cn[:, ri, gs, :], rhs=ident_f, start=True, stop=True),
                 reads=[s3.k("cn%d" % ri), pers.k("idf")], writes=[bkk + (gs,)])
        P.act(lambda e, bk=bk, CT=CT: e.copy(out=CT.rearrange("p g c -> p (g c)"), in_=bk[:, 0:256]),
              reads=[bkk + (0,), bkk + (1,)], writes=[ksm("ct%d" % ri)])
    _STOP = 9
    cj = lambda a: a.unsqueeze(2).to_broadcast([128, 16, 9, 16])
    lj = lambda a: a.unsqueeze(3).to_broadcast([128, 16, 9, 16])
    if _STOP >= 2:
      tt("dve", u1, cj(CTRE), lj(LRE), ALU.mult, [ksm("ct0"), ksm("lre")], [s3.k("u1")])
      tt("dve", u2, cj(CTIM), lj(LIM), ALU.mult, [ksm("ct1"), ksm("lim")], [s3.k("u2")])
      tt("dve", G9R, u1, u2, ALU.subtract, [s3.k("u1"), s3.k("u2")], [s3.k("g9r")])
      tt("dve", u1, cj(CTRE), lj(LIM), ALU.mult, [ksm("ct0"), ksm("lim")], [s3.k("u1")])
      tt("dve", u2, cj(CTIM), lj(LRE), ALU.mult, [ksm("ct1"), ksm("lre")], [s3.k("u2")])
      tt("dve", G9I, u1, u2, ALU.add, [s3.k("u1"), s3.k("u2")], [s3.k("g9i")])

    AR.free(_s3.pop("u1"))
    AR.free(_s3.pop("u2"))
    wq = AR.alloc("wq", 32 * 2 * 128 // 2)
    W_Q = wq.bf().rearrange("p (g r m) -> p g r m", g=32, r=2)
    wm = AR.alloc("wm", 32 * 128 // 2)
    W_M = wm.bf().rearrange("p (g m) -> p g m", g=32)
    P.pool(lambda e: e.memset(wq.f32(), 0.0), writes=[wq.k()])
    _s4 = {"fp0": AR.alloc("s4fp0", 3840), "fp1": AR.alloc("s4fp1", 3840), "zz": AR.alloc("s4zz", 3840), "dc": AR.alloc("s4dc", 32)}

    class _S4:
        @staticmethod
        def k(x=None):
            return "s4all" if x is None else (_s4["dc"].name, x)
    s4 = _S4
    FPl = [_s4["fp0"].bf().rearrange("p (g m) -> p g m", g=32), _s4["fp1"].bf().rearrange("p (g m) -> p g m", g=32)]
    ZZ = _s4["zz"].bf().rearrange("p (r g m) -> p r g m", r=2, g=16)
    DC = _s4["dc"].f32(0, 32)
    S4K = ["s4all"] + [_s4[n].k() for n in ("fp0", "fp1", "zz")]
    for _nm in ("fp0", "fp1", "zz"):
        P.pool(lambda e, _nm=_nm: e.memset(_s4[_nm].f32(), 0.0), writes=S4K)
    for s_ in range(8):
        P.dma(DC[16 * s_:16 * s_ + 16, :], dram_ap(dd_d, 0, [[1, 16], [16, 32]]), writes=[s4.k("dc")], allow_slow_non_contiguous=True)
    for gh in (range(2) if _STOP >= 3 else []):
        hs = slice(64 * gh, 64 * gh + 64)
        gsl = slice(16 * gh, 16 * gh + 16)
        P.act(lambda e, hs=hs, gsl=gsl: e.copy(out=W_Q[hs, gsl, 0, :].rearrange("p g (j c) -> p g j c", j=8), in_=G9R[hs, :, 1:9, :]),
              reads=[s3.k("g9r")], writes=[wq.k()])
        P.act(lambda e, hs=hs, gsl=gsl: e.mul(out=W_Q[hs, gsl, 1, :].rearrange("p g (j c) -> p g j c", j=8), in_=G9I[hs, :, 1:9, :], mul=-1.0),
              reads=[s3.k("g9i")], writes=[wq.k()])
        P.act(lambda e, hs=hs, gsl=gsl: e.copy(out=FPl[0][hs, gsl, 112:240].rearrange("p g (j c) -> p g j c", j=8), in_=G9R[hs, :, 0:8, :]),
              reads=[s3.k("g9r")], writes=S4K)
        P.act(lambda e, hs=hs, gsl=gsl: e.mul(out=FPl[1][hs, gsl, 112:240].rearrange("p g (j c) -> p g j c", j=8), in_=G9I[hs, :, 0:8, :], mul=-1.0),
              reads=[s3.k("g9i")], writes=S4K)
    P.dve(lambda e: e.tensor_copy(out=ZZ[:, 0, :, 112:128], in_=BRE), reads=[ksm("bre")], writes=S4K)
    P.dve(lambda e: e.tensor_copy(out=ZZ[:, 1, :, 112:128], in_=BIM), reads=[ksm("bim")], writes=S4K)
    for g in (range(32) if _STOP >= 4 else []):
        gq = g % 16
        bk, bkk = PS(4 + (g // 4) % 4)
        o_ = bk[:, (g % 4) * 128:(g % 4) * 128 + 128]

        def f(e, o_=o_, g=g, gq=gq):
            last = None
            for s_ in range(8):
                lo = 112 - 16 * s_
                for ri in range(2):
                    last = e.matmul(o_, lhsT=ZZ[:, ri, gq, lo:lo + 128], rhs=FPl[ri][:, g, lo:lo + 128],
                                    start=(s_ == 0 and ri == 0), stop=(s_ == 7 and ri == 1))
            return last
        P.pe(f, reads=S4K, writes=[bkk + (g % 4,)])
        if True:
            P.dve(lambda e, o_=o_, g=g: e.scalar_tensor_tensor(out=W_M[:, g, :], in0=ident_f, scalar=DC[:, g:g + 1], in1=o_,
                                                           op0=ALU.mult, op1=ALU.add),
              reads=[bkk + (g % 4,), s4.k("dc"), pers.k("idf")], writes=[wm.k()])
    for _r in list(_s3.values()) + list(_s4.values()):
        AR.free(_r)
    for gh in range(2):
        P.dma(dram_ap(spre_o, 1024 * gh, [[1, 64], [64, 16]]), HRE[64 * gh:64 * gh + 64, :], reads=[kH], allow_slow_non_contiguous=True)
        P.dma(dram_ap(spim_o, 1024 * gh, [[1, 64], [64, 16]]), HIM[64 * gh:64 * gh + 64, :], reads=[kH], allow_slow_non_contiguous=True)
    if debug:
        dbg["W_Q"] = (wq, [128, 32 * 2 * 128], BF16)
        dbg["W_M"] = (wm, [128, 32 * 128], BF16)

    gl_r = AR.alloc("gl", 2 * 8 * 512 // 2 + 4 * 512 // 2)
    gl_tm8 = gl_r.bf(0, 4096).rearrange("p (s t g c) -> p s t g c", s=2, t=8, g=32)
    gl_s = gl_r.bf(4096, 1024).rearrange("p (t g c) -> p t g c", t=4, g=32)
    yw = AR.alloc("ywork", 4 * 512)
    yA = yw.f32(0, 512); yB = yw.f32(512, 512); yC = yw.f32(1024, 512); yD = yw.f32(1536, 512)
    GC0 = 0.7978845608028654
    GC1 = 0.044715

    def gelu_bank(bk, bkk, nrows, out_ap, out_key):
        pr = slice(0, nrows)
        P.act(lambda e: e.activation(out=yA[pr], in_=bk[pr], func=AF.Copy, scale=0.5), reads=[bkk], writes=[yw.k("a")])
        P.act(lambda e: e.activation(out=yB[pr], in_=bk[pr], func=AF.Square), reads=[bkk], writes=[yw.k("b"), yw.k("b2")])
        P.act(lambda e: e.activation(out=yD[pr], in_=yB[pr], func=AF.Identity, scale=GC1, bias=1.0), reads=[yw.k("b")], writes=[yw.k("d")])
        tt("dve", yB[pr], yD[pr], yA[pr], ALU.mult, [yw.k("d"), yw.k("a")], [yw.k("b2")])
        P.act(lambda e: e.activation(out=yC[pr], in_=yB[pr], func=AF.Tanh, scale=2.0 * GC0), reads=[yw.k("b2")], writes=[yw.k("c")])
        g_ = out_ap.shape[1]
        t_ = out_ap.shape[2]
        P.dve(lambda e: e.tensor_scalar(out=yC[pr], in0=yC[pr], scalar1=1.0, scalar2=None, op0=ALU.add), reads=[yw.k("c")], writes=[yw.k("c")])
        tt("dve", out_ap, yC[pr].rearrange("p (g t c) -> p g t c", g=g_, t=t_), yA[pr].rearrange("p (g t c) -> p g t c", g=g_, t=t_),
           ALU.mult, [yw.k("c"), yw.k("a")], [out_key])

    for st in range(2):
        for q8 in range(8):
            bk, bkk = PS(q8 % 2)

            def f(e, bk=bk, st=st, q8=q8):
                last = None
                for j in range(4):
                    g = 4 * q8 + j
                    gq = g % 16
                    o_ = bk[:, j * 128:(j + 1) * 128]
                    e.matmul(o_, lhsT=U_own[:, st, g, :], rhs=W_M[:, g, :], start=True, stop=False)
                    e.matmul(o_, lhsT=Xprev[:, st, gq, 0, :], rhs=W_Q[:, g, 0, :], start=False, stop=False)
                    last = e.matmul(o_, lhsT=Xprev[:, st, gq, 1, :], rhs=W_Q[:, g, 1, :], start=False, stop=True)
                return last
            P.pe(f, reads=[uown_r.k(st), xp_r.k(st), xp_r.k(st, "im"), wm.k(), wq.k()], writes=[bkk])
            out_ap = gl_tm8[:, st, :, 4 * q8:4 * q8 + 4, :].rearrange("p t g c -> p g t c")
            gelu_bank(bk, bkk, 128, out_ap, gl_r.k(st))
    for q8 in range(4):
        bk, bkk = PS(q8 % 2)

        def f(e, bk=bk, q8=q8):
            last = None
            for j in range(8):
                g = 8 * q8 + j
                gq = g % 16
                o_ = bk[0:16, j * 64:(j + 1) * 64]
                e.matmul(o_, lhsT=U_s[:, 1, g, :], rhs=W_M[:, g, 0:64], start=True, stop=False)
                e.matmul(o_, lhsT=h0b[:, gq, 0, :], rhs=W_Q[:, g, 0, 0:64], start=False, stop=False)
                last = e.matmul(o_, lhsT=h0b[:, gq, 1, :], rhs=W_Q[:, g, 1, 0:64], start=False, stop=True)
            return last
        P.pe(f, reads=[us_r.k(), h0_r.k("b"), wm.k(), wq.k()], writes=[bkk])
        out_ap = gl_s[0:16, :, 8 * q8:8 * q8 + 8, :].rearrange("p t g c -> p g t c")
        gelu_bank(bk, bkk, 16, out_ap, gl_r.k("s"))
    for _r in (sm, wq, wm, xp_r, uown_r, us_r, h0_r, ss_r, yw):
        AR.free(_r)
    if debug:
        dbg["gl"] = (gl_r, [128, 4096 + 1024], BF16)

    x1_r = AR.alloc("x1", 17 * 1024)
    x1 = x1_r.f32().rearrange("p (t d) -> p t d", t=17)
    wgl_r = AR.alloc("wglu", 4 * 512 // 2)
    W_GLU = wgl_r.bf().rearrange("p (k c) -> p k c", k=4)
    wo_r = AR.alloc("wout", 8 * 1024 // 2)
    W_OUT = wo_r.bf().rearrange("p (k c) -> p k c", k=8)
    wst2 = AR.alloc("wstage2", 2 * 1024)
    for kt in range(4):
        stg = wst2.f32(1024 * (kt % 2), 512)
        P.dma(stg, wglu_d[kt * 128:(kt + 1) * 128, :], writes=[wst2.k(kt % 2)])
        evac(W_GLU[:, kt, :], stg, [wst2.k(kt % 2)], [wgl_r.k()])
    for kt in range(8):
        stg = wst2.f32(1024 * (kt % 2), 1024)
        P.dma(stg, wout_d[kt * 128:(kt + 1) * 128, :], writes=[wst2.k(kt % 2)])
        evac(W_OUT[:, kt, :], stg, [wst2.k(kt % 2)], [wo_r.k()])
    AR.free(wst2)
    cw = AR.alloc("cwork", 2048 + 2048 + 512 + 512 + 256 + 1024 + 64 + 2048 + 1024 + 16 + 1024 + 16 + 512)
    glT = cw.bf(0, 2048).rearrange("p (k n) -> p k n", k=4)
    oT = cw.f32(2048, 2048).rearrange("p (k n) -> p k n", k=4)
    ctmp = cw.f32(4096, 512)
    rstd_t = cw.f32(4608, 512)
    osq = cw.bf(5120, 256)
    mTs = cw.bf(5376, 1024).rearrange("p (k n) -> p k n", k=4)
    ones_b = cw.bf(6400, 64)
    xst = cw.f32(6464, 2048)
    osq4 = cw.bf(8512, 1024).rearrange("p (k n) -> p k n", k=4)
    rtok = cw.f32(9536, 16)
    mTs_l = [mTs, cw.bf(9552, 1024).rearrange("p (k n) -> p k n", k=4)]
    rtok_l = [rtok, cw.f32(10576, 16)]
    ctmp_l = [ctmp, cw.f32(10592, 512)]
    P.pool(lambda e: e.memset(ones_b, 1.0), writes=[cw.k("ones")])
    P.dve(lambda e: e.tensor_scalar(out=gsb[:, 4:8], in0=gsb[:, 4:8], scalar1=0.5, scalar2=None, op0=ALU.mult), reads=[gains.k("bg")], writes=[gains.k("bg")])

    def ssm_tail(ntok, tok0, c0, bi=0):
        mTs = mTs_l[bi]
        rtok = rtok_l[bi]
        for ko in range(4):
            bk, bkk = PS(2 + ko % 2)
            ct = ctmp_l[ko % 2]
            kct = cw.k("ctmp", ko % 2)

            def f(e, bk=bk, ko=ko):
                last = None
                for kt in range(4):
                    last = e.matmul(bk[:, 0:ntok], lhsT=W_GLU[:, kt, ko * 128:(ko + 1) * 128], rhs=glT[:, kt, c0:c0 + ntok], start=(kt == 0), stop=(kt == 3))
                return last
            P.pe(f, reads=[wgl_r.k(), cw.k("glT")], writes=[bkk])
            P.act(lambda e, bk=bk, ko=ko, ct=ct: e.activation(out=ct[:, 0:ntok], in_=bk[:, 0:ntok], func=AF.Tanh, scale=0.5, bias=gsb[:, 4 + ko:5 + ko]),
                  reads=[bkk, gains.k("bg")], writes=[kct])
            P.dve(lambda e, ct=ct: e.tensor_scalar(out=ct[:, 0:ntok], in0=ct[:, 0:ntok], scalar1=0.5, scalar2=0.5, op0=ALU.mult, op1=ALU.add),
                  reads=[kct], writes=[kct])
            tt("dve", oT[:, ko, 0:ntok], ct[:, 0:ntok], glT[:, ko, c0:c0 + ntok], ALU.mult, [kct, cw.k("glT")], [cw.k("oT", ko)])
        for ko in range(4):
            P.act(lambda e, ko=ko: e.activation(out=osq4[:, ko, 0:ntok], in_=oT[:, ko, 0:ntok], func=AF.Square), reads=[cw.k("oT", ko)], writes=[cw.k("osq", ko)])
            P.dve(lambda e, ko=ko: e.tensor_scalar(out=mTs[:, ko, 0:ntok], in0=oT[:, ko, 0:ntok], scalar1=gsb[:, ko:ko + 1], scalar2=None, op0=ALU.mult),
                  reads=[cw.k("oT", ko), gains.k("gs")], writes=[cw.k("mTs", bi)])
        bs, bsk = PS(4)
        ntl = (ntok + 127) // 128

        def frs(e, bs=bs):
            last = None
            for tl in range(ntl):
                nr = min(128, ntok - 128 * tl)
                for ko in range(4):
                    last = e.matmul(bs[0:nr, tl:tl + 1], lhsT=osq4[:, ko, 128 * tl:128 * tl + nr], rhs=ones_b[:, 0:1], start=(ko == 0), stop=(ko == 3))
            return last
        P.pe(frs, reads=[cw.k("osq", k_) for k_ in range(4)] + [cw.k("ones")], writes=[bsk])
        nr0 = min(128, ntok)
        P.dve(lambda e: e.tensor_scalar(out=rtok[0:nr0, 0:ntl], in0=bs[0:nr0, 0:ntl], scalar1=1.0 / 512, scalar2=EPS, op0=ALU.mult, op1=ALU.add),
              reads=[bsk], writes=[cw.k("rtok", bi)])
        P.pool(lambda e: e.tensor_tensor(out=rtok[0:nr0, 0:ntl], in0=rtok[0:nr0, 0:ntl], in1=mhalf[0:nr0].to_broadcast([nr0, ntl]), op=ALU.pow),
               reads=[cw.k("rtok", bi), pers.k("mh")], writes=[cw.k("rtok", bi)])

    def out_proj_tile(nrows, tok0, mcol0, tile_idx, bi=0):
        mTs = mTs_l[bi]
        rtok = rtok_l[bi]
        pr = slice(0, nrows)
        xs_ = xst[:, 1024 * (tile_idx % 2):1024 * (tile_idx % 2) + 1024]
        tl = mcol0 // 128
        P.dma(xs_[pr], xo[tok0:tok0 + nrows, :], writes=[cw.k("xst", tile_idx % 2)])
        for hf in range(2):
            bka, bkak = PS(5 + hf)
            bks, bksk = PS(7 if hf == 0 else 1)

            def f(e, bka=bka, bks=bks, hf=hf):
                last = None
                for kt in range(4):
                    e.matmul(bka[pr, :], lhsT=mT_att[:, kt, tok0:tok0 + nrows], rhs=W_OUT[:, kt, 512 * hf:512 * hf + 512], start=(kt == 0), stop=(kt == 3))
                for kt in range(4):
                    last = e.matmul(bks[pr, :], lhsT=mTs[:, kt, mcol0:mcol0 + nrows], rhs=W_OUT[:, 4 + kt, 512 * hf:512 * hf + 512], start=(kt == 0), stop=(kt == 3))
                return last
            P.pe(f, reads=[mT_r.k(), cw.k("mTs", bi), wo_r.k()], writes=[bkak, bksk])
            xo_ = x1[pr, tile_idx, 512 * hf:512 * hf + 512]
            P.dve(lambda e, bks=bks, hf=hf, xo_=xo_: e.scalar_tensor_tensor(out=xo_, in0=bks[pr, :], scalar=rtok[pr, tl:tl + 1], in1=xs_[pr, 512 * hf:512 * hf + 512],
                                                                      op0=ALU.mult, op1=ALU.add),
                  reads=[bksk, cw.k("rtok", bi), cw.k("xst", tile_idx % 2)], writes=[x1_r.k(tile_idx)])
            tt("dve", xo_, bka[pr, :], xo_, ALU.add, [bkak, x1_r.k(tile_idx)], [x1_r.k(tile_idx)])

    for st in range(2):
        for kt in range(4):
            for a in range(2):
                bk, bkk = PS(a)

                def f(e, bk=bk, kt=kt, a=a, st=st):
                    last = None
                    for j in range(4):
                        t = 4 * a + j
                        last = e.matmul(bk[:, j * 128:(j + 1) * 128], lhsT=gl_tm8[:, st, t, 8 * kt:8 * kt + 8, :].rearrange("p g c -> p (g c)"), rhs=ident_b,
                                        start=True, stop=True)
                    return last
                P.pe(f, reads=[gl_r.k(st), pers.k("idb")], writes=[bkk])
                evac(glT[:, kt, :].rearrange("p (k t) -> p t k", t=8)[:, 4 * a:4 * a + 4, :], bk.rearrange("p (t k) -> p t k", t=4), [bkk], [cw.k("glT")])
        for hf in range(2):
            gi_ = 2 * st + hf
            ssm_tail(512, st * ST + hf * 512, hf * 512, gi_ % 2)
            if gi_ > 0:
                pst, phf = (gi_ - 1) // 2, (gi_ - 1) % 2
                for tl in range(4):
                    tok0 = pst * ST + phf * 512 + tl * 128
                    out_proj_tile(128, tok0, tl * 128, tok0 // 128, (gi_ - 1) % 2)
    for tl in range(4):
        tok0 = 1 * ST + 512 + tl * 128
        out_proj_tile(128, tok0, tl * 128, tok0 // 128, 1)
    bk, bkk = PS(0)

    def fsT(e, bk=bk):
        last = None
        for kt in range(4):
            for t in range(4):
                c = (kt * 4 + t) * 16
                last = e.matmul(bk[:, c:c + 16], lhsT=gl_s[0:16, t, 8 * kt:8 * kt + 8, :].rearrange("p g c -> p (g c)"), rhs=ident_b[0:16, 0:16], start=True, stop=True)
        return last
    P.pe(fsT, reads=[gl_r.k("s"), pers.k("idb")], writes=[bkk])
    evac(glT[:, :, 0:64].rearrange("p k (b t) -> p k t b", t=4), bk[:, 0:256].rearrange("p (k t b) -> p k t b", k=4, t=4), [bkk], [cw.k("glT")])
    ssm_tail(64, NP, 0, 0)
    out_proj_tile(64, NP, 0, 16, 0)
    AR.free(cw)
    AR.free(gl_r)
    AR.free(mT_r)
    AR.free(wgl_r)
    AR.free(wo_r)
    if debug:
        dbg["x1"] = (x1_r, [128, 17 * 1024], F32)

    AR.free(gains)
    g2 = AR.alloc("gains2", 2048)
    g_mlp = g2.f32(0, 1024)
    g_fin = g2.f32(1024, 1024)
    P.dma(g_mlp, bc_row(gmlp_d, 1024), writes=[g2.k("mlp")])
    P.dma(g_fin, bc_row(gfin_d, 1024), writes=[g2.k("fin")])
    hm_r = AR.alloc("hmT", 8 * NT // 2)
    hmT = hm_r.bf().rearrange("p (k n) -> p k n", k=8)
    dw = AR.alloc("dwork", 512 + 256)
    hmb = dw.bf(0, 512)
    hmb2_r = AR.alloc("hmb2", 512)
    dst_ = dw.f32(512, 256)

    def rms_stats(tile_idx, nrows, junk_bf, slot):
        pr = slice(0, nrows)
        sc = dst_[:, 4 * (slot % 64):4 * (slot % 64) + 4]
        ks = dw.k("st", slot % 64)
        P.act(lambda e: e.activation(out=junk_bf[pr], in_=x1[pr, tile_idx, :], func=AF.Square, accum_out=sc[pr, 0:1]),
              reads=[x1_r.k(tile_idx)], writes=[dw.k("hmb"), ks])
        P.dve(lambda e: e.tensor_scalar(out=sc[pr, 1:2], in0=sc[pr, 0:1], scalar1=1.0 / 1024, scalar2=EPS, op0=ALU.mult, op1=ALU.add), reads=[ks], writes=[ks + ("a",)])
        P.pool(lambda e: e.tensor_tensor(out=sc[pr, 2:3], in0=sc[pr, 1:2], in1=mhalf[pr], op=ALU.pow), reads=[ks + ("a",), pers.k("mh")], writes=[ks + ("b",)])
        return sc, ks + ("b",)

    hmbs = [dw.bf(0, 512), hmb2_r.bf(0, 512)]

    def hm_front(ti):
        nrows = 128 if ti < 16 else 64
        pr = slice(0, nrows)
        hb_ = hmbs[ti % 2]
        kh_ = ("hmbk", ti % 2)
        sc = dst_[:, 4 * (ti % 64):4 * (ti % 64) + 4]
        ks = dw.k("st", ti % 64)
        P.act(lambda e: e.activation(out=hb_[pr], in_=x1[pr, ti, :], func=AF.Square, accum_out=sc[pr, 0:1]),
              reads=[x1_r.k(ti)], writes=[kh_, ks])
        P.dve(lambda e: e.tensor_scalar(out=sc[pr, 1:2], in0=sc[pr, 0:1], scalar1=1.0 / 1024, scalar2=EPS, op0=ALU.mult, op1=ALU.add), reads=[ks], writes=[ks + ("a",)])
        P.pool(lambda e: e.tensor_tensor(out=sc[pr, 2:3], in0=sc[pr, 1:2], in1=mhalf[pr], op=ALU.pow), reads=[ks + ("a",), pers.k("mh")], writes=[ks + ("b",)])
        P.dve(lambda e: e.scalar_tensor_tensor(out=hb_[pr], in0=x1[pr, ti, :], scalar=sc[pr, 2:3], in1=g_mlp[pr], op0=ALU.mult, op1=ALU.mult),
              reads=[x1_r.k(ti), ks + ("b",), g2.k("mlp")], writes=[kh_])

    def hm_back(ti):
        nrows = 128 if ti < 16 else 64
        pr = slice(0, nrows)
        hb_ = hmbs[ti % 2]
        kh_ = ("hmbk", ti % 2)
        for half in range(2):
            bk, bkk = PS(half)

            def f(e, bk=bk, half=half):
                last = None
                for j in range(4):
                    kt = 4 * half + j
                    last = e.matmul(bk[:, j * 128:j * 128 + nrows], lhsT=hb_[pr, kt * 128:(kt + 1) * 128], rhs=ident_b[pr, pr], start=True, stop=True)
                return last
            P.pe(f, reads=[kh_, pers.k("idb")], writes=[bkk])
            evac(hmT[:, 4 * half:4 * half + 4, 128 * ti:128 * ti + nrows], bk.rearrange("p (j n) -> p j n", j=4)[:, :, 0:nrows], [bkk], [hm_r.k()])

    NCH = 8
    FT = 4
    wup_rs = [AR.alloc("wup%d" % i, 8 * 512 // 2) for i in range(2)]
    wdn_rs = [AR.alloc("wdn%d" % i, FT * 1024 // 2) for i in range(2)]
    wst3 = AR.alloc("wstage3", 3 * 1024)
    hT_rs = [AR.alloc("hTa", FT * 512 // 2), AR.alloc("hTb", FT * 512 // 2)]
    rl_r = AR.alloc("relu", 2 * 512)
    W_UP = [wup_rs[i].bf().rearrange("p (k c) -> p k c", k=8) for i in range(2)]
    W_DN = [wdn_rs[i].bf().rearrange("p (k c) -> p k c", k=FT) for i in range(2)]
    hT = [hT_rs[i].bf().rearrange("p (k n) -> p k n", k=FT) for i in range(2)]

    class _WK:
        def __init__(self, rs):
            self.rs = rs

        def k(self, b):
            return self.rs[b].k()
    wup_r = _WK(wup_rs)
    wdn_r = _WK(wdn_rs)
    sctr = {"n": 0}

    NSTG = 3

    def chunk_jobs(c):
        b = c % 2
        jobs = []
        for kt in range(8):
            st_ = {}

            def d(kt=kt, st_=st_):
                i = sctr["n"] % NSTG
                sctr["n"] += 1
                st_["i"] = i
                P.dma(wst3.f32(1024 * i, 512), wup_d[kt * 128:(kt + 1) * 128, 512 * c:512 * c + 512], writes=[wst3.k(i)])

            def cst(kt=kt, st_=st_):
                i = st_["i"]
                evac(W_UP[b][:, kt, :], wst3.f32(1024 * i, 512), [wst3.k(i)], [wup_r.k(b)], eng="act")
            jobs.append((d, cst))
        for ft in range(FT):
            st_ = {}

            def d(ft=ft, st_=st_):
                i = sctr["n"] % NSTG
                sctr["n"] += 1
                st_["i"] = i
                r0 = 512 * c + 128 * ft
                P.dma(wst3.f32(1024 * i, 1024), wdn_d[r0:r0 + 128, :], writes=[wst3.k(i)])

            def cst(ft=ft, st_=st_):
                i = st_["i"]
                evac(W_DN[b][:, ft, :], wst3.f32(1024 * i, 1024), [wst3.k(i)], [wdn_r.k(b)], eng="act")
            jobs.append((d, cst))
        return jobs

    class JobRunner:
        LAG = 2

        def __init__(self):
            self.q = []
            self.nd = 0
            self.nc_ = 0

        def add(self, jobs):
            self.q += jobs

        def step(self):
            while self.nd < len(self.q) and self.nd < self.nc_ + self.LAG:
                self.q[self.nd][0]()
                self.nd += 1
            if self.nc_ < self.nd:
                self.q[self.nc_][1]()
                self.nc_ += 1

        def pending(self):
            return self.nc_ < len(self.q)

    groups = [(512 * i, 512) for i in range(4)] + [(NP, 64)]
    ys_r = AR.alloc("yst", 2048)
    ysts = [ys_r.f32(0, 1024), ys_r.f32(1024, 1024)]
    junk2_r = AR.alloc("junk2", 512)

    def final_tiles(tis):
        pre = rms_stats(tis[0], 128 if tis[0] < 16 else 64, junk2_r.bf(), 32 + tis[0])
        for n_, ti in enumerate(tis):
            nrows = 128 if ti < 16 else 64
            pr = slice(0, nrows)
            sc, kb_ = pre
            if n_ + 1 < len(tis):
                t2 = tis[n_ + 1]
                pre = rms_stats(t2, 128 if t2 < 16 else 64, junk2_r.bf(), 32 + t2)
            yst = ysts[ti % 2]
            P.dve(lambda e, ti=ti, sc=sc, pr=pr, yst=yst: e.scalar_tensor_tensor(out=yst[pr], in0=x1[pr, ti, :], scalar=sc[pr, 2:3], in1=g_fin[pr], op0=ALU.mult, op1=ALU.mult),
                  reads=[x1_r.k(ti), kb_, g2.k("fin")], writes=[ys_r.k(ti % 2)])
            P.dma(y_o[128 * ti:128 * ti + nrows, :], yst[pr], reads=[ys_r.k(ti % 2)])
    JR = JobRunner()
    JR.add(chunk_jobs(0))
    JR.step()
    hm_front(0)
    for ti in range(17):
        if ti + 1 < 17:
            hm_front(ti + 1)
        hm_back(ti)
        JR.step()
    while JR.pending():
        JR.step()
    gi = 0
    for c in range(NCH):
        if c + 1 < NCH:
            JR.add(chunk_jobs(c + 1))
        b = c % 2
        for (tok0, ntok) in groups:
            hb = gi % 2
            gi += 1
            for ft in range(FT):
                bk, bkk = PS(ft % 2)

                def f(e, bk=bk, ft=ft, b=b, tok0=tok0, ntok=ntok):
                    last = None
                    for kt in range(8):
                        last = e.matmul(bk[:, 0:ntok], lhsT=W_UP[b][:, kt, ft * 128:(ft + 1) * 128], rhs=hmT[:, kt, tok0:tok0 + ntok], start=(kt == 0), stop=(kt == 7))
                    return last
                P.pe(f, reads=[wup_r.k(b), hm_r.k()], writes=[bkk])
                rl = rl_r.f32(512 * (ft % 2), 512)
                P.act(lambda e, bk=bk, rl=rl, ntok=ntok: e.activation(out=rl[:, 0:ntok], in_=bk[:, 0:ntok], func=AF.Relu), reads=[bkk], writes=[rl_r.k(ft % 2)])
                tt("dve", hT[hb][:, ft, 0:ntok], rl[:, 0:ntok], rl[:, 0:ntok], ALU.mult, [rl_r.k(ft % 2)], [hT_rs[hb].k()])
                JR.step()
            ntile = (ntok + 127) // 128
            for tl in range(ntile):
                nrows = min(128, ntok - 128 * tl)
                pr = slice(0, nrows)
                ti = (tok0 + 128 * tl) // 128
                for hf in range(2):
                    bk, bkk = PS(2 + 2 * (tl % 2) + hf)

                    def f(e, bk=bk, hf=hf, hb=hb, b=b, tl=tl, nrows=nrows, pr=pr):
                        last = None
                        for ft in range(FT):
                            last = e.matmul(bk[pr, :], lhsT=hT[hb][:, ft, 128 * tl:128 * tl + nrows], rhs=W_DN[b][:, ft, 512 * hf:512 * hf + 512], start=(ft == 0), stop=(ft == FT - 1))
                        return last
                    P.pe(f, reads=[hT_rs[hb].k(), wdn_r.k(b)], writes=[bkk])
                    tt("dve", x1[pr, ti, 512 * hf:512 * hf + 512], bk[pr, :], x1[pr, ti, 512 * hf:512 * hf + 512], ALU.add, [bkk, x1_r.k(ti)], [x1_r.k(ti)])
            if c == NCH - 1:
                final_tiles([(tok0 + 128 * tl_) // 128 for tl_ in range(ntile)])

    if debug:
        import os
        want = os.environ.get("DBG", "").split(",")
        keep = set(v[0].name for k, v in dbg.items() if k in want) | {"pers"}
        for nm in list(AR.live.keys()):
            if nm not in keep:
                o_, n_ = AR.live[nm]
                AR.free(Region(AR, nm, o_, n_))
        for name, (reg, shape, dt) in dbg.items():
            if name not in want or reg.name not in AR.live:
                continue
            o = nc.dram_tensor("dbg_" + name, list(shape), F32, kind="ExternalOutput").ap()
            if dt == BF16:
                tmp = AR.alloc("dbgtmp_" + name, shape[1])
                P.dve(lambda e, tmp=tmp, reg=reg: e.tensor_copy(out=tmp.f32(), in_=reg.bf()), reads=[reg.k()], writes=[tmp.k()])
                P.dma(o, tmp.f32(), reads=[tmp.k()])
                AR.free(tmp)
            else:
                P.dma(o, reg.f32(0, shape[1]), reads=[reg.k()])
    P.emit()
    return nc


def _bucket(n):
    n = np.maximum(n, 0)
    nf = np.maximum(n, 16).astype(np.float32)
    large = 16 + (np.log(nf / np.float32(16)) / np.float32(math.log(8.0)) * np.float32(16)).astype(np.int32)
    large = np.minimum(large, 31)
    return np.where(n < 16, n, large)


def _static_consts():
    c = np.zeros((128, 160), np.float32)
    c[:, 0:9] = np.arange(9)
    c[:, 16:145] = np.arange(129)
    c[:, 146] = 7 - (np.arange(128) // 16)
    oh = np.zeros((33, 384), np.float32)
    for j in range(384):
        d = j - 127
        if 0 <= d <= 127:
            oh[int(_bucket(np.array(d))), j] = 1.0
        else:
            oh[32, j] = NEG
    mb = np.full((64, 64), NEG, np.float32)
    jm = np.zeros((128, 192), np.float32)
    jm[np.arange(128), 127 - np.arange(128)] = 1.0
    for b in range(16):
        mb[4 * b:4 * b + 4, 4 * b:4 * b + 4] = 0.0
        for t_ in range(4):
            jm[3 - t_, 128 + 4 * b + t_] = 1.0
    return c, oh, mb, jm


def make_core_inputs(inputs, c):
    f = lambda a: np.ascontiguousarray(np.asarray(a), dtype=np.float32)
    seq, m = c // 4, c % 4
    xpr = np.asarray(inputs["x_prompt"])
    xsm = np.asarray(inputs["x_sample"])
    cst, oh, mb, jm = _static_consts()
    d = {}
    d["xo"] = np.concatenate([xpr[seq, 2048 * m:2048 * (m + 1)], xsm[16 * c:16 * c + 16].reshape(64, 1024)], 0)
    xp = np.zeros((NLITE * ST, D), np.float32)
    if m > 0:
        xp[NLITE * ST - 2048 * m:] = xpr[seq, 0:2048 * m]
    d["xp"] = xp
    d["cache_k"] = np.asarray(inputs["cache_k"])[0, 16 * c:16 * c + 16].reshape(16, 128, 128)
    d["cache_v"] = np.asarray(inputs["cache_v"])[0, 16 * c:16 * c + 16].reshape(16, 128, 128)
    d["st_re"] = np.asarray(inputs["state_ssm_re"])[0, 16 * c:16 * c + 16].reshape(16, 2048)
    d["st_im"] = np.asarray(inputs["state_ssm_im"])[0, 16 * c:16 * c + 16].reshape(16, 2048)
    d["rel_bias"] = inputs["rel_bias"]
    d["norm_mix"] = inputs["norm_mix"]
    d["w_in"] = np.asarray(inputs["w_in"])[0]
    d["sinks"] = inputs["attn_sinks"]
    d["a_re"] = np.asarray(inputs["ssm_a_re"])[0]
    d["a_im"] = np.asarray(inputs["ssm_a_im"])[0]
    d["log_step"] = inputs["ssm_log_step"]
    d["b_re"] = np.asarray(inputs["ssm_b_re"])[0]
    d["b_im"] = np.asarray(inputs["ssm_b_im"])[0]
    d["c_re"] = np.asarray(inputs["ssm_c_re"])[0]
    d["c_im"] = np.asarray(inputs["ssm_c_im"])[0]
    d["ssm_d"] = np.asarray(inputs["ssm_d"])[0]
    d["w_glu"] = np.asarray(inputs["w_glu"])[0]
    d["b_glu"] = inputs["b_glu"]
    d["norm_attn"] = inputs["norm_attn_out"]
    d["norm_ssm"] = inputs["norm_ssm_out"]
    d["w_out"] = np.asarray(inputs["w_out"])[0]
    d["norm_mlp"] = inputs["norm_mlp"]
    d["w_up"] = np.asarray(inputs["w_up"])[0]
    d["w_down"] = np.asarray(inputs["w_down"])[0]
    d["norm_final"] = np.asarray(inputs["norm_final"]).reshape(1, D)
    d["consts"] = cst
    d["onehot"] = oh
    d["maskB"] = mb
    d["jmat"] = jm
    d["halo_mask"] = np.full((128, 1), NEG if m == 0 else 0.0, np.float32)
    return {k: f(v) for k, v in d.items()}


_NC_CACHE = {}


def kernel(**inputs):
    if "nc" not in _NC_CACHE:
        _NC_CACHE["nc"] = build_program(debug=False)
    nc = _NC_CACHE["nc"]
    in_maps = [make_core_inputs(inputs, c) for c in range(NCORES)]
    res = run_bass_kernel_spmd(nc, in_maps, core_ids=list(range(NCORES)))
    R = res.results
    y_prompt = np.zeros((2, 8192, D), np.float32)
    y_sample = np.zeros((128, 4, D), np.float32)
    nkp = np.zeros((1, 2, 128, 2, 64), np.float32)
    nvp = np.zeros((1, 2, 128, 2, 64), np.float32)
    srp = np.zeros((1, 2, 32, 64), np.float32)
    sip = np.zeros((1, 2, 32, 64), np.float32)
    nks = np.zeros((1, 128, 128, 2, 64), np.float32)
    nvs = np.zeros((1, 128, 128, 2, 64), np.float32)
    srs = np.zeros((1, 128, 32, 64), np.float32)
    sis = np.zeros((1, 128, 32, 64), np.float32)
    for c in range(NCORES):
        seq, m = c // 4, c % 4
        r = R[c]
        y = np.asarray(r["y"])
        y_prompt[seq, 2048 * m:2048 * (m + 1)] = y[0:2048]
        y_sample[16 * c:16 * c + 16] = y[2048:].reshape(16, 4, D)
        if m == 3:
            nkp[0, seq] = np.asarray(r["nk_p"]).reshape(128, 2, 64)
            nvp[0, seq] = np.asarray(r["nv_p"]).reshape(128, 2, 64)
            srp[0, seq] = np.asarray(r["sp_re"])
            sip[0, seq] = np.asarray(r["sp_im"])
        nks[0, 16 * c:16 * c + 16] = np.asarray(r["nk_s"]).reshape(16, 128, 2, 64)
        nvs[0, 16 * c:16 * c + 16] = np.asarray(r["nv_s"]).reshape(16, 128, 2, 64)
        srs[0, 16 * c:16 * c + 16] = np.asarray(r["ss_re"]).reshape(16, 32, 64)
        sis[0, 16 * c:16 * c + 16] = np.asarray(r["ss_im"]).reshape(16, 32, 64)
    return (y_prompt, y_sample, nkp, nvp, srp, sip, nks, nvs, srs, sis)
```

```python
import math
import numpy as np
import concourse.bass as bass
import concourse.mybir as mybir
from concourse.bass_utils import run_bass_kernel_spmd

F32 = mybir.dt.float32
BF16 = mybir.dt.bfloat16
I32 = mybir.dt.int32
ALU = mybir.AluOpType
AF = mybir.ActivationFunctionType

NCORES = 8
D = 1024
NP = 2048
NSQ = 16
NS = 64
NT = NP + NS
NLITE = 6
ST = 1024
NEG = -30000.0
EPS = 1e-6
TWO_PI = 2.0 * math.pi
C1 = 6.28125
C2 = TWO_PI - C1


class _Op:
    __slots__ = ("eng", "fn", "deps", "dma", "key", "sig", "val", "idx", "sem", "where", "rw")

    def __init__(self, eng, fn, dma, key):
        self.eng = eng
        self.fn = fn
        self.dma = dma
        self.key = key
        self.deps = set()
        self.sig = False
        self.val = 0
        self.sem = None


class Prog:
    ENGS = ("pe", "act", "dve", "pool", "sp")

    def __init__(self, nc):
        self.nc = nc
        self.ops = []
        self.last_w = {}
        self.reads = {}
        self.pending = {}
        self.applied = set()

    def _key_deps(self, k, deps):
        rn = k[0] if isinstance(k, tuple) else k
        pend = self.pending.get(rn)
        if pend and k not in self.applied:
            deps |= pend
            self.applied.add(k)

    def op(self, eng, fn, reads=(), writes=(), dma=False, key=None):
        o = _Op(eng, fn, dma, key)
        o.idx = len(self.ops)
        deps = o.deps
        ispsum = lambda k: isinstance(k, tuple) and k[0] == "ps"
        pk = [k[:2] for k in list(reads) + list(writes) if ispsum(k)]
        reads = [k for k in reads if not ispsum(k)]
        writes = [k for k in writes if not ispsum(k)]
        for k in dict.fromkeys(pk):
            j = self.last_w.get(k)
            if j is not None and (self.ops[j].eng != eng or self.ops[j].dma or dma):
                deps.add(j)
            self.last_w[k] = o.idx
        for r in reads:
            self._key_deps(r, deps)
            j = self.last_w.get(r)
            if j is not None:
                deps.add(j)
        for w in writes:
            self._key_deps(w, deps)
            j = self.last_w.get(w)
            if j is not None:
                deps.add(j)
            for j in self.reads.get(w, ()):
                deps.add(j)
        for r in reads:
            self.reads.setdefault(r, []).append(o.idx)
        for w in writes:
            self.last_w[w] = o.idx
            self.reads[w] = []
        if dma and key is None:
            o.key = (writes[0] if writes else reads[0])
        o.where = None
        o.rw = (list(reads), list(writes))
        self.ops.append(o)
        return o

    def users_of_region(self, rn):
        s = set()
        for k, j in self.last_w.items():
            if (k[0] if isinstance(k, tuple) else k) == rn:
                s.add(j)
        for k, js in self.reads.items():
            if (k[0] if isinstance(k, tuple) else k) == rn:
                s.update(js)
        return s

    def pe(self, fn, reads=(), writes=()):
        return self.op("pe", fn, reads, writes)

    def act(self, fn, reads=(), writes=()):
        return self.op("act", fn, reads, writes)

    def dve(self, fn, reads=(), writes=()):
        return self.op("dve", fn, reads, writes)

    def pool(self, fn, reads=(), writes=()):
        return self.op("pool", fn, reads, writes)

    def dma(self, out, in_, reads=(), writes=(), eng="sp", key=None, **kw):
        return self.op(eng, lambda e: e.dma_start(out=out, in_=in_, **kw), reads, writes, dma=True, key=key)

    def emit(self):
        nc = self.nc
        ops = self.ops
        for o in ops:
            if o.eng == "pe" and not o.dma:
                o.deps = {j for j in o.deps if not (ops[j].eng == "pe" and not ops[j].dma)}
        for o in ops:
            for j in o.deps:
                ops[j].sig = True
        for o in ops:
            if o.dma:
                o.sig = True
        eng_cnt = {e: 0 for e in self.ENGS}
        dma_cnt = {}
        dma_keys = []
        for o in ops:
            if not o.sig:
                continue
            if o.dma:
                if o.key not in dma_cnt:
                    dma_cnt[o.key] = 0
                    dma_keys.append(o.key)
                dma_cnt[o.key] += 16
                o.val = dma_cnt[o.key]
            else:
                eng_cnt[o.eng] += 1
                o.val = eng_cnt[o.eng]
        sems = {}
        for e in ("pe", "act", "dve", "pool"):
            sems[("eng", e)] = nc.alloc_semaphore("s_" + e)
        for i, k in enumerate(dma_keys):
            sems[("dma", k)] = nc.alloc_semaphore("d%d" % i)
        self.n_sems = len(sems)
        for o in ops:
            if o.sig:
                o.sem = sems[("dma", o.key)] if o.dma else sems[("eng", o.eng)]
        dma_hist = {}
        for o in ops:
            if o.dma:
                dma_hist.setdefault(o.key, []).append((o.idx, o.val))
        by_eng = {e: [o for o in ops if o.eng == e] for e in self.ENGS}

        def emit_engine(ename, eng):
            waited = {}
            for o in by_eng[ename]:
                need = {}
                for j in o.deps:
                    p = ops[j]
                    if p.dma:
                        v = p.val
                        for (ii, vv) in dma_hist[p.key]:
                            if ii < o.idx and vv > v:
                                v = vv
                        sk = ("dma", p.key)
                    else:
                        v = p.val
                        sk = ("eng", p.eng)
                    if need.get(sk, 0) < v:
                        need[sk] = v
                for sk, v in need.items():
                    if waited.get(sk, 0) >= v:
                        continue
                    eng.wait_ge(sems[sk], v)
                    waited[sk] = v
                ins = o.fn(eng)
                if o.sig:
                    ins.then_inc(o.sem, 16 if o.dma else 1)
            if ename == "sp":
                for k, v in dma_cnt.items():
                    eng.wait_ge(sems[("dma", k)], v)

        with nc.Block() as block:
            @block.tensor
            def _(e):
                emit_engine("pe", e)

            @block.scalar
            def _(e):
                emit_engine("act", e)

            @block.vector
            def _(e):
                emit_engine("dve", e)

            @block.gpsimd
            def _(e):
                emit_engine("pool", e)

            @block.sync
            def _(e):
                emit_engine("sp", e)


class Arena:
    def __init__(self, nc, P, words):
        self.P = P
        self.words = words
        self.t = nc.alloc_sbuf_tensor("arena", [128, words], F32)
        self.A = self.t.ap()
        self.live = {}
        self.dead = []
        self.peak = 0

    def alloc(self, name, n):
        n = (n + 7) // 8 * 8
        spans = sorted(self.live.values())
        pos = 0
        off = None
        for (o, m) in spans:
            if o - pos >= n:
                off = pos
                break
            pos = max(pos, o + m)
        if off is None:
            if self.words - pos >= n:
                off = pos
            else:
                raise RuntimeError("arena OOM for %s (%d words); live=%s" % (name, n, sorted((v, k) for k, v in self.live.items())))
        assert name not in self.live and name not in self.P.pending
        self.live[name] = (off, n)
        self.peak = max(self.peak, off + n)
        pend = set()
        nd = []
        for (o, m, users) in self.dead:
            if o < off + n and off < o + m:
                pend |= users
            nd.append((o, m, users))
        self.P.pending[name] = pend
        return Region(self, name, off, n)

    def free(self, reg):
        off, n = self.live.pop(reg.name)
        self.dead.append((off, n, self.P.users_of_region(reg.name) | self.P.pending.get(reg.name, set())))


class Region:
    def __init__(self, ar, name, off, n):
        self.ar = ar
        self.name = name
        self.off = off
        self.n = n

    def k(self, *sub):
        return (self.name,) + sub if sub else self.name

    def f32(self, lo=0, n=None):
        n = self.n - lo if n is None else n
        return self.ar.A[:, self.off + lo:self.off + lo + n]

    def bf(self, lo=0, n=None):
        n = self.n - lo if n is None else n
        return self.ar.A[:, self.off + lo:self.off + lo + n].bitcast(BF16)

    def i32(self, lo=0, n=None):
        n = self.n - lo if n is None else n
        return self.ar.A[:, self.off + lo:self.off + lo + n].bitcast(I32)


_DBG_P = [None]


def dram_ap(t_ap, offset, pattern):
    return bass.AP(t_ap.tensor, offset, [list(p) for p in pattern])


def build_program(debug=False):
    nc = bass.Bass("TRN2", target_bir_lowering=False)
    P = Prog(nc)
    _DBG_P[0] = P

    def din(name, shape):
        return nc.dram_tensor(name, list(shape), F32, kind="ExternalInput").ap()

    def dout(name, shape):
        return nc.dram_tensor(name, list(shape), F32, kind="ExternalOutput").ap()

    xo = din("xo", [NT, D])
    xp = din("xp", [NLITE * ST, D])
    ck_d = din("cache_k", [NSQ, 128, 128])
    cv_d = din("cache_v", [NSQ, 128, 128])
    sre_d = din("st_re", [NSQ, 2048])
    sim_d = din("st_im", [NSQ, 2048])
    relb_d = din("rel_bias", [32, 8])
    gmix_d = din("norm_mix", [1, D])
    win_d = din("w_in", [D, 1280])
    sink_d = din("sinks", [1, 8])
    are_d = din("a_re", [32, 64])
    aim_d = din("a_im", [32, 64])
    ls_d = din("log_step", [1, 32])
    bre_d = din("b_re", [32, 64, 16])
    bim_d = din("b_im", [32, 64, 16])
    cre_d = din("c_re", [32, 16, 64])
    cim_d = din("c_im", [32, 16, 64])
    dd_d = din("ssm_d", [32, 16])
    wglu_d = din("w_glu", [512, 512])
    bglu_d = din("b_glu", [1, 512])
    gatt_d = din("norm_attn", [1, 512])
    gssm_d = din("norm_ssm", [1, 512])
    wout_d = din("w_out", [D, D])
    gmlp_d = din("norm_mlp", [1, D])
    wup_d = din("w_up", [D, 4096])
    wdn_d = din("w_down", [4096, D])
    gfin_d = din("norm_final", [1, D])
    cst_d = din("consts", [128, 160])
    oh_d = din("onehot", [33, 384])
    hm_d = din("halo_mask", [128, 1])
    mB_d = din("maskB", [64, 64])
    jm_d = din("jmat", [128, 192])

    y_o = dout("y", [NT, D])
    nkp_o = dout("nk_p", [128, 128])
    nvp_o = dout("nv_p", [128, 128])
    spre_o = dout("sp_re", [32, 64])
    spim_o = dout("sp_im", [32, 64])
    nks_o = dout("nk_s", [NSQ, 128, 128])
    nvs_o = dout("nv_s", [NSQ, 128, 128])
    ssre_o = dout("ss_re", [NSQ, 2048])
    ssim_o = dout("ss_im", [NSQ, 2048])
    fext_d = nc.dram_tensor("fext", [8, 384], F32, kind="Internal").ap()
    dbg = {}

    AR = Arena(nc, P, 53000)
    psb = [nc.alloc_psum_tensor("ps%d" % i, [128, 512], F32).ap() for i in range(8)]

    def PS(i):
        return psb[i], ("ps", i)

    pers = AR.alloc("pers", 128 + 64 + 160 + 2048 + 64 + 520 + 64)
    ident_f = pers.f32(0, 128)
    ident_b = pers.bf(128, 64)
    cst = pers.f32(192, 160)
    biasT = pers.f32(352, 2048).rearrange("p (s h q) -> p s h q", s=2, h=8)
    misc = pers.f32(2400, 64)
    biasA = pers.f32(2464, 32).rearrange("p (h i) -> p h i", h=8)
    biasB = pers.f32(2496, 512).rearrange("p (h q) -> p h q", h=8)
    gains = AR.alloc("gains", 1024 + 512 + 24)
    g_mix = gains.f32(0, 1024)
    g_att = gains.f32(1024, 512)
    sinkexp = gains.f32(1536, 8)
    gsb = gains.f32(1544, 8)

    P.pool(lambda e: e.memset(ident_f, 0.0), writes=[pers.k("idf")])
    P.pool(lambda e: e.affine_select(out=ident_f, in_=ident_f, pattern=[[-1, 128]], compare_op=ALU.not_equal,
                                     fill=1.0, base=0, channel_multiplier=1), reads=[pers.k("idf")], writes=[pers.k("idf")])
    P.dve(lambda e: e.tensor_copy(out=ident_b, in_=ident_f), reads=[pers.k("idf")], writes=[pers.k("idb")])
    P.dma(cst, cst_d, writes=[pers.k("cst")])
    P.pool(lambda e: e.memset(misc[:, 0:1], -0.5), writes=[pers.k("mh")])
    P.dma(misc[:, 1:2], hm_d, writes=[pers.k("hm")])

    def bc_row(row_ap, n):
        return dram_ap(row_ap, 0, [[0, 128], [1, n]])

    P.dma(g_mix, bc_row(gmix_d, 1024), writes=[gains.k("mix")])
    P.dma(g_att, bc_row(gatt_d, 512), writes=[gains.k("att")])
    P.dma(sinkexp, bc_row(sink_d, 8), writes=[gains.k("sink")])
    P.act(lambda e: e.activation(out=sinkexp, in_=sinkexp, func=AF.Exp), reads=[gains.k("sink")], writes=[gains.k("sink")])
    P.dma(gsb[:, 0:4], dram_ap(gssm_d, 0, [[1, 128], [128, 4]]), writes=[gains.k("gs")], allow_slow_non_contiguous=True)
    P.dma(gsb[:, 4:8], dram_ap(bglu_d, 0, [[1, 128], [128, 4]]), writes=[gains.k("bg")], allow_slow_non_contiguous=True)
    mhalf = misc[:, 0:1]
    gcol = misc[:, 16:24]
    P.dma(gcol, dram_ap(gmix_d, 0, [[1, 128], [128, 8]]), writes=[pers.k("gcol")], allow_slow_non_contiguous=True)

    def reduce_angle(x_ap, r_ap, n_i32, n_f32, kx, kr, ktmp, add=0.0):
        ki_, kf_ = ktmp
        if add != 0.0:
            P.dve(lambda e: e.tensor_scalar(out=r_ap, in0=x_ap, scalar1=float(add), scalar2=None, op0=ALU.add),
                  reads=[kx], writes=[kr])
            src, ksrc = r_ap, kr
        else:
            src, ksrc = x_ap, kx
        P.dve(lambda e: e.tensor_scalar(out=n_i32, in0=src, scalar1=1.0 / TWO_PI, scalar2=None, op0=ALU.mult),
              reads=[ksrc], writes=[ki_])
        P.dve(lambda e: e.tensor_copy(out=n_f32, in_=n_i32), reads=[ki_], writes=[kf_])
        P.dve(lambda e: e.scalar_tensor_tensor(out=r_ap, in0=n_f32, scalar=-C1, in1=src, op0=ALU.mult, op1=ALU.add),
              reads=[kf_, ksrc], writes=[kr])
        P.dve(lambda e: e.scalar_tensor_tensor(out=r_ap, in0=n_f32, scalar=-C2, in1=r_ap, op0=ALU.mult, op1=ALU.add),
              reads=[kf_, kr], writes=[kr])

    def tt(eng, out, a, b, op, reads, writes):
        P.op(eng, lambda e: e.tensor_tensor(out=out, in0=a, in1=b, op=op), reads, writes)


    sm = AR.alloc("sm", 2816)
    _o = [0]

    def smv(n):
        v = sm.f32(_o[0], n)
        _o[0] += n
        return v
    ARE = smv(16); AIM = smv(16); LS = smv(16); DEL = smv(16); ARs = smv(16); AIs = smv(16)
    LRE = smv(144).rearrange("p (g j) -> p g j", g=16); LIM = smv(144).rearrange("p (g j) -> p g j", g=16)
    CRE = smv(16); CIM = smv(16); T8R = smv(16); R8 = smv(16); R128 = smv(16)
    BRE = smv(256).rearrange("p (g c) -> p g c", g=16); BIM = smv(256).rearrange("p (g c) -> p g c", g=16)
    CTRE = smv(256).rearrange("p (g c) -> p g c", g=16); CTIM = smv(256).rearrange("p (g c) -> p g c", g=16)
    HRE = smv(16); HIM = smv(16)
    t1 = smv(256); t2 = smv(256); t3 = smv(256); t4 = smv(256)
    tI = sm.i32(_o[0], 256); _o[0] += 256
    ksm = lambda s: sm.k(s)

    for gh in range(2):
        P.dma(ARE[64 * gh:64 * gh + 64, :], dram_ap(are_d, 1024 * gh, [[1, 64], [64, 16]]), writes=[ksm("are")], allow_slow_non_contiguous=True)
        P.dma(AIM[64 * gh:64 * gh + 64, :], dram_ap(aim_d, 1024 * gh, [[1, 64], [64, 16]]), writes=[ksm("aim")], allow_slow_non_contiguous=True)
        P.dma(LS[64 * gh:64 * gh + 64, :], dram_ap(ls_d, 16 * gh, [[0, 64], [1, 16]]), writes=[ksm("ls")])
        P.dma(BRE[64 * gh:64 * gh + 64], dram_ap(bre_d, 16 * 1024 * gh, [[16, 64], [1024, 16], [1, 16]]), writes=[ksm("bre")])
        P.dma(BIM[64 * gh:64 * gh + 64], dram_ap(bim_d, 16 * 1024 * gh, [[16, 64], [1024, 16], [1, 16]]), writes=[ksm("bim")])
    wqkv = AR.alloc("wqkv", 8 * 1280 // 2)
    W_QKV = wqkv.bf().rearrange("p (k c) -> p k c", k=8)
    wu = AR.alloc("wu", 8 * 512 // 2)
    W_U = wu.bf().rearrange("p (k c) -> p k c", k=8)
    wst = AR.alloc("wstage", 8 * 1280)
    P.pool(lambda e: e.memset(wqkv.f32(), 0.0), writes=[wqkv.k()])
    _wc = {"n": 0}

    def wcast(dst, src, kt, reads, writes):
        _wc["n"] += 1
        if True:
            P.act(lambda e: e.activation(out=dst, in_=src, func=AF.Copy, scale=gcol[:, kt:kt + 1]), reads=reads + [pers.k("gcol")], writes=writes)
        else:
            P.dve(lambda e: e.tensor_scalar(out=dst, in0=src, scalar1=gcol[:, kt:kt + 1], scalar2=None, op0=ALU.mult), reads=reads + [pers.k("gcol")], writes=writes)

    for kt in range(8):
        stg = wst.f32(1280 * kt, 1280)
        sk_ = wst.k(kt)
        P.dma(stg, win_d[kt * 128:(kt + 1) * 128, :], writes=[sk_])
        for (dst, src) in ((W_QKV[:, kt, 0:768], stg[:, 0:768]),
                           (W_QKV[:, kt, 768:832], stg[:, 512:576]), (W_QKV[:, kt, 960:1024], stg[:, 512:576]),
                           (W_QKV[:, kt, 1024:1088], stg[:, 576:640]), (W_QKV[:, kt, 1216:1280], stg[:, 576:640])):
            wcast(dst, src, kt, [sk_], [wqkv.k()])
        wcast(W_U[:, kt, :], stg[:, 768:1280], kt, [sk_], [wu.k()])
    AR.free(wst)
    P.act(lambda e: e.activation(out=DEL, in_=LS, func=AF.Exp), reads=[ksm("ls")], writes=[ksm("del")])
    tt("dve", ARs, ARE, DEL, ALU.mult, [ksm("are"), ksm("del")], [ksm("ars")])
    tt("dve", AIs, AIM, DEL, ALU.mult, [ksm("aim"), ksm("del")], [ksm("ais")])
    JV = cst[:, 0:9]
    b3 = lambda a: a.unsqueeze(2).to_broadcast([128, 16, 9])
    jb = JV.unsqueeze(1).to_broadcast([128, 16, 9])
    v144 = lambda t: t[:, 0:144].rearrange("p (g j) -> p g j", g=16)
    tt("dve", v144(t1), b3(ARs), jb, ALU.mult, [ksm("ars"), pers.k("cst")], [ksm("t1")])
    P.act(lambda e: e.activation(out=v144(t1), in_=v144(t1), func=AF.Exp), reads=[ksm("t1")], writes=[ksm("t1")])
    tt("dve", v144(t2), b3(AIs), jb, ALU.mult, [ksm("ais"), pers.k("cst")], [ksm("t2")])
    reduce_angle(t2[:, 0:144], t3[:, 0:144], tI[:, 0:144], t4[:, 0:144], ksm("t2"), ksm("t3"), (ksm("tI"), ksm("t4")))
    P.act(lambda e: e.activation(out=t3[:, 0:144], in_=t3[:, 0:144], func=AF.Sin), reads=[ksm("t3")], writes=[ksm("t3")])
    tt("dve", LIM, v144(t1), v144(t3), ALU.mult, [ksm("t1"), ksm("t3")], [ksm("lim")])
    reduce_angle(t2[:, 0:144], t3[:, 0:144], tI[:, 0:144], t4[:, 0:144], ksm("t2"), ksm("t3"), (ksm("tI"), ksm("t4")), add=math.pi / 2)
    P.act(lambda e: e.activation(out=t3[:, 0:144], in_=t3[:, 0:144], func=AF.Sin), reads=[ksm("t3")], writes=[ksm("t3")])
    tt("dve", LRE, v144(t1), v144(t3), ALU.mult, [ksm("t1"), ksm("t3")], [ksm("lre")])
    L1r = LRE[:, :, 1]; L1i = LIM[:, :, 1]
    a16 = lambda t, i=0: t[:, 16 * i:16 * i + 16]
    P.dve(lambda e: e.tensor_scalar(out=a16(t1, 0), in0=L1r, scalar1=-1.0, scalar2=None, op0=ALU.add), reads=[ksm("lre"), ksm("t1")], writes=[ksm("t1")])
    tt("dve", a16(t1, 1), ARE, ARE, ALU.mult, [ksm("are")], [ksm("t1")])
    tt("dve", a16(t1, 2), AIM, AIM, ALU.mult, [ksm("aim")], [ksm("t1")])
    tt("dve", a16(t1, 1), a16(t1, 1), a16(t1, 2), ALU.add, [ksm("t1"), ksm("t1")], [ksm("t1")])
    P.dve(lambda e: e.reciprocal(out=a16(t1, 1), in_=a16(t1, 1)), reads=[ksm("t1")], writes=[ksm("t1")])
    tt("dve", a16(t1, 3), a16(t1, 0), ARE, ALU.mult, [ksm("t1"), ksm("are")], [ksm("t1")])
    tt("dve", a16(t1, 4), L1i, AIM, ALU.mult, [ksm("lim"), ksm("aim")], [ksm("t1")])
    tt("dve", a16(t1, 3), a16(t1, 3), a16(t1, 4), ALU.add, [ksm("t1"), ksm("t1")], [ksm("t1")])
    tt("dve", CRE, a16(t1, 3), a16(t1, 1), ALU.mult, [ksm("t1"), ksm("t1")], [ksm("cre")])
    tt("dve", a16(t1, 5), L1i, ARE, ALU.mult, [ksm("lim"), ksm("are")], [ksm("t1")])
    tt("dve", a16(t1, 6), a16(t1, 0), AIM, ALU.mult, [ksm("t1"), ksm("aim")], [ksm("t1")])
    tt("dve", a16(t1, 5), a16(t1, 5), a16(t1, 6), ALU.subtract, [ksm("t1"), ksm("t1")], [ksm("t1")])
    tt("dve", CIM, a16(t1, 5), a16(t1, 1), ALU.mult, [ksm("t1"), ksm("t1")], [ksm("cim")])
    cb = lambda a: a.unsqueeze(2).to_broadcast([128, 16, 16])
    v256 = lambda t: t.rearrange("p (g c) -> p g c", g=16)
    tt("dve", v256(t2), cb(CRE), BRE, ALU.mult, [ksm("cre"), ksm("bre")], [ksm("t2")])
    tt("dve", v256(t3), cb(CIM), BIM, ALU.mult, [ksm("cim"), ksm("bim")], [ksm("t3")])
    tt("dve", v256(t4), cb(CRE), BIM, ALU.mult, [ksm("cre"), ksm("bim")], [ksm("t4")])
    tt("dve", v256(t1), cb(CIM), BRE, ALU.mult, [ksm("cim"), ksm("bre"), ksm("t1"), ksm("t1"), ksm("t1"), ksm("t1")], [ksm("t1")])
    tt("dve", BRE, v256(t2), v256(t3), ALU.subtract, [ksm("t2"), ksm("t3")], [ksm("bre")])
    tt("dve", BIM, v256(t4), v256(t1), ALU.add, [ksm("t4"), ksm("t1")], [ksm("bim")])
    P.dve(lambda e: e.tensor_scalar(out=a16(t2, 0), in0=AIs, scalar1=8.0, scalar2=None, op0=ALU.mult), reads=[ksm("ais"), ksm("t2")], writes=[ksm("t2")])
    reduce_angle(a16(t2, 0), T8R, tI[:, 0:16], a16(t4, 0), ksm("t2"), ksm("t8r"), (ksm("tI"), ksm("t4")))
    P.act(lambda e: e.activation(out=R8, in_=ARs, func=AF.Exp, scale=8.0), reads=[ksm("ars")], writes=[ksm("r8")])
    P.act(lambda e: e.activation(out=R128, in_=ARs, func=AF.Exp, scale=1024.0), reads=[ksm("ars")], writes=[ksm("r128")])
    P.pool(lambda e: e.memset(HRE, 0.0), writes=[ksm("hre")])
    P.pool(lambda e: e.memset(HIM, 0.0), writes=[ksm("him")])

    tabs = AR.alloc("tabs", 2 * 16 * 129)
    CK = tabs.f32(0, 2064).rearrange("p (g k) -> p g k", g=16)
    SK = tabs.f32(2064, 2064).rearrange("p (g k) -> p g k", g=16)
    tw = AR.alloc("tabwork", 4 * 2064)
    xk = tw.f32(0, 2064); rk = tw.f32(2064, 2064); nki = tw.i32(4128, 2064); nkf = tw.f32(6192, 2064)
    KK = cst[:, 16:145]
    tt("dve", xk.rearrange("p (g k) -> p g k", g=16), T8R.unsqueeze(2).to_broadcast([128, 16, 129]),
       KK.unsqueeze(1).to_broadcast([128, 16, 129]), ALU.mult, [ksm("t8r"), pers.k("cst")], [tw.k("x")])
    reduce_angle(xk, rk, nki, nkf, tw.k("x"), tw.k("r"), (tw.k("ni"), tw.k("nf")))
    P.act(lambda e: e.activation(out=SK.rearrange("p g k -> p (g k)"), in_=rk, func=AF.Sin), reads=[tw.k("r")], writes=[tabs.k("sk")])
    reduce_angle(xk, rk, nki, nkf, tw.k("x"), tw.k("r"), (tw.k("ni"), tw.k("nf")), add=math.pi / 2)
    P.act(lambda e: e.activation(out=CK.rearrange("p g k -> p (g k)"), in_=rk, func=AF.Sin), reads=[tw.k("r")], writes=[tabs.k("ck")])
    AR.free(tw)

    wp = AR.alloc("wp", 32 * 2 * 128 // 2)
    W_P = wp.bf().rearrange("p (g r m) -> p g r m", g=32, r=2)
    P.pool(lambda e: e.memset(wp.f32(), 0.0), writes=[wp.k()])
    s2 = AR.alloc("setup2", 2048 * 6 + 32 + 4096)
    adr = s2.f32(0, 2048); adi = s2.f32(2048, 2048); ltr = s2.f32(4096, 2048); lti = s2.f32(6144, 2048)
    w1 = s2.f32(8192, 2048); w2 = s2.f32(10240, 2048); lsb = s2.f32(12288, 32)
    wI = s2.i32(8192, 2048)
    brep = s2.bf(12320, 2048).rearrange("p (g r m) -> p g r m", g=16, r=2)
    P.dma(adr, dram_ap(are_d, 0, [[0, 128], [1, 2048]]), writes=[s2.k("adr")])
    P.dma(adi, dram_ap(aim_d, 0, [[0, 128], [1, 2048]]), writes=[s2.k("adi")])
    P.dma(lsb, dram_ap(ls_d, 0, [[0, 128], [1, 32]]), writes=[s2.k("lsb")])
    P.act(lambda e: e.activation(out=lsb, in_=lsb, func=AF.Exp), reads=[s2.k("lsb")], writes=[s2.k("lsb")])
    g64 = lambda t: t.rearrange("p (g q) -> p g q", g=32)
    lb = lsb.unsqueeze(2).to_broadcast([128, 32, 64])
    tt("dve", g64(adr), g64(adr), lb, ALU.mult, [s2.k("adr"), s2.k("lsb")], [s2.k("adr")])
    tt("dve", g64(adi), g64(adi), lb, ALU.mult, [s2.k("adi"), s2.k("lsb")], [s2.k("adi")])
    JC = cst[:, 146:147]
    P.act(lambda e: e.activation(out=w1, in_=adr, func=AF.Exp, scale=JC), reads=[s2.k("adr"), pers.k("cst")], writes=[s2.k("w1")])
    P.dve(lambda e: e.tensor_scalar(out=adi, in0=adi, scalar1=JC, scalar2=None, op0=ALU.mult), reads=[s2.k("adi"), pers.k("cst")], writes=[s2.k("adi")])
    reduce_angle(adi, ltr, s2.i32(10240, 2048), lti, s2.k("adi"), s2.k("ltr"), (s2.k("w2"), s2.k("lti")))
    P.act(lambda e: e.activation(out=lti, in_=ltr, func=AF.Sin), reads=[s2.k("ltr")], writes=[s2.k("lti")])
    tt("dve", lti, lti, w1, ALU.mult, [s2.k("lti"), s2.k("w1")], [s2.k("lti")])
    reduce_angle(adi, ltr, s2.i32(10240, 2048), adr, s2.k("adi"), s2.k("ltr"), (s2.k("w2"), s2.k("adr")), add=math.pi / 2)
    P.act(lambda e: e.activation(out=ltr, in_=ltr, func=AF.Sin), reads=[s2.k("ltr")], writes=[s2.k("ltr")])
    tt("dve", ltr, ltr, w1, ALU.mult, [s2.k("ltr"), s2.k("w1")], [s2.k("ltr")])
    sb = lambda a: a.unsqueeze(2).to_broadcast([128, 16, 8, 16])
    P.dve(lambda e: e.tensor_copy(out=brep[:, :, 0, :].rearrange("p g (s c) -> p g s c", s=8), in_=sb(BRE)), reads=[ksm("bre")], writes=[s2.k("brep0")])
    P.dve(lambda e: e.tensor_copy(out=brep[:, :, 1, :].rearrange("p g (s c) -> p g s c", s=8), in_=sb(BIM)), reads=[ksm("bim")], writes=[s2.k("brep1")])
    LTR = g64(ltr); LTI = g64(lti)
    for gh in range(2):
        for gq in range(16):
            bk, bkk = PS(gq // 4)
            o_ = bk[:, (gq % 4) * 128:(gq % 4) * 128 + 128]

            def f(e, o_=o_, gq=gq, gh=gh):
                e.matmul(o_[:, 0:64], lhsT=brep[:, gq, 0, :], rhs=ident_b[:, 64 * gh:64 * gh + 64], start=True, stop=True)
                return e.matmul(o_[:, 64:128], lhsT=brep[:, gq, 1, :], rhs=ident_b[:, 64 * gh:64 * gh + 64], start=True, stop=True)
            P.pe(f, reads=[s2.k("brep0"), s2.k("brep1"), pers.k("idb")], writes=[bkk + (gq % 4,)])
        for q4 in range(4):
            bk, bkk = PS(q4)
            btv = bk.rearrange("p (g r m) -> p g r m", g=4, r=2)
            gs = slice(16 * gh + 4 * q4, 16 * gh + 4 * q4 + 4)
            rd = [bkk + (i,) for i in range(4)]
            wv1 = w1[:, 0:256].rearrange("p (g m) -> p g m", g=4); wv2 = w2[:, 0:256].rearrange("p (g m) -> p g m", g=4)
            kw1 = s2.k("w1"); kw2 = s2.k("w2")
            tt("dve", wv1, LTR[:, gs, :], btv[:, :, 0, :], ALU.mult, [s2.k("ltr")] + rd, [kw1])
            tt("dve", wv2, LTI[:, gs, :], btv[:, :, 1, :], ALU.mult, [s2.k("lti")] + rd, [kw2])
            tt("dve", W_P[:, gs, 0, 64 * gh:64 * gh + 64], wv1, wv2, ALU.subtract, [kw1, kw2], [wp.k()])
            tt("dve", wv1, LTR[:, gs, :], btv[:, :, 1, :], ALU.mult, [s2.k("ltr")] + rd, [kw1])
            tt("dve", wv2, LTI[:, gs, :], btv[:, :, 0, :], ALU.mult, [s2.k("lti")] + rd, [kw2])
            tt("dve", W_P[:, gs, 1, 64 * gh:64 * gh + 64], wv1, wv2, ALU.add, [kw1, kw2] + rd, [wp.k()] + rd)
    AR.free(s2)
    if debug:
        dbg["W_P"] = (wp, [128, 32 * 2 * 128], BF16)
        dbg["tabs"] = (tabs, [128, 2 * 2064], F32)
        dbg["sm"] = (sm, [128, 2816], F32)


    _rr = {"n": 0}

    def evac(out, in_, reads, writes, eng=None):
        if eng is None:
            eng = "act" if (_rr["n"] % 3 != 2) else "dve"
            _rr["n"] += 1
        if eng == "act":
            P.act(lambda e: e.copy(out=out, in_=in_), reads=reads, writes=writes)
        elif eng == "pool":
            P.pool(lambda e: e.tensor_copy(out=out, in_=in_), reads=reads, writes=writes)
        else:
            P.dve(lambda e: e.tensor_copy(out=out, in_=in_), reads=reads, writes=writes)

    NTK = 128 + NT
    bh_r = AR.alloc("biasH", 1024)
    biasH = bh_r.f32().rearrange("p (h q) -> p h q", h=8)
    ab_r = AR.alloc("attbias_tmp", 8 + 384 + 384 + 32 + 64)
    rb = ab_r.f32(0, 8); ohs = ab_r.f32(8, 384); fsb = ab_r.f32(392, 384); TB = ab_r.f32(776, 32).rearrange("p (h i) -> p h i", h=8)
    mBs = ab_r.f32(808, 64)
    P.pool(lambda e: e.memset(rb[0:64, :], 1.0), writes=[ab_r.k("rb")])
    P.dma(rb[0:32, :], relb_d, writes=[ab_r.k("rb")])
    P.pool(lambda e: e.memset(ohs[0:64, :], 0.0), writes=[ab_r.k("oh")])
    P.dma(ohs[0:33, :], oh_d, writes=[ab_r.k("oh")])
    P.dma(mBs[0:64, :], mB_d, writes=[ab_r.k("mb")])
    ab2 = AR.alloc("attbias_bf", 16 + 8 + 256)
    rbh = ab2.bf(0, 4); rbl = ab2.bf(4, 4); rbr = ab2.f32(8, 8); rbf = ab2.f32(16, 8); ohb = ab2.bf(24, 192)
    P.dve(lambda e: e.tensor_copy(out=rbh[0:64], in_=rb[0:64]), reads=[ab_r.k("rb")], writes=[ab2.k("h")])
    P.dve(lambda e: e.tensor_copy(out=rbf[0:64], in_=rbh[0:64]), reads=[ab2.k("h")], writes=[ab2.k("hf")])
    tt("dve", rbr[0:64], rb[0:64], rbf[0:64], ALU.subtract, [ab_r.k("rb"), ab2.k("hf")], [ab2.k("r")])
    P.dve(lambda e: e.tensor_copy(out=rbl[0:64], in_=rbr[0:64]), reads=[ab2.k("r")], writes=[ab2.k("l")])
    P.dve(lambda e: e.tensor_copy(out=ohb[0:64], in_=ohs[0:64]), reads=[ab_r.k("oh")], writes=[ab2.k("o")])
    bk, bkk = PS(7)

    def ffx(e, bk=bk):
        e.matmul(bk[0:8, 0:384], lhsT=rbh[0:64, :], rhs=ohb[0:64, :], start=True, stop=False)
        return e.matmul(bk[0:8, 0:384], lhsT=rbl[0:64, :], rhs=ohb[0:64, :], start=False, stop=True)
    P.pe(ffx, reads=[ab2.k("h"), ab2.k("l"), ab2.k("o")], writes=[bkk])
    evac(fsb[0:8, :], bk[0:8, 0:384], [bkk], [ab_r.k("f")], eng="act")
    P.dma(fext_d, fsb[0:8, :], reads=[ab_r.k("f")], writes=["fext"])
    hk_r = AR.alloc("hankel", 2048 + 192 + 32 + 32)
    HK = hk_r.f32(0, 2048).rearrange("p (s h q) -> p s h q", s=2, h=8)
    JM = hk_r.f32(2048, 192)
    HA = hk_r.f32(2240, 32).rearrange("p (h i) -> p h i", h=8)
    HB = hk_r.f32(2272, 32).rearrange("p (h i) -> p h i", h=8)
    P.dma(JM, jm_d, writes=[hk_r.k("jm")])
    def bias_reads():
        for slot in range(2):
            for h in range(8):
                P.dma(HK[:, slot, h, :], dram_ap(fext_d, h * 384 + (128 if slot == 0 else 0), [[1, 128], [1, 128]]),
                      reads=["fext"], writes=[hk_r.k("hk")])
        P.dma(HA, dram_ap(fext_d, 128, [[1, 128], [384, 8], [1, 4]]), reads=["fext"], writes=[hk_r.k("ha")])
        P.dma(HB[0:4], dram_ap(fext_d, 124, [[1, 4], [384, 8], [1, 4]]), reads=["fext"], writes=[hk_r.k("hb")])

    qT_r = AR.alloc("qT", 4 * NT // 2)
    qT = qT_r.bf().rearrange("p (t n) -> p t n", t=4)
    kT_r = AR.alloc("kT", 4 * NTK // 2)
    kT = kT_r.bf().rearrange("p (t n) -> p t n", t=4)
    va_r = AR.alloc("vaug", 17 * 2 * 72 // 2)
    v_aug = va_r.bf().rearrange("p (b g d) -> p b g d", b=17, g=2)
    vs_r = AR.alloc("vsaug", 2 * 72 // 2)
    vs_aug = vs_r.bf().rearrange("p (g d) -> p g d", g=2)
    uown_r = AR.alloc("Uown", 2 * 32 * 128 // 2)
    U_own = uown_r.bf().rearrange("p (s g k) -> p s g k", s=2, g=32)
    us_r = AR.alloc("Usamp", 2 * 32 * 16 // 2)
    xs_r = AR.alloc("xs", 2 * 1024)
    xn_rs = [AR.alloc("xn_a", 512), AR.alloc("xn_b", 512)]
    hn_r = AR.alloc("hnT", 8 * 1024 // 2)
    hnT = hn_r.bf().rearrange("p (k n) -> p k n", k=8)
    utm_r = AR.alloc("utm8", 32 * 12 * 16 // 2)
    u_tm8 = utm_r.bf(0, 2048).rearrange("p (g s c) -> p g s c", g=32, s=8)
    u_s12 = utm_r.bf().rearrange("p (g s c) -> p g s c", g=32, s=12)
    stat_r = AR.alloc("stats", 256)
    kvf_r = AR.alloc("kvf", 256)

    P.pool(lambda e: e.memset(va_r.bf(), 1.0), writes=[va_r.k()])
    P.pool(lambda e: e.memset(vs_r.bf(), 1.0), writes=[vs_r.k()])
    def bias_stage3():
        HKf = hk_r.f32(0, 2048)
        bTf = biasT.rearrange("p s h q -> p (s h q)")
        for c4 in range(4):
            bk, bkk = PS(4 + c4)
            P.pe(lambda e, bk=bk, c4=c4: e.matmul(bk, lhsT=JM[:, 0:128], rhs=HKf[:, 512 * c4:512 * c4 + 512], start=True, stop=True),
                 reads=[hk_r.k("jm"), hk_r.k("hk")], writes=[bkk])
            evac(bTf[:, 512 * c4:512 * c4 + 512], bk, [bkk], [pers.k("biasT")])
        bk, bkk = PS(7)

        def fja(e, bk=bk):
            e.matmul(bk[:, 0:32], lhsT=JM[:, 0:128], rhs=hk_r.f32(2240, 32), start=True, stop=True)
            return e.matmul(bk[0:64, 32:64], lhsT=JM[0:4, 128:192], rhs=hk_r.f32(2272, 32)[0:4], start=True, stop=True)
        P.pe(fja, reads=[hk_r.k("jm"), hk_r.k("ha"), hk_r.k("hb")], writes=[bkk])
        evac(biasA.rearrange("p h i -> p (h i)"), bk[:, 0:32], [bkk], [pers.k("biasA")], eng="act")
        evac(TB[0:64].rearrange("p h i -> p (h i)"), bk[0:64, 32:64], [bkk], [ab_r.k("tb")], eng="act")
        tt("dve", biasB[0:64].rearrange("p h (b i) -> p h b i", b=16), TB[0:64].unsqueeze(2).to_broadcast([64, 8, 16, 4]),
           mBs[0:64].rearrange("p (b i) -> p b i", b=16).unsqueeze(1).to_broadcast([64, 8, 16, 4]), ALU.add,
           [ab_r.k("tb"), ab_r.k("mb")], [pers.k("biasB")])
        P.dve(lambda e: e.tensor_scalar(out=biasH, in0=biasT[:, 0], scalar1=misc[:, 1:2], scalar2=None, op0=ALU.add),
              reads=[pers.k("biasT"), pers.k("hm")], writes=[bh_r.k()])
        if debug:
            _o1 = nc.dram_tensor("dbg_abr", [128, 872], F32, kind="ExternalOutput").ap()
            P.dma(_o1, ab_r.f32(0, 872), reads=[ab_r.k(x) for x in ("rb", "oh", "f", "tb", "mb")])
            _o2 = nc.dram_tensor("dbg_hkr", [128, 2304], F32, kind="ExternalOutput").ap()
            P.dma(_o2, hk_r.f32(0, 2304), reads=[hk_r.k(x) for x in ("jm", "hk", "ha", "hb")])
        AR.free(ab_r)
        AR.free(hk_r)
        AR.free(ab2)

    tile_ctr = {"n": 0}

    def norm_front(src_rows, nrows, gain, wname):
        i = tile_ctr["n"]
        tile_ctr["n"] += 1
        b = i % (xs_r.n // 1024)
        xs = xs_r.f32(1024 * b, 1024)
        bn = i % len(xn_rs)
        xn = xn_rs[bn].bf(0, 512)
        kx = xs_r.k(b)
        kn = xn_rs[bn].k()
        sc = stat_r.f32(4 * (i % 64), 4)
        ksc = stat_r.k(i % 64)
        pr = slice(0, nrows)
        P.dma(xs[pr], src_rows, writes=[kx])
        P.act(lambda e: e.activation(out=xn[pr], in_=xs[pr], func=AF.Square, accum_out=sc[pr, 0:1]), reads=[kx], writes=[kn, ksc])
        P.dve(lambda e: e.tensor_scalar(out=sc[pr, 1:2], in0=sc[pr, 0:1], scalar1=1.0 / 1024, scalar2=EPS, op0=ALU.mult, op1=ALU.add),
              reads=[ksc], writes=[ksc + ("a",)])
        P.pool(lambda e: e.tensor_tensor(out=sc[pr, 2:3], in0=sc[pr, 1:2], in1=mhalf[pr], op=ALU.pow), reads=[ksc + ("a",), pers.k("mh")], writes=[ksc + ("b",)])
        P.act(lambda e: e.activation(out=xn[pr], in_=xs[pr], func=AF.Copy, scale=sc[pr, 2:3]), reads=[kx, ksc + ("b",)], writes=[kn])
        return (xn, kn, pr, nrows)

    def norm_back(ctx, hn_dst, hn_key, eng=None):
        xn, kn, pr, nrows = ctx
        for half in range(2):
            bk, bkk = PS(half)

            def f(e, bk=bk, half=half):
                last = None
                for j in range(4):
                    kt = 4 * half + j
                    last = e.matmul(bk[:, j * 128:j * 128 + nrows], lhsT=xn[pr, kt * 128:(kt + 1) * 128], rhs=ident_b[pr, pr], start=True, stop=True)
                return last
            P.pe(f, reads=[kn, pers.k("idb")], writes=[bkk])
            evac(hn_dst[:, 4 * half:4 * half + 4, :], bk.rearrange("p (j n) -> p j n", j=4)[:, :, 0:nrows], [bkk], [hn_key], eng=eng)

    def norm_tile(src_rows, nrows, hn_dst, hn_key, gain, wname):
        norm_back(norm_front(src_rows, nrows, gain, wname), hn_dst, hn_key)

    def run_tiles(specs):
        ctxs = [None] * len(specs)
        ctxs[0] = norm_front(specs[0][0], 128, g_mix, "mix")
        for i in range(len(specs)):
            if i + 1 < len(specs):
                ctxs[i + 1] = norm_front(specs[i + 1][0], 128, g_mix, "mix")
            norm_back(ctxs[i], specs[i][1], specs[i][2])
            if specs[i][3] is not None:
                specs[i][3]()

    def proj_fm(hn_src, hn_key, ncols, wtile, dst, dst_key, bank):
        bk, bkk = PS(bank)

        def f(e):
            last = None
            for kt in range(8):
                last = e.matmul(bk[:, 0:ncols], lhsT=wtile(kt), rhs=hn_src[:, kt, :], start=(kt == 0), stop=(kt == 7))
            return last
        P.pe(f, reads=[hn_key, wqkv.k(), wu.k()], writes=[bkk])
        evac(dst, bk[:, 0:ncols], [bkk], [dst_key])

    def proj_tm(hn_src, hn_key, nrows, wsl, ncols, bank):
        bk, bkk = PS(bank)

        def f(e):
            last = None
            for kt in range(8):
                last = e.matmul(bk[0:nrows, 0:ncols], lhsT=hn_src[:, kt, :], rhs=wsl(kt), start=(kt == 0), stop=(kt == 7))
            return last
        P.pe(f, reads=[hn_key, wqkv.k(), wu.k()], writes=[bkk])
        return bk, bkk

    pbank = {"n": 0}

    def nb():
        pbank["n"] += 1
        return 2 + (pbank["n"] % 2)

    def kv_tile(hn_src, hn_key, nrows, blk, last=False, sample=False):
        bk, bkk = proj_tm(hn_src, hn_key, nrows, lambda kt: W_QKV[:, kt, 512:768], 256, 4)
        if sample:
            evac(vs_aug[0:nrows, :, 0:64], bk[0:nrows, 128:256].rearrange("p (g d) -> p g d", g=2), [bkk], [vs_r.k()], eng="dve")
        else:
            evac(v_aug[:, blk, :, 0:64], bk[:, 128:256].rearrange("p (g d) -> p g d", g=2), [bkk], [va_r.k()], eng="dve")
        if last or sample:
            kvf = kvf_r.f32()
            evac(kvf[0:nrows], bk[0:nrows, 0:256], [bkk], [kvf_r.k()], eng="act")
            if sample:
                for b_ in range(NSQ):
                    P.dma(nks_o[b_, 124:128, :], kvf[4 * b_:4 * b_ + 4, 0:128], reads=[kvf_r.k()])
                    P.dma(nvs_o[b_, 124:128, :], kvf[4 * b_:4 * b_ + 4, 128:256], reads=[kvf_r.k()])
            else:
                P.dma(nkp_o, kvf[:, 0:128], reads=[kvf_r.k()])
                P.dma(nvp_o, kvf[:, 128:256], reads=[kvf_r.k()])

    def k_cols(hn_src, hn_key, ncols, tok0):
        for t in range(4):
            proj_fm(hn_src, hn_key, ncols, lambda kt, t=t: W_QKV[:, kt, 768 + 128 * t:896 + 128 * t],
                    kT[:, t, tok0:tok0 + ncols], kT_r.k(), nb())

    def q_cols(hn_src, hn_key, ncols, tok0):
        for t in range(4):
            proj_fm(hn_src, hn_key, ncols, lambda kt, t=t: W_QKV[:, kt, 128 * t:128 * t + 128],
                    qT[:, t, tok0:tok0 + ncols], qT_r.k(), nb())

    def u_proj_st(hn_key, hnT_, utm8_, utm_k):
        for s_ in range(8):
            bk, bkk = PS(nb())

            def f(e, bk=bk, s_=s_, hnT_=hnT_):
                last = None
                for kt in range(8):
                    last = e.matmul(bk[:, 0:512], lhsT=hnT_[:, kt, s_::8], rhs=W_U[:, kt, :], start=(kt == 0), stop=(kt == 7))
                return last
            P.pe(f, reads=[hn_key, wu.k()], writes=[bkk])
            evac(utm8_[:, :, s_, :], bk.rearrange("p (g c) -> p g c", g=32), [bkk], [utm_k])

    def U_transposes(U_dst, U_key, utm8_, utm_k):
        for q4 in range(8):
            bk, bkk = PS(5 + q4 % 2)

            def f(e, bk=bk, q4=q4, utm8_=utm8_):
                last = None
                for j in range(4):
                    g = 4 * q4 + j
                    last = e.matmul(bk[:, j * 128:(j + 1) * 128], lhsT=utm8_[:, g, :, :].rearrange("p s c -> p (s c)"), rhs=ident_b, start=True, stop=True)
                return last
            P.pe(f, reads=[utm_k, pers.k("idb")], writes=[bkk])
            evac(U_dst[:, 4 * q4:4 * q4 + 4, :], bk.rearrange("p (j k) -> p j k", j=4), [bkk], [U_key])

    P.dma(nks_o[:, 0:124, :], ck_d[:, 4:128, :], key="c2o_k")
    P.dma(nvs_o[:, 0:124, :], cv_d[:, 4:128, :], key="c2o_v")
    hkey = hn_r.k()
    norm_tile(xp[NLITE * ST - 128:NLITE * ST, :], 128, hnT[:, :, 0:128], hkey, g_mix, "mix")
    k_cols(hnT[:, :, 0:128], hkey, 128, 0)
    kv_tile(hnT[:, :, 0:128], hkey, 128, 0)
    for st in range(2):
        specs = []
        for tl in range(8):
            r0 = st * ST + tl * 128
            hsl = hnT[:, :, tl * 128:(tl + 1) * 128]
            specs.append((xo[r0:r0 + 128, :], hsl, hkey,
                          (lambda hsl=hsl, st=st, tl=tl: kv_tile(hsl, hkey, 128, 1 + st * 8 + tl, last=(st == 1 and tl == 7)))))
        run_tiles(specs)
        if st == 0:
            bias_reads()
        for hf in range(2):
            c0 = hf * 512
            q_cols(hnT[:, :, c0:c0 + 512], hkey, 512, st * ST + c0)
            k_cols(hnT[:, :, c0:c0 + 512], hkey, 512, 128 + st * ST + c0)
        u_proj_st(hkey, hnT, u_tm8, utm_r.k())
        U_transposes(U_own[:, st], uown_r.k(st), u_tm8, utm_r.k())
        if st == 0:
            bias_stage3()
    norm_tile(xo[NP:NT, :], 64, hnT[:, :, 0:64], hkey, g_mix, "mix")
    q_cols(hnT[:, :, 0:64], hkey, 64, NP)
    k_cols(hnT[:, :, 0:64], hkey, 64, 128 + NP)
    kv_tile(hnT[:, :, 0:64], hkey, 64, 0, sample=True)
    P.pool(lambda e: e.memset(utm_r.f32(), 0.0), reads=[uown_r.k(0), uown_r.k(1)], writes=[utm_r.k()])
    for t in range(4):
        bk, bkk = PS(nb())

        def f(e, bk=bk, t=t, hnT=hnT):
            last = None
            for kt in range(8):
                last = e.matmul(bk[0:16, 0:512], lhsT=hnT[:, kt, t:64:4], rhs=W_U[:, kt, :], start=(kt == 0), stop=(kt == 7))
            return last
        P.pe(f, reads=[hkey, wu.k()], writes=[bkk])
        evac(u_s12[0:16, :, 4 + t, :], bk[0:16, :].rearrange("p (g c) -> p g c", g=32), [bkk], [utm_r.k()])
    U_s = us_r.bf().rearrange("p (w g b) -> p w g b", w=2, g=32)
    for w in range(2):
        bk, bkk = PS(5 + w)

        def f(e, bk=bk, w=w, u_s12=u_s12):
            last = None
            for g in range(32):
                last = e.matmul(bk[:, g * 16:(g + 1) * 16], lhsT=u_s12[0:16, g, 4 * w:4 * w + 8, :].rearrange("p s c -> p (s c)"),
                                rhs=ident_b[0:16, 0:16], start=True, stop=True)
            return last
        P.pe(f, reads=[utm_r.k(), pers.k("idb")], writes=[bkk])
        evac(U_s[:, w], bk.rearrange("p (g b) -> p g b", g=32), [bkk], [us_r.k()])
    AR.free(wqkv)
    for _r in [xs_r, hn_r, utm_r, stat_r, kvf_r] + xn_rs:
        AR.free(_r)
    if debug:
        dbg["qT"] = (qT_r, [128, 4 * NT], BF16)
        dbg["kT"] = (kT_r, [128, 4 * NTK], BF16)
        dbg["vaug"] = (va_r, [128, 17 * 2 * 72], BF16)
        dbg["Uown"] = (uown_r, [128, 2 * 32 * 128], BF16)
        dbg["Usamp"] = (us_r, [128, 2 * 32 * 16], BF16)

    mT_r = AR.alloc("mTattn", 4 * NT // 2)
    mT_att = mT_r.bf().rearrange("p (t n) -> p t n", t=4)
    aw = AR.alloc("attwork", 2 * 512 + 2 * 512 + 512 + 256 + 64 + 1024)
    tS = [aw.f32(0, 512), aw.f32(512, 512)]
    PT = [aw.bf(1024, 256).rearrange("p (r q) -> p r q", r=4), aw.bf(1280, 256).rearrange("p (r q) -> p r q", r=4),
          aw.bf(1536, 256).rearrange("p (r q) -> p r q", r=4), aw.bf(1792, 256).rearrange("p (r q) -> p r q", r=4)]
    PT += [aw.bf(2880 + 256 * i_, 256).rearrange("p (r q) -> p r q", r=4) for i_ in range(4)]
    o_sb = aw.f32(2048, 512)
    on_b = aw.bf(2560, 256)
    ast = aw.f32(2816, 64)
    ep_ctr = {"n": 0}

    def attn_epilogue(bO, nrows, tok0):
        i = ep_ctr["n"]
        ep_ctr["n"] += 1
        pr = slice(0, nrows)
        sc = ast[:, 16 * (i % 4):16 * (i % 4) + 16]
        ks = aw.k("st", i % 4)
        for g in range(2):
            bk, bkk = bO[g]
            Ov = bk[:, 0:288].rearrange("p (r d) -> p r d", r=4)
            tt("dve", sc[pr, 4 * g:4 * g + 4], Ov[pr, :, 64], sinkexp[pr, 4 * g:4 * g + 4], ALU.add, [bkk, gains.k("sink")], [ks])
        P.dve(lambda e: e.reciprocal(out=sc[pr, 0:8], in_=sc[pr, 0:8]), reads=[ks], writes=[ks])
        for g in range(2):
            bk, bkk = bO[g]
            Ov = bk[:, 0:288].rearrange("p (r d) -> p r d", r=4)
            tt("dve", o_sb[pr, 256 * g:256 * g + 256].rearrange("p (r d) -> p r d", r=4), Ov[pr, :, 0:64],
               sc[pr, 4 * g:4 * g + 4].unsqueeze(2).to_broadcast([nrows, 4, 64]), ALU.mult, [bkk, ks], [aw.k("o")])
        P.act(lambda e: e.activation(out=on_b[pr], in_=o_sb[pr], func=AF.Square, accum_out=sc[pr, 8:9]), reads=[aw.k("o")], writes=[aw.k("on"), ks + ("s",)])
        P.dve(lambda e: e.tensor_scalar(out=sc[pr, 9:10], in0=sc[pr, 8:9], scalar1=1.0 / 512, scalar2=EPS, op0=ALU.mult, op1=ALU.add),
              reads=[ks + ("s",)], writes=[ks + ("a",)])
        P.pool(lambda e: e.tensor_tensor(out=sc[pr, 10:11], in0=sc[pr, 9:10], in1=mhalf[pr], op=ALU.pow), reads=[ks + ("a",), pers.k("mh")], writes=[ks + ("b",)])
        P.dve(lambda e: e.scalar_tensor_tensor(out=on_b[pr], in0=o_sb[pr], scalar=sc[pr, 10:11], in1=g_att[pr], op0=ALU.mult, op1=ALU.mult),
              reads=[aw.k("o"), ks + ("b",), gains.k("att")], writes=[aw.k("on")])
        bk, bkk = PS(6)

        def f(e):
            last = None
            for t in range(4):
                last = e.matmul(bk[:, t * 128:t * 128 + nrows], lhsT=on_b[pr, t * 128:(t + 1) * 128], rhs=ident_b[pr, pr], start=True, stop=True)
            return last
        P.pe(f, reads=[aw.k("on"), pers.k("idb")], writes=[bkk])
        evac(mT_att[:, :, tok0:tok0 + nrows], bk.rearrange("p (t n) -> p t n", t=4)[:, :, 0:nrows], [bkk], [mT_r.k()], eng="act")

    def att_stage_a(b):
        j = b + 1
        ps_ = 4 * (b % 2)
        for g in range(2):
            for slot in range(2):
                kb = j - 1 + slot
                bS, bSk = PS(2 + slot)

                def f(e, bS=bS, g=g, kb=kb, b=b):
                    last = None
                    for r in range(4):
                        last = e.matmul(bS[:, r * 128:(r + 1) * 128], lhsT=kT[:, 2 * g + (r % 2), 128 * kb:128 * kb + 128],
                                        rhs=qT[:, 2 * g + r // 2, 128 * b:128 * b + 128], start=True, stop=True)
                    return last
                P.pe(f, reads=[kT_r.k(), qT_r.k()], writes=[bSk])
                bias = (biasH if b == 0 else biasT[:, 0]) if slot == 0 else biasT[:, 1]
                bkey = (bh_r.k() if b == 0 else pers.k("biasT")) if slot == 0 else pers.k("biasT")
                P.dve(lambda e, bS=bS, bias=bias, slot=slot, g=g: e.scalar_tensor_tensor(
                    out=tS[slot], in0=bS, scalar=0.125, in1=bias[:, 4 * g:4 * g + 4, :].rearrange("p r q -> p (r q)"), op0=ALU.mult, op1=ALU.add),
                    reads=[bSk, bkey], writes=[aw.k("tS", slot)])
                pt = PT[ps_ + 2 * g + slot]
                P.act(lambda e, pt=pt, slot=slot: e.activation(out=pt.rearrange("p r q -> p (r q)"), in_=tS[slot], func=AF.Exp),
                      reads=[aw.k("tS", slot)], writes=[aw.k("PT", ps_ + 2 * g + slot)])

    def att_stage_b(b):
        j = b + 1
        ps_ = 4 * (b % 2)
        bO = [PS(4), PS(5)]
        for g in range(2):
            bk, bkk = bO[g]

            def fpv(e, bk=bk, g=g, j=j, ps_=ps_):
                last = None
                for r in range(4):
                    for slot in range(2):
                        last = e.matmul(bk[:, r * 72:r * 72 + 65], lhsT=PT[ps_ + 2 * g + slot][:, r, :], rhs=v_aug[:, j - 1 + slot, g, 0:65],
                                        start=(slot == 0), stop=(slot == 1))
                return last
            P.pe(fpv, reads=[aw.k("PT", ps_ + 2 * g), aw.k("PT", ps_ + 2 * g + 1), va_r.k()], writes=[bkk])
        attn_epilogue(bO, 128, 128 * b)

    att_stage_a(0)
    for b in range(16):
        if b + 1 < 16:
            att_stage_a(b + 1)
        att_stage_b(b)

    sa = AR.alloc("sa_st", 2048)
    sa2 = AR.alloc("sa_kc", 2048)
    sa3 = AR.alloc("sa_vc", 1152)

    class _SK:
        def k(self, s_):
            return ("sa", s_)
    cst32 = sa.f32(0, 2048).rearrange("p (b d) -> p b d", b=16)
    kc_n = sa2.bf(0, 1024).rearrange("p (b d) -> p b d", b=16)
    kc_s = sa2.bf(1024, 1024).rearrange("p (b d) -> p b d", b=16)
    Vc = sa3.bf(0, 1152).rearrange("p (b g d) -> p b g d", b=16, g=2)
    Kz = [None] * 4
    _unused = [lambda i: sa4.bf(1024 * i, 1024).rearrange("p (b k) -> p b k", b=16) for i in range(4)]
    _sa0 = sa
    sa = type("X", (), {"k": staticmethod(lambda s_: ("sa_" + {"st": "st", "kcn": "kc", "kcs": "kc", "vc": "vc", "kz": "kz", "pfa": "pfa", "tsa": "ts", "tsb": "ts", "pb": "ts"}[s_], s_))})
    P.dma(cst32, ck_d.rearrange("b k d -> k b d"), writes=[sa.k("st")])
    P.pool(lambda e: e.tensor_copy(out=kc_n, in_=cst32), reads=[sa.k("st")], writes=[sa.k("kcn")])
    P.pool(lambda e: e.tensor_copy(out=kc_s[:, :, 0:64], in_=cst32[:, :, 64:128]), reads=[sa.k("st")], writes=[sa.k("kcs")])
    P.pool(lambda e: e.tensor_copy(out=kc_s[:, :, 64:128], in_=cst32[:, :, 0:64]), reads=[sa.k("st")], writes=[sa.k("kcs")])
    P.pool(lambda e: e.memset(Vc, 1.0), writes=[sa.k("vc")])
    P.dma(cst32, cv_d.rearrange("b k d -> k b d"), writes=[sa.k("st")])
    P.pool(lambda e: e.tensor_copy(out=Vc[:, :, :, 0:64], in_=cst32.rearrange("p b (g d) -> p b g d", g=2)), reads=[sa.k("st")], writes=[sa.k("vc")])
    AR.free(_sa0)
    sa4 = AR.alloc("sa_kz", 4096)
    Kz = [sa4.bf(1024 * i, 1024).rearrange("p (b k) -> p b k", b=16) for i in range(4)]
    P.pool(lambda e: e.memset(sa4.f32(), 0.0), writes=[sa.k("kz")])
    for typ, src, (lo_i, hi_i) in ((0, kc_n, (0, 1)), (1, kc_s, (2, 3))):
        for b4 in range(4):
            bk, bkk = PS(2 + b4 % 2)

            def f(e, bk=bk, src=src, b4=b4):
                last = None
                for jj in range(4):
                    last = e.matmul(bk[:, jj * 128:(jj + 1) * 128], lhsT=src[:, 4 * b4 + jj, :], rhs=ident_b, start=True, stop=True)
                return last
            P.pe(f, reads=[sa.k("kcn"), sa.k("kcs"), pers.k("idb")], writes=[bkk])
            bv = bk.rearrange("p (j k) -> p j k", j=4)
            evac(Kz[lo_i][0:64, 4 * b4:4 * b4 + 4, :], bv[0:64], [bkk], [sa.k("kz")], eng="act")
            evac(Kz[hi_i][64:128, 4 * b4:4 * b4 + 4, :], bv[64:128], [bkk], [sa.k("kz")], eng="dve")
    AR.free(sa2)
    sa5 = AR.alloc("sa_ts", 512 + 512 + 256)
    sa6 = AR.alloc("sa_pfa", 2048)
    tSA = sa5.f32(0, 512)
    PfA = sa6.bf(0, 2048)
    tSB = sa5.f32(512, 512)
    PBs = sa5.bf(1024, 256).rearrange("p (h q) -> p h q", h=8)
    P.pool(lambda e: e.memset(sa6.f32(), 0.0), writes=[sa.k("pfa")])
    _sa_regs = [sa3, sa4, sa5, sa6]
    KV = {(0, 0): Kz[0], (0, 1): Kz[3], (1, 0): Kz[2], (1, 1): Kz[1]}
    bA, bAk = PS(2)

    def fqa(e, bA=bA):
        last = None
        for b_ in range(NSQ):
            for g in range(2):
                for r in range(4):
                    c0 = ((b_ * 2 + g) * 4 + r) * 4
                    last = e.matmul(bA[:, c0:c0 + 4], lhsT=KV[(g, r % 2)][:, b_, :], rhs=qT[:, 2 * g + r // 2, NP + 4 * b_:NP + 4 * b_ + 4],
                                    start=True, stop=True)
        return last
    P.pe(fqa, reads=[sa.k("kz"), qT_r.k()], writes=[bAk])
    P.dve(lambda e: e.scalar_tensor_tensor(out=tSA.rearrange("p (b h i) -> p b h i", b=16, h=8), in0=bA.rearrange("p (b h i) -> p b h i", b=16, h=8),
                                           scalar=0.125, in1=biasA.unsqueeze(1).to_broadcast([128, 16, 8, 4]), op0=ALU.mult, op1=ALU.add),
          reads=[bAk, pers.k("biasA")], writes=[sa.k("tsa")])
    bB, bBk = PS(3)

    def fqb(e, bB=bB):
        last = None
        for h in range(8):
            g, r = h // 4, h % 4
            last = e.matmul(bB[0:64, h * 64:(h + 1) * 64], lhsT=kT[:, 2 * g + (r % 2), 128 + NP:128 + NP + 64],
                            rhs=qT[:, 2 * g + r // 2, NP:NP + 64], start=True, stop=True)
        return last
    P.pe(fqb, reads=[kT_r.k(), qT_r.k()], writes=[bBk])
    P.dve(lambda e: e.scalar_tensor_tensor(out=tSB[0:64], in0=bB[0:64], scalar=0.125, in1=biasB[0:64].rearrange("p h q -> p (h q)"),
                                           op0=ALU.mult, op1=ALU.add), reads=[bBk, pers.k("biasB")], writes=[sa.k("tsb")])
    P.act(lambda e: e.activation(out=PBs[0:64].rearrange("p h q -> p (h q)"), in_=tSB[0:64], func=AF.Exp), reads=[sa.k("tsb")], writes=[sa.k("pb")])
    PfAv = PfA.rearrange("p (r b q) -> p r b q", r=4, b=16)
    pfa_out = bass.AP(PfA.tensor, PfA.offset, [list(PfA.ap[0]), [68, 16], [1024, 4], [1, 4]])
    bO = [PS(4), PS(5)]
    for g in range(2):
        bk, bkk = bO[g]
        P.act(lambda e, g=g: e.activation(out=pfa_out, in_=tSA.rearrange("p (b h i) -> p b h i", b=16, h=8)[:, :, 4 * g:4 * g + 4, :], func=AF.Exp),
              reads=[sa.k("tsa"), sa.k("pfa")], writes=[sa.k("pfa")])

        def fpvs(e, bk=bk, g=g):
            last = None
            for r in range(4):
                h = 4 * g + r
                for b_ in range(NSQ):
                    e.matmul(bk[0:64, r * 72:r * 72 + 65], lhsT=PfAv[:, r, b_, :], rhs=Vc[:, b_, g, 0:65], start=(b_ == 0), stop=False)
                last = e.matmul(bk[0:64, r * 72:r * 72 + 65], lhsT=PBs[0:64, h, :], rhs=vs_aug[0:64, g, 0:65], start=False, stop=True)
            return last
        P.pe(fpvs, reads=[sa.k("pfa"), sa.k("pb"), sa.k("vc"), vs_r.k()], writes=[bkk])
    attn_epilogue(bO, 64, NP)
    for _r in _sa_regs:
        AR.free(_r)
    AR.free(aw)
    AR.free(qT_r)
    AR.free(kT_r)
    AR.free(va_r)
    AR.free(vs_r)
    AR.free(bh_r)
    if debug:
        dbg["mTattn"] = (mT_r, [128, 4 * NT], BF16)
        dbg["pers"] = (pers, [128, 3048], F32)

    hn2_rs = [AR.alloc("hnT2a", 8 * 1024 // 2), AR.alloc("hnT2b", 8 * 1024 // 2)]
    hnT2s = [r_.bf().rearrange("p (k n) -> p k n", k=8) for r_ in hn2_rs]
    gb_r = AR.alloc("gbuf", 16 * 2 * 128)
    G = gb_r.f32().rearrange("p (g r k) -> p g r k", g=16, r=2)
    xs_r = AR.alloc("xs2", 3 * 1024)
    utm2_r = AR.alloc("utm8b", 32 * 8 * 16 // 2)
    u_tm8b = utm2_r.bf().rearrange("p (g s c) -> p g s c", g=32, s=8)
    ul_rs = [AR.alloc("Ulitea", 32 * 128 // 2), AR.alloc("Uliteb", 32 * 128 // 2)]
    U_ls = [r_.bf().rearrange("p (g k) -> p g k", g=32) for r_ in ul_rs]
    rt_r = AR.alloc("rottmp", 2048)
    xn_rs = [AR.alloc("xn2a", 512), AR.alloc("xn2b", 512)]
    stat_r = AR.alloc("stats2", 256)
    ss_r = AR.alloc("sscr", 16 * 12)
    SS = [ss_r.f32(16 * i, 16) for i in range(12)]
    kH = ksm("H")
    P.pool(lambda e: e.memset(SS[11], 0.0), reads=[ksm("hre"), ksm("him")], writes=[kH])

    def S_rotate(Uv, Ukey, eng="dve"):
        for q4 in range(4):
            (bre, brek), (bim, bimk) = (PS(4), PS(5)) if q4 % 2 == 0 else (PS(6), PS(7))

            def f(e, bre=bre, bim=bim, q4=q4, Uv=Uv):
                last = None
                for j in range(4):
                    gq = 4 * q4 + j
                    for ri, bank in ((0, bre), (1, bim)):
                        e.matmul(bank[:, j * 128:(j + 1) * 128], lhsT=W_P[:, gq, ri, :], rhs=Uv[:, gq, :], start=True, stop=False)
                        last = e.matmul(bank[:, j * 128:(j + 1) * 128], lhsT=W_P[:, 16 + gq, ri, :], rhs=Uv[:, 16 + gq, :], start=False, stop=True)
                return last
            P.pe(f, reads=[wp.k(), Ukey], writes=[brek, bimk])
            gs = slice(4 * q4, 4 * q4 + 4)
            ck = CK[:, gs, 0:128]
            sk = SK[:, gs, 0:128]
            tk = [tabs.k("ck"), tabs.k("sk")]
            if eng == "dve":
                Sre = bre.rearrange("p (g k) -> p g k", g=4)
                Sim = bim.rearrange("p (g k) -> p g k", g=4)
                tA = rt_r.f32(0, 512).rearrange("p (g k) -> p g k", g=4)
                tB = rt_r.f32(512, 512).rearrange("p (g k) -> p g k", g=4)
                kA = rt_r.k("a"); kB = rt_r.k("b")
                tt("dve", tA, ck, Sre, ALU.mult, tk + [brek], [kA])
                tt("dve", tB, sk, Sim, ALU.mult, tk + [bimk], [kB])
                tt("dve", G[:, gs, 0, :], tA, tB, ALU.add, [kA, kB], [gb_r.k(q4)])
                tt("dve", tA, ck, Sim, ALU.mult, tk + [bimk], [kA])
                tt("dve", tB, sk, Sre, ALU.mult, tk + [brek], [kB])
                tt("dve", G[:, gs, 1, :], tA, tB, ALU.subtract, [kA, kB], [gb_r.k(q4)])
            else:
                gk = gb_r.k(q4)
                evac(G[:, gs, 0, :], bre.rearrange("p (g k) -> p g k", g=4), [brek], [gk], eng="act")
                evac(G[:, gs, 1, :], bim.rearrange("p (g k) -> p g k", g=4), [bimk], [gk], eng="act")
                tmps = [rt_r.f32(512 * i, 512).rearrange("p (g k) -> p g k", g=4) for i in range(4)]
                kt_ = [rt_r.k("t", i) for i in range(4)]
                tt(eng, tmps[0], ck, G[:, gs, 0, :], ALU.mult, tk + [gk], [kt_[0]])
                tt(eng, tmps[1], sk, G[:, gs, 1, :], ALU.mult, tk + [gk], [kt_[1]])
                tt(eng, tmps[2], ck, G[:, gs, 1, :], ALU.mult, tk + [gk], [kt_[2]])
                tt(eng, tmps[3], sk, G[:, gs, 0, :], ALU.mult, tk + [gk], [kt_[3]])
                tt(eng, G[:, gs, 0, :], tmps[0], tmps[1], ALU.add, [kt_[0], kt_[1]], [gk])
                tt(eng, G[:, gs, 1, :], tmps[2], tmps[3], ALU.subtract, [kt_[2], kt_[3]], [gk])

    def scan_all(init, eng="dve"):
        for gq in range(16):
            for ri in range(2):
                ini = 0.0 if init is None else init[ri][:, gq:gq + 1]
                rd = [gb_r.k(gq // 4), ksm("r8")] + ([] if init is None else [ss_r.k("ini")])
                P.op(eng, lambda e, gq=gq, ri=ri, ini=ini: e.tensor_tensor_scan(
                    out=G[:, gq, ri, :], data0=R8[:, gq:gq + 1].to_broadcast([128, 128]), data1=G[:, gq, ri, :],
                    initial=ini, op0=ALU.mult, op1=ALU.add), reads=rd, writes=[gb_r.k(gq // 4)])

    def rot_small(ore, oim, cc, sn, xre, xim, rd, wr, eng="dve"):
        tt(eng, SS[0], cc, xre, ALU.mult, rd, [ss_r.k(0)])
        tt(eng, SS[1], sn, xim, ALU.mult, rd, [ss_r.k(1)])
        tt(eng, SS[2], cc, xim, ALU.mult, rd, [ss_r.k(2)])
        tt(eng, SS[3], sn, xre, ALU.mult, rd, [ss_r.k(3)])
        tt(eng, ore, SS[0], SS[1], ALU.subtract, [ss_r.k(0), ss_r.k(1)], wr)
        tt(eng, oim, SS[2], SS[3], ALU.add, [ss_r.k(2), ss_r.k(3)], wr)

    tabk = [tabs.k("ck"), tabs.k("sk")]
    GK = [gb_r.k(i_) for i_ in range(4)]

    def final_state(ore, oim, wr, eng="dve"):
        rot_small(ore, oim, CK[:, :, 127], SK[:, :, 127], G[:, :, 0, 127], G[:, :, 1, 127], tabk + GK, wr, eng=eng)

    def lite_T_gen(lt):
        pb = lt % 2
        hk2 = hn2_rs[pb].k()
        hnT_ = hnT2s[pb]
        specs = [(xp[lt * ST + tl * 128:lt * ST + tl * 128 + 128, :], hnT_[:, :, tl * 128:(tl + 1) * 128]) for tl in range(8)]
        ctxs = [None] * 8
        ctxs[0] = norm_front(specs[0][0], 128, g_mix, "mix")
        for i in range(8):
            if i + 1 < 8:
                ctxs[i + 1] = norm_front(specs[i + 1][0], 128, g_mix, "mix")
            norm_back(ctxs[i], specs[i][1], hk2)
            yield

    def lite_U_gen(lt):
        pb = lt % 2
        hk2 = hn2_rs[pb].k()
        hnT_ = hnT2s[pb]
        for s_ in range(8):
            bk, bkk = PS(nb())

            def f(e, bk=bk, s_=s_, hnT_=hnT_):
                last = None
                for kt in range(8):
                    last = e.matmul(bk[:, 0:512], lhsT=hnT_[:, kt, s_::8], rhs=W_U[:, kt, :], start=(kt == 0), stop=(kt == 7))
                return last
            P.pe(f, reads=[hk2, wu.k()], writes=[bkk])
            evac(u_tm8b[:, :, s_, :], bk.rearrange("p (g c) -> p g c", g=32), [bkk], [utm2_r.k()])
            yield
        U_dst = U_ls[pb]
        for q4 in range(8):
            bk, bkk = PS(5 + q4 % 2)

            def f(e, bk=bk, q4=q4):
                last = None
                for j in range(4):
                    g = 4 * q4 + j
                    last = e.matmul(bk[:, j * 128:(j + 1) * 128], lhsT=u_tm8b[:, g, :, :].rearrange("p s c -> p (s c)"), rhs=ident_b, start=True, stop=True)
                return last
            P.pe(f, reads=[utm2_r.k(), pers.k("idb")], writes=[bkk])
            evac(U_dst[:, 4 * q4:4 * q4 + 4, :], bk.rearrange("p (j k) -> p j k", j=4), [bkk], [ul_rs[pb].k()])
            yield

    def scan_gen():
        for gq in range(16):
            for ri in range(2):
                P.dve(lambda e, gq=gq, ri=ri: e.tensor_tensor_scan(
                    out=G[:, gq, ri, :], data0=R8[:, gq:gq + 1].to_broadcast([128, 128]), data1=G[:, gq, ri, :],
                    initial=0.0, op0=ALU.mult, op1=ALU.add), reads=[gb_r.k(gq // 4), ksm("r8")], writes=[gb_r.k(gq // 4)])
            yield

    def drain(g):
        for _ in g:
            pass

    def interleave(gens):
        gens = [g for g in gens if g is not None]
        while gens:
            for g in list(gens):
                try:
                    next(g)
                except StopIteration:
                    gens.remove(g)

    drain(lite_T_gen(0))
    interleave([lite_T_gen(1), lite_U_gen(0)])
    interleave([lite_T_gen(2), lite_U_gen(1)])
    for lt in range(NLITE):
        S_rotate(U_ls[lt % 2], ul_rs[lt % 2].k(), eng=("pool" if lt >= 3 else "dve"))
        interleave([lite_T_gen(lt + 3) if lt + 3 < NLITE else None,
                    lite_U_gen(lt + 2) if lt + 2 < NLITE else None,
                    scan_gen()])
        final_state(SS[4], SS[5], [ss_r.k("F")])
        rot_small(SS[6], SS[7], CK[:, :, 128], SK[:, :, 128], HRE, HIM, tabk + [kH], [ss_r.k("LH")])
        tt("dve", HRE, SS[6], R128, ALU.mult, [ss_r.k("LH"), ksm("r128")], [kH])
        tt("dve", HIM, SS[7], R128, ALU.mult, [ss_r.k("LH"), ksm("r128")], [kH])
        tt("dve", HRE, HRE, SS[4], ALU.add, [kH, ss_r.k("F")], [kH])
        tt("dve", HIM, HIM, SS[5], ALU.add, [kH, ss_r.k("F")], [kH])
    for _r in [xs_r, stat_r, utm2_r, rt_r] + xn_rs + hn2_rs + ul_rs:
        AR.free(_r)
    rt_r = AR.alloc("rottmp2", 2 * 2032)
    rt2_r = AR.alloc("rottmp3", 2 * 2032)
    xp_r = AR.alloc("Xprev", 2 * 16 * 2 * 128 // 2)
    Xprev = xp_r.bf().rearrange("p (s g r k) -> p s g r k", s=2, g=16, r=2)

    for st in range(2):
        S_rotate(U_own[:, st], uown_r.k(st))
        rot_small(SS[8], SS[9], CK[:, :, 1], SK[:, :, 1], HRE, HIM, tabk + [kH], [ss_r.k("ini")])
        scan_all((SS[8], SS[9]))
        P.dve(lambda e, st=st: e.tensor_copy(out=Xprev[:, st, :, 0, 0], in_=HRE), reads=[kH], writes=[xp_r.k(st)])
        P.dve(lambda e, st=st: e.tensor_copy(out=Xprev[:, st, :, 1, 0], in_=HIM), reads=[kH], writes=[xp_r.k(st)])
        tA = rt_r.f32(0, 2032).rearrange("p (g k) -> p g k", g=16)
        tB = rt_r.f32(2032, 2032).rearrange("p (g k) -> p g k", g=16)
        kA = rt_r.k("a"); kB = rt_r.k("b")
        ck = CK[:, :, 0:127]; sk = SK[:, :, 0:127]
        tt("dve", tA, ck, G[:, :, 0, 0:127], ALU.mult, tabk + GK, [kA])
        tt("dve", tB, sk, G[:, :, 1, 0:127], ALU.mult, tabk + GK, [kB])
        tt("dve", Xprev[:, st, :, 0, 1:128], tA, tB, ALU.subtract, [kA, kB], [xp_r.k(st)])
        tC = rt2_r.f32(0, 2032).rearrange("p (g k) -> p g k", g=16)
        tD = rt2_r.f32(2032, 2032).rearrange("p (g k) -> p g k", g=16)
        kC = rt2_r.k("c"); kD = rt2_r.k("d")
        tt("pool", tC, ck, G[:, :, 1, 0:127], ALU.mult, tabk + GK, [kC])
        tt("pool", tD, sk, G[:, :, 0, 0:127], ALU.mult, tabk + GK, [kD])
        tt("pool", Xprev[:, st, :, 1, 1:128], tC, tD, ALU.add, [kC, kD], [xp_r.k(st, "im")])
        final_state(HRE, HIM, [kH])
    AR.free(gb_r)
    AR.free(rt_r)
    AR.free(rt2_r)
    AR.free(tabs)
    AR.free(wu)

    h0_r = AR.alloc("h0", 2 * 256 + 256 + 2 * 256 + 512)
    h0n_r = AR.alloc("h0n", 2 * 2048)
    h0n = [h0n_r.f32(0, 2048), h0n_r.f32(2048, 2048)]
    h0T = [h0_r.f32(0, 256).rearrange("p (g b) -> p g b", g=16), h0_r.f32(256, 256).rearrange("p (g b) -> p g b", g=16)]
    h0b = h0_r.bf(512, 256).rearrange("p (g r b) -> p g r b", g=16, r=2)
    XN = [h0_r.f32(768, 256).rearrange("p (g b) -> p g b", g=16), h0_r.f32(1024, 256).rearrange("p (g b) -> p g b", g=16)]
    for pl, src in ((0, sre_d), (1, sim_d)):
        for gh in range(2):
            P.dma(h0n[pl][0:16].rearrange("p (g h q) -> p g h q", g=16, h=2)[:, :, gh, :], dram_ap(src, 1024 * gh, [[2048, 16], [64, 16], [1, 64]]),
                  writes=[h0n_r.k(pl)])
        bk, bkk = PS(4 + pl)

        def f(e, bk=bk, pl=pl):
            last = None
            for gq in range(16):
                last = e.matmul(bk[:, gq * 16:(gq + 1) * 16], lhsT=h0n[pl][0:16, gq * 128:(gq + 1) * 128], rhs=ident_f[0:16, 0:16], start=True, stop=True)
            return last
        P.pe(f, reads=[h0n_r.k(pl), pers.k("idf")], writes=[bkk])
        evac(h0T[pl].rearrange("p g b -> p (g b)"), bk[:, 0:256], [bkk], [h0_r.k("T", pl)], eng="act")
        P.dve(lambda e, pl=pl: e.tensor_copy(out=h0b[:, :, pl, :], in_=h0T[pl]), reads=[h0_r.k("T", pl)], writes=[h0_r.k("b")])
    AR.free(h0n_r)
    xo_r = AR.alloc("xosb", 2048)
    xo_sb = xo_r.f32()
    bS, bSk = PS(6)

    def fss(e, bS=bS):
        last = None
        for gq in range(16):
            for ri in range(2):
                c0 = (gq * 2 + ri) * 16
                e.matmul(bS[:, c0:c0 + 16], lhsT=W_P[:, gq, ri, :], rhs=U_s[:, 0, gq, :], start=True, stop=False)
                last = e.matmul(bS[:, c0:c0 + 16], lhsT=W_P[:, 16 + gq, ri, :], rhs=U_s[:, 0, 16 + gq, :], start=False, stop=True)
        return last
    P.pe(fss, reads=[wp.k(), us_r.k()], writes=[bSk])
    Sv = bS.rearrange("p (g r b) -> p g r b", g=16, r=2)
    l4r = LRE[:, :, 4].unsqueeze(2).to_broadcast([128, 16, 16])
    l4i = LIM[:, :, 4].unsqueeze(2).to_broadcast([128, 16, 16])
    vv1 = h0_r.f32(1280, 256).rearrange("p (g b) -> p g b", g=16)
    vv2 = h0_r.f32(1536, 256).rearrange("p (g b) -> p g b", g=16)
    kv1_ = h0_r.k("w1"); kv2_ = h0_r.k("w2")
    lk = [ksm("lre"), ksm("lim")]
    tt("dve", vv1, l4r, h0T[0], ALU.mult, lk + [h0_r.k("T", 0)], [kv1_])
    tt("dve", vv2, l4i, h0T[1], ALU.mult, lk + [h0_r.k("T", 1)], [kv2_])
    tt("dve", vv1, vv1, vv2, ALU.subtract, [kv1_, kv2_], [kv1_])
    tt("dve", XN[0], vv1, Sv[:, :, 0, :], ALU.add, [kv1_, bSk], [h0_r.k("X", 0)])
    tt("dve", vv1, l4r, h0T[1], ALU.mult, lk + [h0_r.k("T", 1)], [kv1_])
    tt("dve", vv2, l4i, h0T[0], ALU.mult, lk + [h0_r.k("T", 0)], [kv2_])
    tt("dve", vv1, vv1, vv2, ALU.add, [kv1_, kv2_], [kv1_])
    tt("dve", XN[1], vv1, Sv[:, :, 1, :], ALU.add, [kv1_, bSk], [h0_r.k("X", 1)])
    for pl, dst in ((0, ssre_o), (1, ssim_o)):
        for q4 in range(4):
            bk, bkk = PS(4 + q4 % 2)

            def f(e, bk=bk, pl=pl, q4=q4):
                last = None
                for j in range(4):
                    last = e.matmul(bk[0:16, j * 128:(j + 1) * 128], lhsT=XN[pl][:, 4 * q4 + j, :], rhs=ident_f, start=True, stop=True)
                return last
            P.pe(f, reads=[h0_r.k("X", pl), pers.k("idf")], writes=[bkk])
            evac(xo_sb[0:16].rearrange("p (h g q) -> p g h q", h=2, g=16)[:, 4 * q4:4 * q4 + 4, :, :],
                 bk[0:16, :].rearrange("p (g h q) -> p g h q", g=4, h=2), [bkk], [xo_r.k()], eng="act")
        P.dma(dst, xo_sb[0:16], reads=[xo_r.k()])
    AR.free(xo_r)
    AR.free(wp)
    if debug:
        dbg["Xprev"] = (xp_r, [128, 2 * 16 * 2 * 128], BF16)
        dbg["sm2"] = (sm, [128, 2816], F32)

    _s3 = {"cn": AR.alloc("s3cn", 512), "g9r": AR.alloc("s3g9r", 2304), "g9i": AR.alloc("s3g9i", 2304), "u1": AR.alloc("s3u1", 2304), "u2": AR.alloc("s3u2", 2304)}

    class _S3:
        @staticmethod
        def k(x):
            return (_s3["cn" if x.startswith("cn") else x].name, x)
    s3 = _S3
    Cn = _s3["cn"].f32(0, 512).rearrange("p (r s q) -> p r s q", r=2, s=2)
    G9R = _s3["g9r"].f32().rearrange("p (g j c) -> p g j c", g=16, j=9)
    G9I = _s3["g9i"].f32().rearrange("p (g j c) -> p g j c", g=16, j=9)
    u1 = _s3["u1"].f32().rearrange("p (g j c) -> p g j c", g=16, j=9)
    u2 = _s3["u2"].f32().rearrange("p (g j c) -> p g j c", g=16, j=9)
    for gs in range(2):
        P.dma(Cn[:, 0, gs, :].rearrange("p (h q) -> p h q", h=2), dram_ap(cre_d, 8192 * gs, [[64, 128], [16384, 2], [1, 64]]), writes=[s3.k("cn0")])
        P.dma(Cn[:, 1, gs, :].rearrange("p (h q) -> p h q", h=2), dram_ap(cim_d, 8192 * gs, [[64, 128], [16384, 2], [1, 64]]), writes=[s3.k("cn1")])
    for ri, CT in ((0, CTRE), (1, CTIM)):
        bk, bkk = PS(4 + ri)
        for gs in range(2):
            P.pe(lambda e, bk=bk, gs=gs, ri=ri: e.matmul(bk[:, gs * 128:(gs + 1) * 128], lhsT=Cn[:, ri, gs, :], rhs=ident_f, start=True, stop=True),
                 reads=[s3.k("cn%d" % ri), pers.k("idf")], writes=[bkk + (gs,)])
        P.act(lambda e, bk=bk, CT=CT: e.copy(out=CT.rearrange("p g c -> p (g c)"), in_=bk[:, 0:256]),
              reads=[bkk + (0,), bkk + (1,)], writes=[ksm("ct%d" % ri)])
    _STOP = 9
    cj = lambda a: a.unsqueeze(2).to_broadcast([128, 16, 9, 16])
    lj = lambda a: a.unsqueeze(3).to_broadcast([128, 16, 9, 16])
    if _STOP >= 2:
      tt("dve", u1, cj(CTRE), lj(LRE), ALU.mult, [ksm("ct0"), ksm("lre")], [s3.k("u1")])
      tt("dve", u2, cj(CTIM), lj(LIM), ALU.mult, [ksm("ct1"), ksm("lim")], [s3.k("u2")])
      tt("dve", G9R, u1, u2, ALU.subtract, [s3.k("u1"), s3.k("u2")], [s3.k("g9r")])
      tt("dve", u1, cj(CTRE), lj(LIM), ALU.mult, [ksm("ct0"), ksm("lim")], [s3.k("u1")])
      tt("dve", u2, cj(CTIM), lj(LRE), ALU.mult, [ksm("ct1"), ksm("lre")], [s3.k("u2")])
      tt("dve", G9I, u1, u2, ALU.add, [s3.k("u1"), s3.k("u2")], [s3.k("g9i")])

    AR.free(_s3.pop("u1"))
    AR.free(_s3.pop("u2"))
    wq = AR.alloc("wq", 32 * 2 * 128 // 2)
    W_Q = wq.bf().rearrange("p (g r m) -> p g r m", g=32, r=2)
    wm = AR.alloc("wm", 32 * 128 // 2)
    W_M = wm.bf().rearrange("p (g m) -> p g m", g=32)
    P.pool(lambda e: e.memset(wq.f32(), 0.0), writes=[wq.k()])
    _s4 = {"fp0": AR.alloc("s4fp0", 3840), "fp1": AR.alloc("s4fp1", 3840), "zz": AR.alloc("s4zz", 3840), "dc": AR.alloc("s4dc", 32)}

    class _S4:
        @staticmethod
        def k(x=None):
            return "s4all" if x is None else (_s4["dc"].name, x)
    s4 = _S4
    FPl = [_s4["fp0"].bf().rearrange("p (g m) -> p g m", g=32), _s4["fp1"].bf().rearrange("p (g m) -> p g m", g=32)]
    ZZ = _s4["zz"].bf().rearrange("p (r g m) -> p r g m", r=2, g=16)
    DC = _s4["dc"].f32(0, 32)
    S4K = ["s4all"] + [_s4[n].k() for n in ("fp0", "fp1", "zz")]
    for _nm in ("fp0", "fp1", "zz"):
        P.pool(lambda e, _nm=_nm: e.memset(_s4[_nm].f32(), 0.0), writes=S4K)
    for s_ in range(8):
        P.dma(DC[16 * s_:16 * s_ + 16, :], dram_ap(dd_d, 0, [[1, 16], [16, 32]]), writes=[s4.k("dc")], allow_slow_non_contiguous=True)
    for gh in (range(2) if _STOP >= 3 else []):
        hs = slice(64 * gh, 64 * gh + 64)
        gsl = slice(16 * gh, 16 * gh + 16)
        P.act(lambda e, hs=hs, gsl=gsl: e.copy(out=W_Q[hs, gsl, 0, :].rearrange("p g (j c) -> p g j c", j=8), in_=G9R[hs, :, 1:9, :]),
              reads=[s3.k("g9r")], writes=[wq.k()])
        P.act(lambda e, hs=hs, gsl=gsl: e.mul(out=W_Q[hs, gsl, 1, :].rearrange("p g (j c) -> p g j c", j=8), in_=G9I[hs, :, 1:9, :], mul=-1.0),
              reads=[s3.k("g9i")], writes=[wq.k()])
        P.act(lambda e, hs=hs, gsl=gsl: e.copy(out=FPl[0][hs, gsl, 112:240].rearrange("p g (j c) -> p g j c", j=8), in_=G9R[hs, :, 0:8, :]),
              reads=[s3.k("g9r")], writes=S4K)
        P.act(lambda e, hs=hs, gsl=gsl: e.mul(out=FPl[1][hs, gsl, 112:240].rearrange("p g (j c) -> p g j c", j=8), in_=G9I[hs, :, 0:8, :], mul=-1.0),
              reads=[s3.k("g9i")], writes=S4K)
    P.dve(lambda e: e.tensor_copy(out=ZZ[:, 0, :, 112:128], in_=BRE), reads=[ksm("bre")], writes=S4K)
    P.dve(lambda e: e.tensor_copy(out=ZZ[:, 1, :, 112:128], in_=BIM), reads=[ksm("bim")], writes=S4K)
    for g in (range(32) if _STOP >= 4 else []):
        gq = g % 16
        bk, bkk = PS(4 + (g // 4) % 4)
        o_ = bk[:, (g % 4) * 128:(g % 4) * 128 + 128]

        def f(e, o_=o_, g=g, gq=gq):
            last = None
            for s_ in range(8):
                lo = 112 - 16 * s_
                for ri in range(2):
                    last = e.matmul(o_, lhsT=ZZ[:, ri, gq, lo:lo + 128], rhs=FPl[ri][:, g, lo:lo + 128],
                                    start=(s_ == 0 and ri == 0), stop=(s_ == 7 and ri == 1))
            return last
        P.pe(f, reads=S4K, writes=[bkk + (g % 4,)])
        if True:
            P.dve(lambda e, o_=o_, g=g: e.scalar_tensor_tensor(out=W_M[:, g, :], in0=ident_f, scalar=DC[:, g:g + 1], in1=o_,
                                                           op0=ALU.mult, op1=ALU.add),
              reads=[bkk + (g % 4,), s4.k("dc"), pers.k("idf")], writes=[wm.k()])
    for _r in list(_s3.values()) + list(_s4.values()):
        AR.free(_r)
    for gh in range(2):
        P.dma(dram_ap(spre_o, 1024 * gh, [[1, 64], [64, 16]]), HRE[64 * gh:64 * gh + 64, :], reads=[kH], allow_slow_non_contiguous=True)
        P.dma(dram_ap(spim_o, 1024 * gh, [[1, 64], [64, 16]]), HIM[64 * gh:64 * gh + 64, :], reads=[kH], allow_slow_non_contiguous=True)
    if debug:
        dbg["W_Q"] = (wq, [128, 32 * 2 * 128], BF16)
        dbg["W_M"] = (wm, [128, 32 * 128], BF16)

    gl_r = AR.alloc("gl", 2 * 8 * 512 // 2 + 4 * 512 // 2)
    gl_tm8 = gl_r.bf(0, 4096).rearrange("p (s t g c) -> p s t g c", s=2, t=8, g=32)
    gl_s = gl_r.bf(4096, 1024).rearrange("p (t g c) -> p t g c", t=4, g=32)
    yw = AR.alloc("ywork", 4 * 512)
    yA = yw.f32(0, 512); yB = yw.f32(512, 512); yC = yw.f32(1024, 512); yD = yw.f32(1536, 512)
    GC0 = 0.7978845608028654
    GC1 = 0.044715

    def gelu_bank(bk, bkk, nrows, out_ap, out_key):
        pr = slice(0, nrows)
        P.act(lambda e: e.activation(out=yA[pr], in_=bk[pr], func=AF.Copy, scale=0.5), reads=[bkk], writes=[yw.k("a")])
        P.act(lambda e: e.activation(out=yB[pr], in_=bk[pr], func=AF.Square), reads=[bkk], writes=[yw.k("b"), yw.k("b2")])
        P.act(lambda e: e.activation(out=yD[pr], in_=yB[pr], func=AF.Identity, scale=GC1, bias=1.0), reads=[yw.k("b")], writes=[yw.k("d")])
        tt("dve", yB[pr], yD[pr], yA[pr], ALU.mult, [yw.k("d"), yw.k("a")], [yw.k("b2")])
        P.act(lambda e: e.activation(out=yC[pr], in_=yB[pr], func=AF.Tanh, scale=2.0 * GC0), reads=[yw.k("b2")], writes=[yw.k("c")])
        g_ = out_ap.shape[1]
        t_ = out_ap.shape[2]
        P.dve(lambda e: e.tensor_scalar(out=yC[pr], in0=yC[pr], scalar1=1.0, scalar2=None, op0=ALU.add), reads=[yw.k("c")], writes=[yw.k("c")])
        tt("dve", out_ap, yC[pr].rearrange("p (g t c) -> p g t c", g=g_, t=t_), yA[pr].rearrange("p (g t c) -> p g t c", g=g_, t=t_),
           ALU.mult, [yw.k("c"), yw.k("a")], [out_key])

    for st in range(2):
        for q8 in range(8):
            bk, bkk = PS(q8 % 2)

            def f(e, bk=bk, st=st, q8=q8):
                last = None
                for j in range(4):
                    g = 4 * q8 + j
                    gq = g % 16
                    o_ = bk[:, j * 128:(j + 1) * 128]
                    e.matmul(o_, lhsT=U_own[:, st, g, :], rhs=W_M[:, g, :], start=True, stop=False)
                    e.matmul(o_, lhsT=Xprev[:, st, gq, 0, :], rhs=W_Q[:, g, 0, :], start=False, stop=False)
                    last = e.matmul(o_, lhsT=Xprev[:, st, gq, 1, :], rhs=W_Q[:, g, 1, :], start=False, stop=True)
                return last
            P.pe(f, reads=[uown_r.k(st), xp_r.k(st), xp_r.k(st, "im"), wm.k(), wq.k()], writes=[bkk])
            out_ap = gl_tm8[:, st, :, 4 * q8:4 * q8 + 4, :].rearrange("p t g c -> p g t c")
            gelu_bank(bk, bkk, 128, out_ap, gl_r.k(st))
    for q8 in range(4):
        bk, bkk = PS(q8 % 2)

        def f(e, bk=bk, q8=q8):
            last = None
            for j in range(8):
                g = 8 * q8 + j
                gq = g % 16
                o_ = bk[0:16, j * 64:(j + 1) * 64]
                e.matmul(o_, lhsT=U_s[:, 1, g, :], rhs=W_M[:, g, 0:64], start=True, stop=False)
                e.matmul(o_, lhsT=h0b[:, gq, 0, :], rhs=W_Q[:, g, 0, 0:64], start=False, stop=False)
                last = e.matmul(o_, lhsT=h0b[:, gq, 1, :], rhs=W_Q[:, g, 1, 0:64], start=False, stop=True)
            return last
        P.pe(f, reads=[us_r.k(), h0_r.k("b"), wm.k(), wq.k()], writes=[bkk])
        out_ap = gl_s[0:16, :, 8 * q8:8 * q8 + 8, :].rearrange("p t g c -> p g t c")
        gelu_bank(bk, bkk, 16, out_ap, gl_r.k("s"))
    for _r in (sm, wq, wm, xp_r, uown_r, us_r, h0_r, ss_r, yw):
        AR.free(_r)
    if debug:
        dbg["gl"] = (gl_r, [128, 4096 + 1024], BF16)

    x1_r = AR.alloc("x1", 17 * 1024)
    x1 = x1_r.f32().rearrange("p (t d) -> p t d", t=17)
    wgl_r = AR.alloc("wglu", 4 * 512 // 2)
    W_GLU = wgl_r.bf().rearrange("p (k c) -> p k c", k=4)
    wo_r = AR.alloc("wout", 8 * 1024 // 2)
    W_OUT = wo_r.bf().rearrange("p (k c) -> p k c", k=8)
    wst2 = AR.alloc("wstage2", 2 * 1024)
    for kt in range(4):
        stg = wst2.f32(1024 * (kt % 2), 512)
        P.dma(stg, wglu_d[kt * 128:(kt + 1) * 128, :], writes=[wst2.k(kt % 2)])
        evac(W_GLU[:, kt, :], stg, [wst2.k(kt % 2)], [wgl_r.k()])
    for kt in range(8):
        stg = wst2.f32(1024 * (kt % 2), 1024)
        P.dma(stg, wout_d[kt * 128:(kt + 1) * 128, :], writes=[wst2.k(kt % 2)])
        evac(W_OUT[:, kt, :], stg, [wst2.k(kt % 2)], [wo_r.k()])
    AR.free(wst2)
    cw = AR.alloc("cwork", 2048 + 2048 + 512 + 512 + 256 + 1024 + 64 + 2048 + 1024 + 16 + 1024 + 16 + 512)
    glT = cw.bf(0, 2048).rearrange("p (k n) -> p k n", k=4)
    oT = cw.f32(2048, 2048).rearrange("p (k n) -> p k n", k=4)
    ctmp = cw.f32(4096, 512)
    rstd_t = cw.f32(4608, 512)
    osq = cw.bf(5120, 256)
    mTs = cw.bf(5376, 1024).rearrange("p (k n) -> p k n", k=4)
    ones_b = cw.bf(6400, 64)
    xst = cw.f32(6464, 2048)
    osq4 = cw.bf(8512, 1024).rearrange("p (k n) -> p k n", k=4)
    rtok = cw.f32(9536, 16)
    mTs_l = [mTs, cw.bf(9552, 1024).rearrange("p (k n) -> p k n", k=4)]
    rtok_l = [rtok, cw.f32(10576, 16)]
    ctmp_l = [ctmp, cw.f32(10592, 512)]
    P.pool(lambda e: e.memset(ones_b, 1.0), writes=[cw.k("ones")])
    P.dve(lambda e: e.tensor_scalar(out=gsb[:, 4:8], in0=gsb[:, 4:8], scalar1=0.5, scalar2=None, op0=ALU.mult), reads=[gains.k("bg")], writes=[gains.k("bg")])

    def ssm_tail(ntok, tok0, c0, bi=0):
        mTs = mTs_l[bi]
        rtok = rtok_l[bi]
        for ko in range(4):
            bk, bkk = PS(2 + ko % 2)
            ct = ctmp_l[ko % 2]
            kct = cw.k("ctmp", ko % 2)

            def f(e, bk=bk, ko=ko):
                last = None
                for kt in range(4):
                    last = e.matmul(bk[:, 0:ntok], lhsT=W_GLU[:, kt, ko * 128:(ko + 1) * 128], rhs=glT[:, kt, c0:c0 + ntok], start=(kt == 0), stop=(kt == 3))
                return last
            P.pe(f, reads=[wgl_r.k(), cw.k("glT")], writes=[bkk])
            P.act(lambda e, bk=bk, ko=ko, ct=ct: e.activation(out=ct[:, 0:ntok], in_=bk[:, 0:ntok], func=AF.Tanh, scale=0.5, bias=gsb[:, 4 + ko:5 + ko]),
                  reads=[bkk, gains.k("bg")], writes=[kct])
            P.dve(lambda e, ct=ct: e.tensor_scalar(out=ct[:, 0:ntok], in0=ct[:, 0:ntok], scalar1=0.5, scalar2=0.5, op0=ALU.mult, op1=ALU.add),
                  reads=[kct], writes=[kct])
            tt("dve", oT[:, ko, 0:ntok], ct[:, 0:ntok], glT[:, ko, c0:c0 + ntok], ALU.mult, [kct, cw.k("glT")], [cw.k("oT", ko)])
        for ko in range(4):
            P.act(lambda e, ko=ko: e.activation(out=osq4[:, ko, 0:ntok], in_=oT[:, ko, 0:ntok], func=AF.Square), reads=[cw.k("oT", ko)], writes=[cw.k("osq", ko)])
            P.dve(lambda e, ko=ko: e.tensor_scalar(out=mTs[:, ko, 0:ntok], in0=oT[:, ko, 0:ntok], scalar1=gsb[:, ko:ko + 1], scalar2=None, op0=ALU.mult),
                  reads=[cw.k("oT", ko), gains.k("gs")], writes=[cw.k("mTs", bi)])
        bs, bsk = PS(4)
        ntl = (ntok + 127) // 128

        def frs(e, bs=bs):
            last = None
            for tl in range(ntl):
                nr = min(128, ntok - 128 * tl)
                for ko in range(4):
                    last = e.matmul(bs[0:nr, tl:tl + 1], lhsT=osq4[:, ko, 128 * tl:128 * tl + nr], rhs=ones_b[:, 0:1], start=(ko == 0), stop=(ko == 3))
            return last
        P.pe(frs, reads=[cw.k("osq", k_) for k_ in range(4)] + [cw.k("ones")], writes=[bsk])
        nr0 = min(128, ntok)
        P.dve(lambda e: e.tensor_scalar(out=rtok[0:nr0, 0:ntl], in0=bs[0:nr0, 0:ntl], scalar1=1.0 / 512, scalar2=EPS, op0=ALU.mult, op1=ALU.add),
              reads=[bsk], writes=[cw.k("rtok", bi)])
        P.pool(lambda e: e.tensor_tensor(out=rtok[0:nr0, 0:ntl], in0=rtok[0:nr0, 0:ntl], in1=mhalf[0:nr0].to_broadcast([nr0, ntl]), op=ALU.pow),
               reads=[cw.k("rtok", bi), pers.k("mh")], writes=[cw.k("rtok", bi)])

    def out_proj_tile(nrows, tok0, mcol0, tile_idx, bi=0):
        mTs = mTs_l[bi]
        rtok = rtok_l[bi]
        pr = slice(0, nrows)
        xs_ = xst[:, 1024 * (tile_idx % 2):1024 * (tile_idx % 2) + 1024]
        tl = mcol0 // 128
        P.dma(xs_[pr], xo[tok0:tok0 + nrows, :], writes=[cw.k("xst", tile_idx % 2)])
        for hf in range(2):
            bka, bkak = PS(5 + hf)
            bks, bksk = PS(7 if hf == 0 else 1)

            def f(e, bka=bka, bks=bks, hf=hf):
                last = None
                for kt in range(4):
                    e.matmul(bka[pr, :], lhsT=mT_att[:, kt, tok0:tok0 + nrows], rhs=W_OUT[:, kt, 512 * hf:512 * hf + 512], start=(kt == 0), stop=(kt == 3))
                for kt in range(4):
                    last = e.matmul(bks[pr, :], lhsT=mTs[:, kt, mcol0:mcol0 + nrows], rhs=W_OUT[:, 4 + kt, 512 * hf:512 * hf + 512], start=(kt == 0), stop=(kt == 3))
                return last
            P.pe(f, reads=[mT_r.k(), cw.k("mTs", bi), wo_r.k()], writes=[bkak, bksk])
            xo_ = x1[pr, tile_idx, 512 * hf:512 * hf + 512]
            P.dve(lambda e, bks=bks, hf=hf, xo_=xo_: e.scalar_tensor_tensor(out=xo_, in0=bks[pr, :], scalar=rtok[pr, tl:tl + 1], in1=xs_[pr, 512 * hf:512 * hf + 512],
                                                                      op0=ALU.mult, op1=ALU.add),
                  reads=[bksk, cw.k("rtok", bi), cw.k("xst", tile_idx % 2)], writes=[x1_r.k(tile_idx)])
            tt("dve", xo_, bka[pr, :], xo_, ALU.add, [bkak, x1_r.k(tile_idx)], [x1_r.k(tile_idx)])

    for st in range(2):
        for kt in range(4):
            for a in range(2):
                bk, bkk = PS(a)

                def f(e, bk=bk, kt=kt, a=a, st=st):
                    last = None
                    for j in range(4):
                        t = 4 * a + j
                        last = e.matmul(bk[:, j * 128:(j + 1) * 128], lhsT=gl_tm8[:, st, t, 8 * kt:8 * kt + 8, :].rearrange("p g c -> p (g c)"), rhs=ident_b,
                                        start=True, stop=True)
                    return last
                P.pe(f, reads=[gl_r.k(st), pers.k("idb")], writes=[bkk])
                evac(glT[:, kt, :].rearrange("p (k t) -> p t k", t=8)[:, 4 * a:4 * a + 4, :], bk.rearrange("p (t k) -> p t k", t=4), [bkk], [cw.k("glT")])
        for hf in range(2):
            gi_ = 2 * st + hf
            ssm_tail(512, st * ST + hf * 512, hf * 512, gi_ % 2)
            if gi_ > 0:
                pst, phf = (gi_ - 1) // 2, (gi_ - 1) % 2
                for tl in range(4):
                    tok0 = pst * ST + phf * 512 + tl * 128
                    out_proj_tile(128, tok0, tl * 128, tok0 // 128, (gi_ - 1) % 2)
    for tl in range(4):
        tok0 = 1 * ST + 512 + tl * 128
        out_proj_tile(128, tok0, tl * 128, tok0 // 128, 1)
    bk, bkk = PS(0)

    def fsT(e, bk=bk):
        last = None
        for kt in range(4):
            for t in range(4):
                c = (kt * 4 + t) * 16
                last = e.matmul(bk[:, c:c + 16], lhsT=gl_s[0:16, t, 8 * kt:8 * kt + 8, :].rearrange("p g c -> p (g c)"), rhs=ident_b[0:16, 0:16], start=True, stop=True)
        return last
    P.pe(fsT, reads=[gl_r.k("s"), pers.k("idb")], writes=[bkk])
    evac(glT[:, :, 0:64].rearrange("p k (b t) -> p k t b", t=4), bk[:, 0:256].rearrange("p (k t b) -> p k t b", k=4, t=4), [bkk], [cw.k("glT")])
    ssm_tail(64, NP, 0, 0)
    out_proj_tile(64, NP, 0, 16, 0)
    AR.free(cw)
    AR.free(gl_r)
    AR.free(mT_r)
    AR.free(wgl_r)
    AR.free(wo_r)
    if debug:
        dbg["x1"] = (x1_r, [128, 17 * 1024], F32)

    AR.free(gains)
    g2 = AR.alloc("gains2", 2048)
    g_mlp = g2.f32(0, 1024)
    g_fin = g2.f32(1024, 1024)
    P.dma(g_mlp, bc_row(gmlp_d, 1024), writes=[g2.k("mlp")])
    P.dma(g_fin, bc_row(gfin_d, 1024), writes=[g2.k("fin")])
    hm_r = AR.alloc("hmT", 8 * NT // 2)
    hmT = hm_r.bf().rearrange("p (k n) -> p k n", k=8)
    dw = AR.alloc("dwork", 512 + 256)
    hmb = dw.bf(0, 512)
    hmb2_r = AR.alloc("hmb2", 512)
    dst_ = dw.f32(512, 256)

    def rms_stats(tile_idx, nrows, junk_bf, slot):
        pr = slice(0, nrows)
        sc = dst_[:, 4 * (slot % 64):4 * (slot % 64) + 4]
        ks = dw.k("st", slot % 64)
        P.act(lambda e: e.activation(out=junk_bf[pr], in_=x1[pr, tile_idx, :], func=AF.Square, accum_out=sc[pr, 0:1]),
              reads=[x1_r.k(tile_idx)], writes=[dw.k("hmb"), ks])
        P.dve(lambda e: e.tensor_scalar(out=sc[pr, 1:2], in0=sc[pr, 0:1], scalar1=1.0 / 1024, scalar2=EPS, op0=ALU.mult, op1=ALU.add), reads=[ks], writes=[ks + ("a",)])
        P.pool(lambda e: e.tensor_tensor(out=sc[pr, 2:3], in0=sc[pr, 1:2], in1=mhalf[pr], op=ALU.pow), reads=[ks + ("a",), pers.k("mh")], writes=[ks + ("b",)])
        return sc, ks + ("b",)

    hmbs = [dw.bf(0, 512), hmb2_r.bf(0, 512)]

    def hm_front(ti):
        nrows = 128 if ti < 16 else 64
        pr = slice(0, nrows)
        hb_ = hmbs[ti % 2]
        kh_ = ("hmbk", ti % 2)
        sc = dst_[:, 4 * (ti % 64):4 * (ti % 64) + 4]
        ks = dw.k("st", ti % 64)
        P.act(lambda e: e.activation(out=hb_[pr], in_=x1[pr, ti, :], func=AF.Square, accum_out=sc[pr, 0:1]),
              reads=[x1_r.k(ti)], writes=[kh_, ks])
        P.dve(lambda e: e.tensor_scalar(out=sc[pr, 1:2], in0=sc[pr, 0:1], scalar1=1.0 / 1024, scalar2=EPS, op0=ALU.mult, op1=ALU.add), reads=[ks], writes=[ks + ("a",)])
        P.pool(lambda e: e.tensor_tensor(out=sc[pr, 2:3], in0=sc[pr, 1:2], in1=mhalf[pr], op=ALU.pow), reads=[ks + ("a",), pers.k("mh")], writes=[ks + ("b",)])
        P.dve(lambda e: e.scalar_tensor_tensor(out=hb_[pr], in0=x1[pr, ti, :], scalar=sc[pr, 2:3], in1=g_mlp[pr], op0=ALU.mult, op1=ALU.mult),
              reads=[x1_r.k(ti), ks + ("b",), g2.k("mlp")], writes=[kh_])

    def hm_back(ti):
        nrows = 128 if ti < 16 else 64
        pr = slice(0, nrows)
        hb_ = hmbs[ti % 2]
        kh_ = ("hmbk", ti % 2)
        for half in range(2):
            bk, bkk = PS(half)

            def f(e, bk=bk, half=half):
                last = None
                for j in range(4):
                    kt = 4 * half + j
                    last = e.matmul(bk[:, j * 128:j * 128 + nrows], lhsT=hb_[pr, kt * 128:(kt + 1) * 128], rhs=ident_b[pr, pr], start=True, stop=True)
                return last
            P.pe(f, reads=[kh_, pers.k("idb")], writes=[bkk])
            evac(hmT[:, 4 * half:4 * half + 4, 128 * ti:128 * ti + nrows], bk.rearrange("p (j n) -> p j n", j=4)[:, :, 0:nrows], [bkk], [hm_r.k()])

    NCH = 8
    FT = 4
    wup_rs = [AR.alloc("wup%d" % i, 8 * 512 // 2) for i in range(2)]
    wdn_rs = [AR.alloc("wdn%d" % i, FT * 1024 // 2) for i in range(2)]
    wst3 = AR.alloc("wstage3", 3 * 1024)
    hT_rs = [AR.alloc("hTa", FT * 512 // 2), AR.alloc("hTb", FT * 512 // 2)]
    rl_r = AR.alloc("relu", 2 * 512)
    W_UP = [wup_rs[i].bf().rearrange("p (k c) -> p k c", k=8) for i in range(2)]
    W_DN = [wdn_rs[i].bf().rearrange("p (k c) -> p k c", k=FT) for i in range(2)]
    hT = [hT_rs[i].bf().rearrange("p (k n) -> p k n", k=FT) for i in range(2)]

    class _WK:
        def __init__(self, rs):
            self.rs = rs

        def k(self, b):
            return self.rs[b].k()
    wup_r = _WK(wup_rs)
    wdn_r = _WK(wdn_rs)
    sctr = {"n": 0}

    NSTG = 3

    def chunk_jobs(c):
        b = c % 2
        jobs = []
        for kt in range(8):
            st_ = {}

            def d(kt=kt, st_=st_):
                i = sctr["n"] % NSTG
                sctr["n"] += 1
                st_["i"] = i
                P.dma(wst3.f32(1024 * i, 512), wup_d[kt * 128:(kt + 1) * 128, 512 * c:512 * c + 512], writes=[wst3.k(i)])

            def cst(kt=kt, st_=st_):
                i = st_["i"]
                evac(W_UP[b][:, kt, :], wst3.f32(1024 * i, 512), [wst3.k(i)], [wup_r.k(b)], eng="act")
            jobs.append((d, cst))
        for ft in range(FT):
            st_ = {}

            def d(ft=ft, st_=st_):
                i = sctr["n"] % NSTG
                sctr["n"] += 1
                st_["i"] = i
                r0 = 512 * c + 128 * ft
                P.dma(wst3.f32(1024 * i, 1024), wdn_d[r0:r0 + 128, :], writes=[wst3.k(i)])

            def cst(ft=ft, st_=st_):
                i = st_["i"]
                evac(W_DN[b][:, ft, :], wst3.f32(1024 * i, 1024), [wst3.k(i)], [wdn_r.k(b)], eng="act")
            jobs.append((d, cst))
        return jobs

    class JobRunner:
        LAG = 2

        def __init__(self):
            self.q = []
            self.nd = 0
            self.nc_ = 0

        def add(self, jobs):
            self.q += jobs

        def step(self):
            while self.nd < len(self.q) and self.nd < self.nc_ + self.LAG:
                self.q[self.nd][0]()
                self.nd += 1
            if self.nc_ < self.nd:
                self.q[self.nc_][1]()
                self.nc_ += 1

        def pending(self):
            return self.nc_ < len(self.q)

    groups = [(512 * i, 512) for i in range(4)] + [(NP, 64)]
    ys_r = AR.alloc("yst", 2048)
    ysts = [ys_r.f32(0, 1024), ys_r.f32(1024, 1024)]
    junk2_r = AR.alloc("junk2", 512)

    def final_tiles(tis):
        pre = rms_stats(tis[0], 128 if tis[0] < 16 else 64, junk2_r.bf(), 32 + tis[0])
        for n_, ti in enumerate(tis):
            nrows = 128 if ti < 16 else 64
            pr = slice(0, nrows)
            sc, kb_ = pre
            if n_ + 1 < len(tis):
                t2 = tis[n_ + 1]
                pre = rms_stats(t2, 128 if t2 < 16 else 64, junk2_r.bf(), 32 + t2)
            yst = ysts[ti % 2]
            P.dve(lambda e, ti=ti, sc=sc, pr=pr, yst=yst: e.scalar_tensor_tensor(out=yst[pr], in0=x1[pr, ti, :], scalar=sc[pr, 2:3], in1=g_fin[pr], op0=ALU.mult, op1=ALU.mult),
                  reads=[x1_r.k(ti), kb_, g2.k("fin")], writes=[ys_r.k(ti % 2)])
            P.dma(y_o[128 * ti:128 * ti + nrows, :], yst[pr], reads=[ys_r.k(ti % 2)])
    JR = JobRunner()
    JR.add(chunk_jobs(0))
    JR.step()
    hm_front(0)
    for ti in range(17):
        if ti + 1 < 17:
            hm_front(ti + 1)
        hm_back(ti)
        JR.step()
    while JR.pending():
        JR.step()
    gi = 0
    for c in range(NCH):
        if c + 1 < NCH:
            JR.add(chunk_jobs(c + 1))
        b = c % 2
        for (tok0, ntok) in groups:
            hb = gi % 2
            gi += 1
            for ft in range(FT):
                bk, bkk = PS(ft % 2)

                def f(e, bk=bk, ft=ft, b=b, tok0=tok0, ntok=ntok):
                    last = None
                    for kt in range(8):
                        last = e.matmul(bk[:, 0:ntok], lhsT=W_UP[b][:, kt, ft * 128:(ft + 1) * 128], rhs=hmT[:, kt, tok0:tok0 + ntok], start=(kt == 0), stop=(kt == 7))
                    return last
                P.pe(f, reads=[wup_r.k(b), hm_r.k()], writes=[bkk])
                rl = rl_r.f32(512 * (ft % 2), 512)
                P.act(lambda e, bk=bk, rl=rl, ntok=ntok: e.activation(out=rl[:, 0:ntok], in_=bk[:, 0:ntok], func=AF.Relu), reads=[bkk], writes=[rl_r.k(ft % 2)])
                tt("dve", hT[hb][:, ft, 0:ntok], rl[:, 0:ntok], rl[:, 0:ntok], ALU.mult, [rl_r.k(ft % 2)], [hT_rs[hb].k()])
                JR.step()
            ntile = (ntok + 127) // 128
            for tl in range(ntile):
                nrows = min(128, ntok - 128 * tl)
                pr = slice(0, nrows)
                ti = (tok0 + 128 * tl) // 128
                for hf in range(2):
                    bk, bkk = PS(2 + 2 * (tl % 2) + hf)

                    def f(e, bk=bk, hf=hf, hb=hb, b=b, tl=tl, nrows=nrows, pr=pr):
                        last = None
                        for ft in range(FT):
                            last = e.matmul(bk[pr, :], lhsT=hT[hb][:, ft, 128 * tl:128 * tl + nrows], rhs=W_DN[b][:, ft, 512 * hf:512 * hf + 512], start=(ft == 0), stop=(ft == FT - 1))
                        return last
                    P.pe(f, reads=[hT_rs[hb].k(), wdn_r.k(b)], writes=[bkk])
                    tt("dve", x1[pr, ti, 512 * hf:512 * hf + 512], bk[pr, :], x1[pr, ti, 512 * hf:512 * hf + 512], ALU.add, [bkk, x1_r.k(ti)], [x1_r.k(ti)])
            if c == NCH - 1:
                final_tiles([(tok0 + 128 * tl_) // 128 for tl_ in range(ntile)])

    if debug:
        import os
        want = os.environ.get("DBG", "").split(",")
        keep = set(v[0].name for k, v in dbg.items() if k in want) | {"pers"}
        for nm in list(AR.live.keys()):
            if nm not in keep:
                o_, n_ = AR.live[nm]
                AR.free(Region(AR, nm, o_, n_))
        for name, (reg, shape, dt) in dbg.items():
            if name not in want or reg.name not in AR.live:
                continue
            o = nc.dram_tensor("dbg_" + name, list(shape), F32, kind="ExternalOutput").ap()
            if dt == BF16:
                tmp = AR.alloc("dbgtmp_" + name, shape[1])
                P.dve(lambda e, tmp=tmp, reg=reg: e.tensor_copy(out=tmp.f32(), in_=reg.bf()), reads=[reg.k()], writes=[tmp.k()])
                P.dma(o, tmp.f32(), reads=[tmp.k()])
                AR.free(tmp)
            else:
                P.dma(o, reg.f32(0, shape[1]), reads=[reg.k()])
    P.emit()
    return nc


def _bucket(n):
    n = np.maximum(n, 0)
    nf = np.maximum(n, 16).astype(np.float32)
    large = 16 + (np.log(nf / np.float32(16)) / np.float32(math.log(8.0)) * np.float32(16)).astype(np.int32)
    large = np.minimum(large, 31)
    return np.where(n < 16, n, large)


def _static_consts():
    c = np.zeros((128, 160), np.float32)
    c[:, 0:9] = np.arange(9)
    c[:, 16:145] = np.arange(129)
    c[:, 146] = 7 - (np.arange(128) // 16)
    oh = np.zeros((33, 384), np.float32)
    for j in range(384):
        d = j - 127
        if 0 <= d <= 127:
            oh[int(_bucket(np.array(d))), j] = 1.0
        else:
            oh[32, j] = NEG
    mb = np.full((64, 64), NEG, np.float32)
    jm = np.zeros((128, 192), np.float32)
    jm[np.arange(128), 127 - np.arange(128)] = 1.0
    for b in range(16):
        mb[4 * b:4 * b + 4, 4 * b:4 * b + 4] = 0.0
        for t_ in range(4):
            jm[3 - t_, 128 + 4 * b + t_] = 1.0
    return c, oh, mb, jm


def make_core_inputs(inputs, c):
    f = lambda a: np.ascontiguousarray(np.asarray(a), dtype=np.float32)
    seq, m = c // 4, c % 4
    xpr = np.asarray(inputs["x_prompt"])
    xsm = np.asarray(inputs["x_sample"])
    cst, oh, mb, jm = _static_consts()
    d = {}
    d["xo"] = np.concatenate([xpr[seq, 2048 * m:2048 * (m + 1)], xsm[16 * c:16 * c + 16].reshape(64, 1024)], 0)
    xp = np.zeros((NLITE * ST, D), np.float32)
    if m > 0:
        xp[NLITE * ST - 2048 * m:] = xpr[seq, 0:2048 * m]
    d["xp"] = xp
    d["cache_k"] = np.asarray(inputs["cache_k"])[0, 16 * c:16 * c + 16].reshape(16, 128, 128)
    d["cache_v"] = np.asarray(inputs["cache_v"])[0, 16 * c:16 * c + 16].reshape(16, 128, 128)
    d["st_re"] = np.asarray(inputs["state_ssm_re"])[0, 16 * c:16 * c + 16].reshape(16, 2048)
    d["st_im"] = np.asarray(inputs["state_ssm_im"])[0, 16 * c:16 * c + 16].reshape(16, 2048)
    d["rel_bias"] = inputs["rel_bias"]
    d["norm_mix"] = inputs["norm_mix"]
    d["w_in"] = np.asarray(inputs["w_in"])[0]
    d["sinks"] = inputs["attn_sinks"]
    d["a_re"] = np.asarray(inputs["ssm_a_re"])[0]
    d["a_im"] = np.asarray(inputs["ssm_a_im"])[0]
    d["log_step"] = inputs["ssm_log_step"]
    d["b_re"] = np.asarray(inputs["ssm_b_re"])[0]
    d["b_im"] = np.asarray(inputs["ssm_b_im"])[0]
    d["c_re"] = np.asarray(inputs["ssm_c_re"])[0]
    d["c_im"] = np.asarray(inputs["ssm_c_im"])[0]
    d["ssm_d"] = np.asarray(inputs["ssm_d"])[0]
    d["w_glu"] = np.asarray(inputs["w_glu"])[0]
    d["b_glu"] = inputs["b_glu"]
    d["norm_attn"] = inputs["norm_attn_out"]
    d["norm_ssm"] = inputs["norm_ssm_out"]
    d["w_out"] = np.asarray(inputs["w_out"])[0]
    d["norm_mlp"] = inputs["norm_mlp"]
    d["w_up"] = np.asarray(inputs["w_up"])[0]
    d["w_down"] = np.asarray(inputs["w_down"])[0]
    d["norm_final"] = np.asarray(inputs["norm_final"]).reshape(1, D)
    d["consts"] = cst
    d["onehot"] = oh
    d["maskB"] = mb
    d["jmat"] = jm
    d["halo_mask"] = np.full((128, 1), NEG if m == 0 else 0.0, np.float32)
    return {k: f(v) for k, v in d.items()}


_NC_CACHE = {}


def kernel(**inputs):
    if "nc" not in _NC_CACHE:
        _NC_CACHE["nc"] = build_program(debug=False)
    nc = _NC_CACHE["nc"]
    in_maps = [make_core_inputs(inputs, c) for c in range(NCORES)]
    res = run_bass_kernel_spmd(nc, in_maps, core_ids=list(range(NCORES)))
    R = res.results
    y_prompt = np.zeros((2, 8192, D), np.float32)
    y_sample = np.zeros((128, 4, D), np.float32)
    nkp = np.zeros((1, 2, 128, 2, 64), np.float32)
    nvp = np.zeros((1, 2, 128, 2, 64), np.float32)
    srp = np.zeros((1, 2, 32, 64), np.float32)
    sip = np.zeros((1, 2, 32, 64), np.float32)
    nks = np.zeros((1, 128, 128, 2, 64), np.float32)
    nvs = np.zeros((1, 128, 128, 2, 64), np.float32)
    srs = np.zeros((1, 128, 32, 64), np.float32)
    sis = np.zeros((1, 128, 32, 64), np.float32)
    for c in range(NCORES):
        seq, m = c // 4, c % 4
        r = R[c]
        y = np.asarray(r["y"])
        y_prompt[seq, 2048 * m:2048 * (m + 1)] = y[0:2048]
        y_sample[16 * c:16 * c + 16] = y[2048:].reshape(16, 4, D)
        if m == 3:
            nkp[0, seq] = np.asarray(r["nk_p"]).reshape(128, 2, 64)
            nvp[0, seq] = np.asarray(r["nv_p"]).reshape(128, 2, 64)
            srp[0, seq] = np.asarray(r["sp_re"])
            sip[0, seq] = np.asarray(r["sp_im"])
        nks[0, 16 * c:16 * c + 16] = np.asarray(r["nk_s"]).reshape(16, 128, 2, 64)
        nvs[0, 16 * c:16 * c + 16] = np.asarray(r["nv_s"]).reshape(16, 128, 2, 64)
        srs[0, 16 * c:16 * c + 16] = np.asarray(r["ss_re"]).reshape(16, 32, 64)
        sis[0, 16 * c:16 * c + 16] = np.asarray(r["ss_im"]).reshape(16, 32, 64)
    return (y_prompt, y_sample, nkp, nvp, srp, sip, nks, nvs, srs, sis)
```

```python
import math
import numpy as np
import concourse.bass as bass
import concourse.mybir as mybir
from concourse.bass_utils import run_bass_kernel_spmd

F32 = mybir.dt.float32
BF16 = mybir.dt.bfloat16
I32 = mybir.dt.int32
ALU = mybir.AluOpType
AF = mybir.ActivationFunctionType

NCORES = 8
D = 1024
NP = 2048
NSQ = 16
NS = 64
NT = NP + NS
NLITE = 6
ST = 1024
NEG = -30000.0
EPS = 1e-6
TWO_PI = 2.0 * math.pi
C1 = 6.28125
C2 = TWO_PI - C1


class _Op:
    __slots__ = ("eng", "fn", "deps", "dma", "key", "sig", "val", "idx", "sem", "where", "rw")

    def __init__(self, eng, fn, dma, key):
        self.eng = eng
        self.fn = fn
        self.dma = dma
        self.key = key
        self.deps = set()
        self.sig = False
        self.val = 0
        self.sem = None


class Prog:
    ENGS = ("pe", "act", "dve", "pool", "sp")

    def __init__(self, nc):
        self.nc = nc
        self.ops = []
        self.last_w = {}
        self.reads = {}
        self.pending = {}
        self.applied = set()

    def _key_deps(self, k, deps):
        rn = k[0] if isinstance(k, tuple) else k
        pend = self.pending.get(rn)
        if pend and k not in self.applied:
            deps |= pend
            self.applied.add(k)

    def op(self, eng, fn, reads=(), writes=(), dma=False, key=None):
        o = _Op(eng, fn, dma, key)
        o.idx = len(self.ops)
        deps = o.deps
        ispsum = lambda k: isinstance(k, tuple) and k[0] == "ps"
        pk = [k[:2] for k in list(reads) + list(writes) if ispsum(k)]
        reads = [k for k in reads if not ispsum(k)]
        writes = [k for k in writes if not ispsum(k)]
        for k in dict.fromkeys(pk):
            j = self.last_w.get(k)
            if j is not None and (self.ops[j].eng != eng or self.ops[j].dma or dma):
                deps.add(j)
            self.last_w[k] = o.idx
        for r in reads:
            self._key_deps(r, deps)
            j = self.last_w.get(r)
            if j is not None:
                deps.add(j)
        for w in writes:
            self._key_deps(w, deps)
            j = self.last_w.get(w)
            if j is not None:
                deps.add(j)
            for j in self.reads.get(w, ()):
                deps.add(j)
        for r in reads:
            self.reads.setdefault(r, []).append(o.idx)
        for w in writes:
            self.last_w[w] = o.idx
            self.reads[w] = []
        if dma and key is None:
            o.key = (writes[0] if writes else reads[0])
        o.where = None
        o.rw = (list(reads), list(writes))
        self.ops.append(o)
        return o

    def users_of_region(self, rn):
        s = set()
        for k, j in self.last_w.items():
            if (k[0] if isinstance(k, tuple) else k) == rn:
                s.add(j)
        for k, js in self.reads.items():
            if (k[0] if isinstance(k, tuple) else k) == rn:
                s.update(js)
        return s

    def pe(self, fn, reads=(), writes=()):
        return self.op("pe", fn, reads, writes)

    def act(self, fn, reads=(), writes=()):
        return self.op("act", fn, reads, writes)

    def dve(self, fn, reads=(), writes=()):
        return self.op("dve", fn, reads, writes)

    def pool(self, fn, reads=(), writes=()):
        return self.op("pool", fn, reads, writes)

    def dma(self, out, in_, reads=(), writes=(), eng="sp", key=None, **kw):
        return self.op(eng, lambda e: e.dma_start(out=out, in_=in_, **kw), reads, writes, dma=True, key=key)

    def emit(self):
        nc = self.nc
        ops = self.ops
        for o in ops:
            if o.eng == "pe" and not o.dma:
                o.deps = {j for j in o.deps if not (ops[j].eng == "pe" and not ops[j].dma)}
        for o in ops:
            for j in o.deps:
                ops[j].sig = True
        for o in ops:
            if o.dma:
                o.sig = True
        eng_cnt = {e: 0 for e in self.ENGS}
        dma_cnt = {}
        dma_keys = []
        for o in ops:
            if not o.sig:
                continue
            if o.dma:
                if o.key not in dma_cnt:
                    dma_cnt[o.key] = 0
                    dma_keys.append(o.key)
                dma_cnt[o.key] += 16
                o.val = dma_cnt[o.key]
            else:
                eng_cnt[o.eng] += 1
                o.val = eng_cnt[o.eng]
        sems = {}
        for e in ("pe", "act", "dve", "pool"):
            sems[("eng", e)] = nc.alloc_semaphore("s_" + e)
        for i, k in enumerate(dma_keys):
            sems[("dma", k)] = nc.alloc_semaphore("d%d" % i)
        self.n_sems = len(sems)
        for o in ops:
            if o.sig:
                o.sem = sems[("dma", o.key)] if o.dma else sems[("eng", o.eng)]
        dma_hist = {}
        for o in ops:
            if o.dma:
                dma_hist.setdefault(o.key, []).append((o.idx, o.val))
        by_eng = {e: [o for o in ops if o.eng == e] for e in self.ENGS}

        def emit_engine(ename, eng):
            waited = {}
            for o in by_eng[ename]:
                need = {}
                for j in o.deps:
                    p = ops[j]
                    if p.dma:
                        v = p.val
                        for (ii, vv) in dma_hist[p.key]:
                            if ii < o.idx and vv > v:
                                v = vv
                        sk = ("dma", p.key)
                    else:
                        v = p.val
                        sk = ("eng", p.eng)
                    if need.get(sk, 0) < v:
                        need[sk] = v
                for sk, v in need.items():
                    if waited.get(sk, 0) >= v:
                        continue
                    eng.wait_ge(sems[sk], v)
                    waited[sk] = v
                ins = o.fn(eng)
                if o.sig:
                    ins.then_inc(o.sem, 16 if o.dma else 1)
            if ename == "sp":
                for k, v in dma_cnt.items():
                    eng.wait_ge(sems[("dma", k)], v)

        with nc.Block() as block:
            @block.tensor
            def _(e):
                emit_engine("pe", e)

            @block.scalar
            def _(e):
                emit_engine("act", e)

            @block.vector
            def _(e):
                emit_engine("dve", e)

            @block.gpsimd
            def _(e):
                emit_engine("pool", e)

            @block.sync
            def _(e):
                emit_engine("sp", e)


class Arena:
    def __init__(self, nc, P, words):
        self.P = P
        self.words = words
        self.t = nc.alloc_sbuf_tensor("arena", [128, words], F32)
        self.A = self.t.ap()
        self.live = {}
        self.dead = []
        self.peak = 0

    def alloc(self, name, n):
        n = (n + 7) // 8 * 8
        spans = sorted(self.live.values())
        pos = 0
        off = None
        for (o, m) in spans:
            if o - pos >= n:
                off = pos
                break
            pos = max(pos, o + m)
        if off is None:
            if self.words - pos >= n:
                off = pos
            else:
                raise RuntimeError("arena OOM for %s (%d words); live=%s" % (name, n, sorted((v, k) for k, v in self.live.items())))
        assert name not in self.live and name not in self.P.pending
        self.live[name] = (off, n)
        self.peak = max(self.peak, off + n)
        pend = set()
        nd = []
        for (o, m, users) in self.dead:
            if o < off + n and off < o + m:
                pend |= users
            nd.append((o, m, users))
        self.P.pending[name] = pend
        return Region(self, name, off, n)

    def free(self, reg):
        off, n = self.live.pop(reg.name)
        self.dead.append((off, n, self.P.users_of_region(reg.name) | self.P.pending.get(reg.name, set())))


class Region:
    def __init__(self, ar, name, off, n):
        self.ar = ar
        self.name = name
        self.off = off
        self.n = n

    def k(self, *sub):
        return (self.name,) + sub if sub else self.name

    def f32(self, lo=0, n=None):
        n = self.n - lo if n is None else n
        return self.ar.A[:, self.off + lo:self.off + lo + n]

    def bf(self, lo=0, n=None):
        n = self.n - lo if n is None else n
        return self.ar.A[:, self.off + lo:self.off + lo + n].bitcast(BF16)

    def i32(self, lo=0, n=None):
        n = self.n - lo if n is None else n
        return self.ar.A[:, self.off + lo:self.off + lo + n].bitcast(I32)


_DBG_P = [None]


def dram_ap(t_ap, offset, pattern):
    return bass.AP(t_ap.tensor, offset, [list(p) for p in pattern])


def build_program(debug=False):
    nc = bass.Bass("TRN2", target_bir_lowering=False)
    P = Prog(nc)
    _DBG_P[0] = P

    def din(name, shape):
        return nc.dram_tensor(name, list(shape), F32, kind="ExternalInput").ap()

    def dout(name, shape):
        return nc.dram_tensor(name, list(shape), F32, kind="ExternalOutput").ap()

    xo = din("xo", [NT, D])
    xp = din("xp", [NLITE * ST, D])
    ck_d = din("cache_k", [NSQ, 128, 128])
    cv_d = din("cache_v", [NSQ, 128, 128])
    sre_d = din("st_re", [NSQ, 2048])
    sim_d = din("st_im", [NSQ, 2048])
    relb_d = din("rel_bias", [32, 8])
    gmix_d = din("norm_mix", [1, D])
    win_d = din("w_in", [D, 1280])
    sink_d = din("sinks", [1, 8])
    are_d = din("a_re", [32, 64])
    aim_d = din("a_im", [32, 64])
    ls_d = din("log_step", [1, 32])
    bre_d = din("b_re", [32, 64, 16])
    bim_d = din("b_im", [32, 64, 16])
    cre_d = din("c_re", [32, 16, 64])
    cim_d = din("c_im", [32, 16, 64])
    dd_d = din("ssm_d", [32, 16])
    wglu_d = din("w_glu", [512, 512])
    bglu_d = din("b_glu", [1, 512])
    gatt_d = din("norm_attn", [1, 512])
    gssm_d = din("norm_ssm", [1, 512])
    wout_d = din("w_out", [D, D])
    gmlp_d = din("norm_mlp", [1, D])
    wup_d = din("w_up", [D, 4096])
    wdn_d = din("w_down", [4096, D])
    gfin_d = din("norm_final", [1, D])
    cst_d = din("consts", [128, 160])
    oh_d = din("onehot", [33, 384])
    hm_d = din("halo_mask", [128, 1])
    mB_d = din("maskB", [64, 64])
    jm_d = din("jmat", [128, 192])

    y_o = dout("y", [NT, D])
    nkp_o = dout("nk_p", [128, 128])
    nvp_o = dout("nv_p", [128, 128])
    spre_o = dout("sp_re", [32, 64])
    spim_o = dout("sp_im", [32, 64])
    nks_o = dout("nk_s", [NSQ, 128, 128])
    nvs_o = dout("nv_s", [NSQ, 128, 128])
    ssre_o = dout("ss_re", [NSQ, 2048])
    ssim_o = dout("ss_im", [NSQ, 2048])
    fext_d = nc.dram_tensor("fext", [8, 384], F32, kind="Internal").ap()
    dbg = {}

    AR = Arena(nc, P, 53000)
    psb = [nc.alloc_psum_tensor("ps%d" % i, [128, 512], F32).ap() for i in range(8)]

    def PS(i):
        return psb[i], ("ps", i)

    pers = AR.alloc("pers", 128 + 64 + 160 + 2048 + 64 + 520 + 64)
    ident_f = pers.f32(0, 128)
    ident_b = pers.bf(128, 64)
    cst = pers.f32(192, 160)
    biasT = pers.f32(352, 2048).rearrange("p (s h q) -> p s h q", s=2, h=8)
    misc = pers.f32(2400, 64)
    biasA = pers.f32(2464, 32).rearrange("p (h i) -> p h i", h=8)
    biasB = pers.f32(2496, 512).rearrange("p (h q) -> p h q", h=8)
    gains = AR.alloc("gains", 1024 + 512 + 24)
    g_mix = gains.f32(0, 1024)
    g_att = gains.f32(1024, 512)
    sinkexp = gains.f32(1536, 8)
    gsb = gains.f32(1544, 8)

    P.pool(lambda e: e.memset(ident_f, 0.0), writes=[pers.k("idf")])
    P.pool(lambda e: e.affine_select(out=ident_f, in_=ident_f, pattern=[[-1, 128]], compare_op=ALU.not_equal,
                                     fill=1.0, base=0, channel_multiplier=1), reads=[pers.k("idf")], writes=[pers.k("idf")])
    P.dve(lambda e: e.tensor_copy(out=ident_b, in_=ident_f), reads=[pers.k("idf")], writes=[pers.k("idb")])
    P.dma(cst, cst_d, writes=[pers.k("cst")])
    P.pool(lambda e: e.memset(misc[:, 0:1], -0.5), writes=[pers.k("mh")])
    P.dma(misc[:, 1:2], hm_d, writes=[pers.k("hm")])

    def bc_row(row_ap, n):
        return dram_ap(row_ap, 0, [[0, 128], [1, n]])

    P.dma(g_mix, bc_row(gmix_d, 1024), writes=[gains.k("mix")])
    P.dma(g_att, bc_row(gatt_d, 512), writes=[gains.k("att")])
    P.dma(sinkexp, bc_row(sink_d, 8), writes=[gains.k("sink")])
    P.act(lambda e: e.activation(out=sinkexp, in_=sinkexp, func=AF.Exp), reads=[gains.k("sink")], writes=[gains.k("sink")])
    P.dma(gsb[:, 0:4], dram_ap(gssm_d, 0, [[1, 128], [128, 4]]), writes=[gains.k("gs")], allow_slow_non_contiguous=True)
    P.dma(gsb[:, 4:8], dram_ap(bglu_d, 0, [[1, 128], [128, 4]]), writes=[gains.k("bg")], allow_slow_non_contiguous=True)
    mhalf = misc[:, 0:1]
    gcol = misc[:, 16:24]
    P.dma(gcol, dram_ap(gmix_d, 0, [[1, 128], [128, 8]]), writes=[pers.k("gcol")], allow_slow_non_contiguous=True)

    def reduce_angle(x_ap, r_ap, n_i32, n_f32, kx, kr, ktmp, add=0.0):
        ki_, kf_ = ktmp
        if add != 0.0:
            P.dve(lambda e: e.tensor_scalar(out=r_ap, in0=x_ap, scalar1=float(add), scalar2=None, op0=ALU.add),
                  reads=[kx], writes=[kr])
            src, ksrc = r_ap, kr
        else:
            src, ksrc = x_ap, kx
        P.dve(lambda e: e.tensor_scalar(out=n_i32, in0=src, scalar1=1.0 / TWO_PI, scalar2=None, op0=ALU.mult),
              reads=[ksrc], writes=[ki_])
        P.dve(lambda e: e.tensor_copy(out=n_f32, in_=n_i32), reads=[ki_], writes=[kf_])
        P.dve(lambda e: e.scalar_tensor_tensor(out=r_ap, in0=n_f32, scalar=-C1, in1=src, op0=ALU.mult, op1=ALU.add),
              reads=[kf_, ksrc], writes=[kr])
        P.dve(lambda e: e.scalar_tensor_tensor(out=r_ap, in0=n_f32, scalar=-C2, in1=r_ap, op0=ALU.mult, op1=ALU.add),
              reads=[kf_, kr], writes=[kr])

    def tt(eng, out, a, b, op, reads, writes):
        P.op(eng, lambda e: e.tensor_tensor(out=out, in0=a, in1=b, op=op), reads, writes)


    sm = AR.alloc("sm", 2816)
    _o = [0]

    def smv(n):
        v = sm.f32(_o[0], n)
        _o[0] += n
        return v
    ARE = smv(16); AIM = smv(16); LS = smv(16); DEL = smv(16); ARs = smv(16); AIs = smv(16)
    LRE = smv(144).rearrange("p (g j) -> p g j", g=16); LIM = smv(144).rearrange("p (g j) -> p g j", g=16)
    CRE = smv(16); CIM = smv(16); T8R = smv(16); R8 = smv(16); R128 = smv(16)
    BRE = smv(256).rearrange("p (g c) -> p g c", g=16); BIM = smv(256).rearrange("p (g c) -> p g c", g=16)
    CTRE = smv(256).rearrange("p (g c) -> p g c", g=16); CTIM = smv(256).rearrange("p (g c) -> p g c", g=16)
    HRE = smv(16); HIM = smv(16)
    t1 = smv(256); t2 = smv(256); t3 = smv(256); t4 = smv(256)
    tI = sm.i32(_o[0], 256); _o[0] += 256
    ksm = lambda s: sm.k(s)

    for gh in range(2):
        P.dma(ARE[64 * gh:64 * gh + 64, :], dram_ap(are_d, 1024 * gh, [[1, 64], [64, 16]]), writes=[ksm("are")], allow_slow_non_contiguous=True)
        P.dma(AIM[64 * gh:64 * gh + 64, :], dram_ap(aim_d, 1024 * gh, [[1, 64], [64, 16]]), writes=[ksm("aim")], allow_slow_non_contiguous=True)
        P.dma(LS[64 * gh:64 * gh + 64, :], dram_ap(ls_d, 16 * gh, [[0, 64], [1, 16]]), writes=[ksm("ls")])
        P.dma(BRE[64 * gh:64 * gh + 64], dram_ap(bre_d, 16 * 1024 * gh, [[16, 64], [1024, 16], [1, 16]]), writes=[ksm("bre")])
        P.dma(BIM[64 * gh:64 * gh + 64], dram_ap(bim_d, 16 * 1024 * gh, [[16, 64], [1024, 16], [1, 16]]), writes=[ksm("bim")])
    wqkv = AR.alloc("wqkv", 8 * 1280 // 2)
    W_QKV = wqkv.bf().rearrange("p (k c) -> p k c", k=8)
    wu = AR.alloc("wu", 8 * 512 // 2)
    W_U = wu.bf().rearrange("p (k c) -> p k c", k=8)
    wst = AR.alloc("wstage", 8 * 1280)
    P.pool(lambda e: e.memset(wqkv.f32(), 0.0), writes=[wqkv.k()])
    _wc = {"n": 0}

    def wcast(dst, src, kt, reads, writes):
        _wc["n"] += 1
        if True:
            P.act(lambda e: e.activation(out=dst, in_=src, func=AF.Copy, scale=gcol[:, kt:kt + 1]), reads=reads + [pers.k("gcol")], writes=writes)
        else:
            P.dve(lambda e: e.tensor_scalar(out=dst, in0=src, scalar1=gcol[:, kt:kt + 1], scalar2=None, op0=ALU.mult), reads=reads + [pers.k("gcol")], writes=writes)

    for kt in range(8):
        stg = wst.f32(1280 * kt, 1280)
        sk_ = wst.k(kt)
        P.dma(stg, win_d[kt * 128:(kt + 1) * 128, :], writes=[sk_])
        for (dst, src) in ((W_QKV[:, kt, 0:768], stg[:, 0:768]),
                           (W_QKV[:, kt, 768:832], stg[:, 512:576]), (W_QKV[:, kt, 960:1024], stg[:, 512:576]),
                           (W_QKV[:, kt, 1024:1088], stg[:, 576:640]), (W_QKV[:, kt, 1216:1280], stg[:, 576:640])):
            wcast(dst, src, kt, [sk_], [wqkv.k()])
        wcast(W_U[:, kt, :], stg[:, 768:1280], kt, [sk_], [wu.k()])
    AR.free(wst)
    P.act(lambda e: e.activation(out=DEL, in_=LS, func=AF.Exp), reads=[ksm("ls")], writes=[ksm("del")])
    tt("dve", ARs, ARE, DEL, ALU.mult, [ksm("are"), ksm("del")], [ksm("ars")])
    tt("dve", AIs, AIM, DEL, ALU.mult, [ksm("aim"), ksm("del")], [ksm("ais")])
    JV = cst[:, 0:9]
    b3 = lambda a: a.unsqueeze(2).to_broadcast([128, 16, 9])
    jb = JV.unsqueeze(1).to_broadcast([128, 16, 9])
    v144 = lambda t: t[:, 0:144].rearrange("p (g j) -> p g j", g=16)
    tt("dve", v144(t1), b3(ARs), jb, ALU.mult, [ksm("ars"), pers.k("cst")], [ksm("t1")])
    P.act(lambda e: e.activation(out=v144(t1), in_=v144(t1), func=AF.Exp), reads=[ksm("t1")], writes=[ksm("t1")])
    tt("dve", v144(t2), b3(AIs), jb, ALU.mult, [ksm("ais"), pers.k("cst")], [ksm("t2")])
    reduce_angle(t2[:, 0:144], t3[:, 0:144], tI[:, 0:144], t4[:, 0:144], ksm("t2"), ksm("t3"), (ksm("tI"), ksm("t4")))
    P.act(lambda e: e.activation(out=t3[:, 0:144], in_=t3[:, 0:144], func=AF.Sin), reads=[ksm("t3")], writes=[ksm("t3")])
    tt("dve", LIM, v144(t1), v144(t3), ALU.mult, [ksm("t1"), ksm("t3")], [ksm("lim")])
    reduce_angle(t2[:, 0:144], t3[:, 0:144], tI[:, 0:144], t4[:, 0:144], ksm("t2"), ksm("t3"), (ksm("tI"), ksm("t4")), add=math.pi / 2)
    P.act(lambda e: e.activation(out=t3[:, 0:144], in_=t3[:, 0:144], func=AF.Sin), reads=[ksm("t3")], writes=[ksm("t3")])
    tt("dve", LRE, v144(t1), v144(t3), ALU.mult, [ksm("t1"), ksm("t3")], [ksm("lre")])
    L1r = LRE[:, :, 1]; L1i = LIM[:, :, 1]
    a16 = lambda t, i=0: t[:, 16 * i:16 * i + 16]
    P.dve(lambda e: e.tensor_scalar(out=a16(t1, 0), in0=L1r, scalar1=-1.0, scalar2=None, op0=ALU.add), reads=[ksm("lre"), ksm("t1")], writes=[ksm("t1")])
    tt("dve", a16(t1, 1), ARE, ARE, ALU.mult, [ksm("are")], [ksm("t1")])
    tt("dve", a16(t1, 2), AIM, AIM, ALU.mult, [ksm("aim")], [ksm("t1")])
    tt("dve", a16(t1, 1), a16(t1, 1), a16(t1, 2), ALU.add, [ksm("t1"), ksm("t1")], [ksm("t1")])
    P.dve(lambda e: e.reciprocal(out=a16(t1, 1), in_=a16(t1, 1)), reads=[ksm("t1")], writes=[ksm("t1")])
    tt("dve", a16(t1, 3), a16(t1, 0), ARE, ALU.mult, [ksm("t1"), ksm("are")], [ksm("t1")])
    tt("dve", a16(t1, 4), L1i, AIM, ALU.mult, [ksm("lim"), ksm("aim")], [ksm("t1")])
    tt("dve", a16(t1, 3), a16(t1, 3), a16(t1, 4), ALU.add, [ksm("t1"), ksm("t1")], [ksm("t1")])
    tt("dve", CRE, a16(t1, 3), a16(t1, 1), ALU.mult, [ksm("t1"), ksm("t1")], [ksm("cre")])
    tt("dve", a16(t1, 5), L1i, ARE, ALU.mult, [ksm("lim"), ksm("are")], [ksm("t1")])
    tt("dve", a16(t1, 6), a16(t1, 0), AIM, ALU.mult, [ksm("t1"), ksm("aim")], [ksm("t1")])
    tt("dve", a16(t1, 5), a16(t1, 5), a16(t1, 6), ALU.subtract, [ksm("t1"), ksm("t1")], [ksm("t1")])
    tt("dve", CIM, a16(t1, 5), a16(t1, 1), ALU.mult, [ksm("t1"), ksm("t1")], [ksm("cim")])
    cb = lambda a: a.unsqueeze(2).to_broadcast([128, 16, 16])
    v256 = lambda t: t.rearrange("p (g c) -> p g c", g=16)
    tt("dve", v256(t2), cb(CRE), BRE, ALU.mult, [ksm("cre"), ksm("bre")], [ksm("t2")])
    tt("dve", v256(t3), cb(CIM), BIM, ALU.mult, [ksm("cim"), ksm("bim")], [ksm("t3")])
    tt("dve", v256(t4), cb(CRE), BIM, ALU.mult, [ksm("cre"), ksm("bim")], [ksm("t4")])
    tt("dve", v256(t1), cb(CIM), BRE, ALU.mult, [ksm("cim"), ksm("bre"), ksm("t1"), ksm("t1"), ksm("t1"), ksm("t1")], [ksm("t1")])
    tt("dve", BRE, v256(t2), v256(t3), ALU.subtract, [ksm("t2"), ksm("t3")], [ksm("bre")])
    tt("dve", BIM, v256(t4), v256(t1), ALU.add, [ksm("t4"), ksm("t1")], [ksm("bim")])
    P.dve(lambda e: e.tensor_scalar(out=a16(t2, 0), in0=AIs, scalar1=8.0, scalar2=None, op0=ALU.mult), reads=[ksm("ais"), ksm("t2")], writes=[ksm("t2")])
    reduce_angle(a16(t2, 0), T8R, tI[:, 0:16], a16(t4, 0), ksm("t2"), ksm("t8r"), (ksm("tI"), ksm("t4")))
    P.act(lambda e: e.activation(out=R8, in_=ARs, func=AF.Exp, scale=8.0), reads=[ksm("ars")], writes=[ksm("r8")])
    P.act(lambda e: e.activation(out=R128, in_=ARs, func=AF.Exp, scale=1024.0), reads=[ksm("ars")], writes=[ksm("r128")])
    P.pool(lambda e: e.memset(HRE, 0.0), writes=[ksm("hre")])
    P.pool(lambda e: e.memset(HIM, 0.0), writes=[ksm("him")])

    tabs = AR.alloc("tabs", 2 * 16 * 129)
    CK = tabs.f32(0, 2064).rearrange("p (g k) -> p g k", g=16)
    SK = tabs.f32(2064, 2064).rearrange("p (g k) -> p g k", g=16)
    tw = AR.alloc("tabwork", 4 * 2064)
    xk = tw.f32(0, 2064); rk = tw.f32(2064, 2064); nki = tw.i32(4128, 2064); nkf = tw.f32(6192, 2064)
    KK = cst[:, 16:145]
    tt("dve", xk.rearrange("p (g k) -> p g k", g=16), T8R.unsqueeze(2).to_broadcast([128, 16, 129]),
       KK.unsqueeze(1).to_broadcast([128, 16, 129]), ALU.mult, [ksm("t8r"), pers.k("cst")], [tw.k("x")])
    reduce_angle(xk, rk, nki, nkf, tw.k("x"), tw.k("r"), (tw.k("ni"), tw.k("nf")))
    P.act(lambda e: e.activation(out=SK.rearrange("p g k -> p (g k)"), in_=rk, func=AF.Sin), reads=[tw.k("r")], writes=[tabs.k("sk")])
    reduce_angle(xk, rk, nki, nkf, tw.k("x"), tw.k("r"), (tw.k("ni"), tw.k("nf")), add=math.pi / 2)
    P.act(lambda e: e.activation(out=CK.rearrange("p g k -> p (g k)"), in_=rk, func=AF.Sin), reads=[tw.k("r")], writes=[tabs.k("ck")])
    AR.free(tw)

    wp = AR.alloc("wp", 32 * 2 * 128 // 2)
    W_P = wp.bf().rearrange("p (g r m) -> p g r m", g=32, r=2)
    P.pool(lambda e: e.memset(wp.f32(), 0.0), writes=[wp.k()])
    s2 = AR.alloc("setup2", 2048 * 6 + 32 + 4096)
    adr = s2.f32(0, 2048); adi = s2.f32(2048, 2048); ltr = s2.f32(4096, 2048); lti = s2.f32(6144, 2048)
    w1 = s2.f32(8192, 2048); w2 = s2.f32(10240, 2048); lsb = s2.f32(12288, 32)
    wI = s2.i32(8192, 2048)
    brep = s2.bf(12320, 2048).rearrange("p (g r m) -> p g r m", g=16, r=2)
    P.dma(adr, dram_ap(are_d, 0, [[0, 128], [1, 2048]]), writes=[s2.k("adr")])
    P.dma(adi, dram_ap(aim_d, 0, [[0, 128], [1, 2048]]), writes=[s2.k("adi")])
    P.dma(lsb, dram_ap(ls_d, 0, [[0, 128], [1, 32]]), writes=[s2.k("lsb")])
    P.act(lambda e: e.activation(out=lsb, in_=lsb, func=AF.Exp), reads=[s2.k("lsb")], writes=[s2.k("lsb")])
    g64 = lambda t: t.rearrange("p (g q) -> p g q", g=32)
    lb = lsb.unsqueeze(2).to_broadcast([128, 32, 64])
    tt("dve", g64(adr), g64(adr), lb, ALU.mult, [s2.k("adr"), s2.k("lsb")], [s2.k("adr")])
    tt("dve", g64(adi), g64(adi), lb, ALU.mult, [s2.k("adi"), s2.k("lsb")], [s2.k("adi")])
    JC = cst[:, 146:147]
    P.act(lambda e: e.activation(out=w1, in_=adr, func=AF.Exp, scale=JC), reads=[s2.k("adr"), pers.k("cst")], writes=[s2.k("w1")])
    P.dve(lambda e: e.tensor_scalar(out=adi, in0=adi, scalar1=JC, scalar2=None, op0=ALU.mult), reads=[s2.k("adi"), pers.k("cst")], writes=[s2.k("adi")])
    reduce_angle(adi, ltr, s2.i32(10240, 2048), lti, s2.k("adi"), s2.k("ltr"), (s2.k("w2"), s2.k("lti")))
    P.act(lambda e: e.activation(out=lti, in_=ltr, func=AF.Sin), reads=[s2.k("ltr")], writes=[s2.k("lti")])
    tt("dve", lti, lti, w1, ALU.mult, [s2.k("lti"), s2.k("w1")], [s2.k("lti")])
    reduce_angle(adi, ltr, s2.i32(10240, 2048), adr, s2.k("adi"), s2.k("ltr"), (s2.k("w2"), s2.k("adr")), add=math.pi / 2)
    P.act(lambda e: e.activation(out=ltr, in_=ltr, func=AF.Sin), reads=[s2.k("ltr")], writes=[s2.k("ltr")])
    tt("dve", ltr, ltr, w1, ALU.mult, [s2.k("ltr"), s2.k("w1")], [s2.k("ltr")])
    sb = lambda a: a.unsqueeze(2).to_broadcast([128, 16, 8, 16])
    P.dve(lambda e: e.tensor_copy(out=brep[:, :, 0, :].rearrange("p g (s c) -> p g s c", s=8), in_=sb(BRE)), reads=[ksm("bre")], writes=[s2.k("brep0")])
    P.dve(lambda e: e.tensor_copy(out=brep[:, :, 1, :].rearrange("p g (s c) -> p g s c", s=8), in_=sb(BIM)), reads=[ksm("bim")], writes=[s2.k("brep1")])
    LTR = g64(ltr); LTI = g64(lti)
    for gh in range(2):
        for gq in range(16):
            bk, bkk = PS(gq // 4)
            o_ = bk[:, (gq % 4) * 128:(gq % 4) * 128 + 128]

            def f(e, o_=o_, gq=gq, gh=gh):
                e.matmul(o_[:, 0:64], lhsT=brep[:, gq, 0, :], rhs=ident_b[:, 64 * gh:64 * gh + 64], start=True, stop=True)
                return e.matmul(o_[:, 64:128], lhsT=brep[:, gq, 1, :], rhs=ident_b[:, 64 * gh:64 * gh + 64], start=True, stop=True)
            P.pe(f, reads=[s2.k("brep0"), s2.k("brep1"), pers.k("idb")], writes=[bkk + (gq % 4,)])
        for q4 in range(4):
            bk, bkk = PS(q4)
            btv = bk.rearrange("p (g r m) -> p g r m", g=4, r=2)
            gs = slice(16 * gh + 4 * q4, 16 * gh + 4 * q4 + 4)
            rd = [bkk + (i,) for i in range(4)]
            wv1 = w1[:, 0:256].rearrange("p (g m) -> p g m", g=4); wv2 = w2[:, 0:256].rearrange("p (g m) -> p g m", g=4)
            kw1 = s2.k("w1"); kw2 = s2.k("w2")
            tt("dve", wv1, LTR[:, gs, :], btv[:, :, 0, :], ALU.mult, [s2.k("ltr")] + rd, [kw1])
            tt("dve", wv2, LTI[:, gs, :], btv[:, :, 1, :], ALU.mult, [s2.k("lti")] + rd, [kw2])
            tt("dve", W_P[:, gs, 0, 64 * gh:64 * gh + 64], wv1, wv2, ALU.subtract, [kw1, kw2], [wp.k()])
            tt("dve", wv1, LTR[:, gs, :], btv[:, :, 1, :], ALU.mult, [s2.k("ltr")] + rd, [kw1])
            tt("dve", wv2, LTI[:, gs, :], btv[:, :, 0, :], ALU.mult, [s2.k("lti")] + rd, [kw2])
            tt("dve", W_P[:, gs, 1, 64 * gh:64 * gh + 64], wv1, wv2, ALU.add, [kw1, kw2] + rd, [wp.k()] + rd)
    AR.free(s2)
    if debug:
        dbg["W_P"] = (wp, [128, 32 * 2 * 128], BF16)
        dbg["tabs"] = (tabs, [128, 2 * 2064], F32)
        dbg["sm"] = (sm, [128, 2816], F32)


    _rr = {"n": 0}

    def evac(out, in_, reads, writes, eng=None):
        if eng is None:
            eng = "act" if (_rr["n"] % 3 != 2) else "dve"
            _rr["n"] += 1
        if eng == "act":
            P.act(lambda e: e.copy(out=out, in_=in_), reads=reads, writes=writes)
        elif eng == "pool":
            P.pool(lambda e: e.tensor_copy(out=out, in_=in_), reads=reads, writes=writes)
        else:
            P.dve(lambda e: e.tensor_copy(out=out, in_=in_), reads=reads, writes=writes)

    NTK = 128 + NT
    bh_r = AR.alloc("biasH", 1024)
    biasH = bh_r.f32().rearrange("p (h q) -> p h q", h=8)
    ab_r = AR.alloc("attbias_tmp", 8 + 384 + 384 + 32 + 64)
    rb = ab_r.f32(0, 8); ohs = ab_r.f32(8, 384); fsb = ab_r.f32(392, 384); TB = ab_r.f32(776, 32).rearrange("p (h i) -> p h i", h=8)
    mBs = ab_r.f32(808, 64)
    P.pool(lambda e: e.memset(rb[0:64, :], 1.0), writes=[ab_r.k("rb")])
    P.dma(rb[0:32, :], relb_d, writes=[ab_r.k("rb")])
    P.pool(lambda e: e.memset(ohs[0:64, :], 0.0), writes=[ab_r.k("oh")])
    P.dma(ohs[0:33, :], oh_d, writes=[ab_r.k("oh")])
    P.dma(mBs[0:64, :], mB_d, writes=[ab_r.k("mb")])
    ab2 = AR.alloc("attbias_bf", 16 + 8 + 256)
    rbh = ab2.bf(0, 4); rbl = ab2.bf(4, 4); rbr = ab2.f32(8, 8); rbf = ab2.f32(16, 8); ohb = ab2.bf(24, 192)
    P.dve(lambda e: e.tensor_copy(out=rbh[0:64], in_=rb[0:64]), reads=[ab_r.k("rb")], writes=[ab2.k("h")])
    P.dve(lambda e: e.tensor_copy(out=rbf[0:64], in_=rbh[0:64]), reads=[ab2.k("h")], writes=[ab2.k("hf")])
    tt("dve", rbr[0:64], rb[0:64], rbf[0:64], ALU.subtract, [ab_r.k("rb"), ab2.k("hf")], [ab2.k("r")])
    P.dve(lambda e: e.tensor_copy(out=rbl[0:64], in_=rbr[0:64]), reads=[ab2.k("r")], writes=[ab2.k("l")])
    P.dve(lambda e: e.tensor_copy(out=ohb[0:64], in_=ohs[0:64]), reads=[ab_r.k("oh")], writes=[ab2.k("o")])
    bk, bkk = PS(7)

    def ffx(e, bk=bk):
        e.matmul(bk[0:8, 0:384], lhsT=rbh[0:64, :], rhs=ohb[0:64, :], start=True, stop=False)
        return e.matmul(bk[0:8, 0:384], lhsT=rbl[0:64, :], rhs=ohb[0:64, :], start=False, stop=True)
    P.pe(ffx, reads=[ab2.k("h"), ab2.k("l"), ab2.k("o")], writes=[bkk])
    evac(fsb[0:8, :], bk[0:8, 0:384], [bkk], [ab_r.k("f")], eng="act")
    P.dma(fext_d, fsb[0:8, :], reads=[ab_r.k("f")], writes=["fext"])
    hk_r = AR.alloc("hankel", 2048 + 192 + 32 + 32)
    HK = hk_r.f32(0, 2048).rearrange("p (s h q) -> p s h q", s=2, h=8)
    JM = hk_r.f32(2048, 192)
    HA = hk_r.f32(2240, 32).rearrange("p (h i) -> p h i", h=8)
    HB = hk_r.f32(2272, 32).rearrange("p (h i) -> p h i", h=8)
    P.dma(JM, jm_d, writes=[hk_r.k("jm")])
    def bias_reads():
        for slot in range(2):
            for h in range(8):
                P.dma(HK[:, slot, h, :], dram_ap(fext_d, h * 384 + (128 if slot == 0 else 0), [[1, 128], [1, 128]]),
                      reads=["fext"], writes=[hk_r.k("hk")])
        P.dma(HA, dram_ap(fext_d, 128, [[1, 128], [384, 8], [1, 4]]), reads=["fext"], writes=[hk_r.k("ha")])
        P.dma(HB[0:4], dram_ap(fext_d, 124, [[1, 4], [384, 8], [1, 4]]), reads=["fext"], writes=[hk_r.k("hb")])

    qT_r = AR.alloc("qT", 4 * NT // 2)
    qT = qT_r.bf().rearrange("p (t n) -> p t n", t=4)
    kT_r = AR.alloc("kT", 4 * NTK // 2)
    kT = kT_r.bf().rearrange("p (t n) -> p t n", t=4)
    va_r = AR.alloc("vaug", 17 * 2 * 72 // 2)
    v_aug = va_r.bf().rearrange("p (b g d) -> p b g d", b=17, g=2)
    vs_r = AR.alloc("vsaug", 2 * 72 // 2)
    vs_aug = vs_r.bf().rearrange("p (g d) -> p g d", g=2)
    uown_r = AR.alloc("Uown", 2 * 32 * 128 // 2)
    U_own = uown_r.bf().rearrange("p (s g k) -> p s g k", s=2, g=32)
    us_r = AR.alloc("Usamp", 2 * 32 * 16 // 2)
    xs_r = AR.alloc("xs", 2 * 1024)
    xn_rs = [AR.alloc("xn_a", 512), AR.alloc("xn_b", 512)]
    hn_r = AR.alloc("hnT", 8 * 1024 // 2)
    hnT = hn_r.bf().rearrange("p (k n) -> p k n", k=8)
    utm_r = AR.alloc("utm8", 32 * 12 * 16 // 2)
    u_tm8 = utm_r.bf(0, 2048).rearrange("p (g s c) -> p g s c", g=32, s=8)
    u_s12 = utm_r.bf().rearrange("p (g s c) -> p g s c", g=32, s=12)
    stat_r = AR.alloc("stats", 256)
    kvf_r = AR.alloc("kvf", 256)

    P.pool(lambda e: e.memset(va_r.bf(), 1.0), writes=[va_r.k()])
    P.pool(lambda e: e.memset(vs_r.bf(), 1.0), writes=[vs_r.k()])
    def bias_stage3():
        HKf = hk_r.f32(0, 2048)
        bTf = biasT.rearrange("p s h q -> p (s h q)")
        for c4 in range(4):
            bk, bkk = PS(4 + c4)
            P.pe(lambda e, bk=bk, c4=c4: e.matmul(bk, lhsT=JM[:, 0:128], rhs=HKf[:, 512 * c4:512 * c4 + 512], start=True, stop=True),
                 reads=[hk_r.k("jm"), hk_r.k("hk")], writes=[bkk])
            evac(bTf[:, 512 * c4:512 * c4 + 512], bk, [bkk], [pers.k("biasT")])
        bk, bkk = PS(7)

        def fja(e, bk=bk):
            e.matmul(bk[:, 0:32], lhsT=JM[:, 0:128], rhs=hk_r.f32(2240, 32), start=True, stop=True)
            return e.matmul(bk[0:64, 32:64], lhsT=JM[0:4, 128:192], rhs=hk_r.f32(2272, 32)[0:4], start=True, stop=True)
        P.pe(fja, reads=[hk_r.k("jm"), hk_r.k("ha"), hk_r.k("hb")], writes=[bkk])
        evac(biasA.rearrange("p h i -> p (h i)"), bk[:, 0:32], [bkk], [pers.k("biasA")], eng="act")
        evac(TB[0:64].rearrange("p h i -> p (h i)"), bk[0:64, 32:64], [bkk], [ab_r.k("tb")], eng="act")
        tt("dve", biasB[0:64].rearrange("p h (b i) -> p h b i", b=16), TB[0:64].unsqueeze(2).to_broadcast([64, 8, 16, 4]),
           mBs[0:64].rearrange("p (b i) -> p b i", b=16).unsqueeze(1).to_broadcast([64, 8, 16, 4]), ALU.add,
           [ab_r.k("tb"), ab_r.k("mb")], [pers.k("biasB")])
        P.dve(lambda e: e.tensor_scalar(out=biasH, in0=biasT[:, 0], scalar1=misc[:, 1:2], scalar2=None, op0=ALU.add),
              reads=[pers.k("biasT"), pers.k("hm")], writes=[bh_r.k()])
        if debug:
            _o1 = nc.dram_tensor("dbg_abr", [128, 872], F32, kind="ExternalOutput").ap()
            P.dma(_o1, ab_r.f32(0, 872), reads=[ab_r.k(x) for x in ("rb", "oh", "f", "tb", "mb")])
            _o2 = nc.dram_tensor("dbg_hkr", [128, 2304], F32, kind="ExternalOutput").ap()
            P.dma(_o2, hk_r.f32(0, 2304), reads=[hk_r.k(x) for x in ("jm", "hk", "ha", "hb")])
        AR.free(ab_r)
        AR.free(hk_r)
        AR.free(ab2)

    tile_ctr = {"n": 0}

    def norm_front(src_rows, nrows, gain, wname):
        i = tile_ctr["n"]
        tile_ctr["n"] += 1
        b = i % (xs_r.n // 1024)
        xs = xs_r.f32(1024 * b, 1024)
        bn = i % len(xn_rs)
        xn = xn_rs[bn].bf(0, 512)
        kx = xs_r.k(b)
        kn = xn_rs[bn].k()
        sc = stat_r.f32(4 * (i % 64), 4)
        ksc = stat_r.k(i % 64)
        pr = slice(0, nrows)
        P.dma(xs[pr], src_rows, writes=[kx])
        P.act(lambda e: e.activation(out=xn[pr], in_=xs[pr], func=AF.Square, accum_out=sc[pr, 0:1]), reads=[kx], writes=[kn, ksc])
        P.dve(lambda e: e.tensor_scalar(out=sc[pr, 1:2], in0=sc[pr, 0:1], scalar1=1.0 / 1024, scalar2=EPS, op0=ALU.mult, op1=ALU.add),
              reads=[ksc], writes=[ksc + ("a",)])
        P.pool(lambda e: e.tensor_tensor(out=sc[pr, 2:3], in0=sc[pr, 1:2], in1=mhalf[pr], op=ALU.pow), reads=[ksc + ("a",), pers.k("mh")], writes=[ksc + ("b",)])
        P.act(lambda e: e.activation(out=xn[pr], in_=xs[pr], func=AF.Copy, scale=sc[pr, 2:3]), reads=[kx, ksc + ("b",)], writes=[kn])
        return (xn, kn, pr, nrows)

    def norm_back(ctx, hn_dst, hn_key, eng=None):
        xn, kn, pr, nrows = ctx
        for half in range(2):
            bk, bkk = PS(half)

            def f(e, bk=bk, half=half):
                last = None
                for j in range(4):
                    kt = 4 * half + j
                    last = e.matmul(bk[:, j * 128:j * 128 + nrows], lhsT=xn[pr, kt * 128:(kt + 1) * 128], rhs=ident_b[pr, pr], start=True, stop=True)
                return last
            P.pe(f, reads=[kn, pers.k("idb")], writes=[bkk])
            evac(hn_dst[:, 4 * half:4 * half + 4, :], bk.rearrange("p (j n) -> p j n", j=4)[:, :, 0:nrows], [bkk], [hn_key], eng=eng)

    def norm_tile(src_rows, nrows, hn_dst, hn_key, gain, wname):
        norm_back(norm_front(src_rows, nrows, gain, wname), hn_dst, hn_key)

    def run_tiles(specs):
        ctxs = [None] * len(specs)
        ctxs[0] = norm_front(specs[0][0], 128, g_mix, "mix")
        for i in range(len(specs)):
            if i + 1 < len(specs):
                ctxs[i + 1] = norm_front(specs[i + 1][0], 128, g_mix, "mix")
            norm_back(ctxs[i], specs[i][1], specs[i][2])
            if specs[i][3] is not None:
                specs[i][3]()

    def proj_fm(hn_src, hn_key, ncols, wtile, dst, dst_key, bank):
        bk, bkk = PS(bank)

        def f(e):
            last = None
            for kt in range(8):
                last = e.matmul(bk[:, 0:ncols], lhsT=wtile(kt), rhs=hn_src[:, kt, :], start=(kt == 0), stop=(kt == 7))
            return last
        P.pe(f, reads=[hn_key, wqkv.k(), wu.k()], writes=[bkk])
        evac(dst, bk[:, 0:ncols], [bkk], [dst_key])

    def proj_tm(hn_src, hn_key, nrows, wsl, ncols, bank):
        bk, bkk = PS(bank)

        def f(e):
            last = None
            for kt in range(8):
                last = e.matmul(bk[0:nrows, 0:ncols], lhsT=hn_src[:, kt, :], rhs=wsl(kt), start=(kt == 0), stop=(kt == 7))
            return last
        P.pe(f, reads=[hn_key, wqkv.k(), wu.k()], writes=[bkk])
        return bk, bkk

    pbank = {"n": 0}

    def nb():
        pbank["n"] += 1
        return 2 + (pbank["n"] % 2)

    def kv_tile(hn_src, hn_key, nrows, blk, last=False, sample=False):
        bk, bkk = proj_tm(hn_src, hn_key, nrows, lambda kt: W_QKV[:, kt, 512:768], 256, 4)
        if sample:
            evac(vs_aug[0:nrows, :, 0:64], bk[0:nrows, 128:256].rearrange("p (g d) -> p g d", g=2), [bkk], [vs_r.k()], eng="dve")
        else:
            evac(v_aug[:, blk, :, 0:64], bk[:, 128:256].rearrange("p (g d) -> p g d", g=2), [bkk], [va_r.k()], eng="dve")
        if last or sample:
            kvf = kvf_r.f32()
            evac(kvf[0:nrows], bk[0:nrows, 0:256], [bkk], [kvf_r.k()], eng="act")
            if sample:
                for b_ in range(NSQ):
                    P.dma(nks_o[b_, 124:128, :], kvf[4 * b_:4 * b_ + 4, 0:128], reads=[kvf_r.k()])
                    P.dma(nvs_o[b_, 124:128, :], kvf[4 * b_:4 * b_ + 4, 128:256], reads=[kvf_r.k()])
            else:
                P.dma(nkp_o, kvf[:, 0:128], reads=[kvf_r.k()])
                P.dma(nvp_o, kvf[:, 128:256], reads=[kvf_r.k()])

    def k_cols(hn_src, hn_key, ncols, tok0):
        for t in range(4):
            proj_fm(hn_src, hn_key, ncols, lambda kt, t=t: W_QKV[:, kt, 768 + 128 * t:896 + 128 * t],
                    kT[:, t, tok0:tok0 + ncols], kT_r.k(), nb())

    def q_cols(hn_src, hn_key, ncols, tok0):
        for t in range(4):
            proj_fm(hn_src, hn_key, ncols, lambda kt, t=t: W_QKV[:, kt, 128 * t:128 * t + 128],
                    qT[:, t, tok0:tok0 + ncols], qT_r.k(), nb())

    def u_proj_st(hn_key, hnT_, utm8_, utm_k):
        for s_ in range(8):
            bk, bkk = PS(nb())

            def f(e, bk=bk, s_=s_, hnT_=hnT_):
                last = None
                for kt in range(8):
                    last = e.matmul(bk[:, 0:512], lhsT=hnT_[:, kt, s_::8], rhs=W_U[:, kt, :], start=(kt == 0), stop=(kt == 7))
                return last
            P.pe(f, reads=[hn_key, wu.k()], writes=[bkk])
            evac(utm8_[:, :, s_, :], bk.rearrange("p (g c) -> p g c", g=32), [bkk], [utm_k])

    def U_transposes(U_dst, U_key, utm8_, utm_k):
        for q4 in range(8):
            bk, bkk = PS(5 + q4 % 2)

            def f(e, bk=bk, q4=q4, utm8_=utm8_):
                last = None
                for j in range(4):
                    g = 4 * q4 + j
                    last = e.matmul(bk[:, j * 128:(j + 1) * 128], lhsT=utm8_[:, g, :, :].rearrange("p s c -> p (s c)"), rhs=ident_b, start=True, stop=True)
                return last
            P.pe(f, reads=[utm_k, pers.k("idb")], writes=[bkk])
            evac(U_dst[:, 4 * q4:4 * q4 + 4, :], bk.rearrange("p (j k) -> p j k", j=4), [bkk], [U_key])

    P.dma(nks_o[:, 0:124, :], ck_d[:, 4:128, :], key="c2o_k")
    P.dma(nvs_o[:, 0:124, :], cv_d[:, 4:128, :], key="c2o_v")
    hkey = hn_r.k()
    norm_tile(xp[NLITE * ST - 128:NLITE * ST, :], 128, hnT[:, :, 0:128], hkey, g_mix, "mix")
    k_cols(hnT[:, :, 0:128], hkey, 128, 0)
    kv_tile(hnT[:, :, 0:128], hkey, 128, 0)
    for st in range(2):
        specs = []
        for tl in range(8):
            r0 = st * ST + tl * 128
            hsl = hnT[:, :, tl * 128:(tl + 1) * 128]
            specs.append((xo[r0:r0 + 128, :], hsl, hkey,
                          (lambda hsl=hsl, st=st, tl=tl: kv_tile(hsl, hkey, 128, 1 + st * 8 + tl, last=(st == 1 and tl == 7)))))
        run_tiles(specs)
        if st == 0:
            bias_reads()
        for hf in range(2):
            c0 = hf * 512
            q_cols(hnT[:, :, c0:c0 + 512], hkey, 512, st * ST + c0)
            k_cols(hnT[:, :, c0:c0 + 512], hkey, 512, 128 + st * ST + c0)
        u_proj_st(hkey, hnT, u_tm8, utm_r.k())
        U_transposes(U_own[:, st], uown_r.k(st), u_tm8, utm_r.k())
        if st == 0:
            bias_stage3()
    norm_tile(xo[NP:NT, :], 64, hnT[:, :, 0:64], hkey, g_mix, "mix")
    q_cols(hnT[:, :, 0:64], hkey, 64, NP)
    k_cols(hnT[:, :, 0:64], hkey, 64, 128 + NP)
    kv_tile(hnT[:, :, 0:64], hkey, 64, 0, sample=True)
    P.pool(lambda e: e.memset(utm_r.f32(), 0.0), reads=[uown_r.k(0), uown_r.k(1)], writes=[utm_r.k()])
    for t in range(4):
        bk, bkk = PS(nb())

        def f(e, bk=bk, t=t, hnT=hnT):
            last = None
            for kt in range(8):
                last = e.matmul(bk[0:16, 0:512], lhsT=hnT[:, kt, t:64:4], rhs=W_U[:, kt, :], start=(kt == 0), stop=(kt == 7))
            return last
        P.pe(f, reads=[hkey, wu.k()], writes=[bkk])
        evac(u_s12[0:16, :, 4 + t, :], bk[0:16, :].rearrange("p (g c) -> p g c", g=32), [bkk], [utm_r.k()])
    U_s = us_r.bf().rearrange("p (w g b) -> p w g b", w=2, g=32)
    for w in range(2):
        bk, bkk = PS(5 + w)

        def f(e, bk=bk, w=w, u_s12=u_s12):
            last = None
            for g in range(32):
                last = e.matmul(bk[:, g * 16:(g + 1) * 16], lhsT=u_s12[0:16, g, 4 * w:4 * w + 8, :].rearrange("p s c -> p (s c)"),
                                rhs=ident_b[0:16, 0:16], start=True, stop=True)
            return last
        P.pe(f, reads=[utm_r.k(), pers.k("idb")], writes=[bkk])
        evac(U_s[:, w], bk.rearrange("p (g b) -> p g b", g=32), [bkk], [us_r.k()])
    AR.free(wqkv)
    for _r in [xs_r, hn_r, utm_r, stat_r, kvf_r] + xn_rs:
        AR.free(_r)
    if debug:
        dbg["qT"] = (qT_r, [128, 4 * NT], BF16)
        dbg["kT"] = (kT_r, [128, 4 * NTK], BF16)
        dbg["vaug"] = (va_r, [128, 17 * 2 * 72], BF16)
        dbg["Uown"] = (uown_r, [128, 2 * 32 * 128], BF16)
        dbg["Usamp"] = (us_r, [128, 2 * 32 * 16], BF16)

    mT_r = AR.alloc("mTattn", 4 * NT // 2)
    mT_att = mT_r.bf().rearrange("p (t n) -> p t n", t=4)
    aw = AR.alloc("attwork", 2 * 512 + 2 * 512 + 512 + 256 + 64 + 1024)
    tS = [aw.f32(0, 512), aw.f32(512, 512)]
    PT = [aw.bf(1024, 256).rearrange("p (r q) -> p r q", r=4), aw.bf(1280, 256).rearrange("p (r q) -> p r q", r=4),
          aw.bf(1536, 256).rearrange("p (r q) -> p r q", r=4), aw.bf(1792, 256).rearrange("p (r q) -> p r q", r=4)]
    PT += [aw.bf(2880 + 256 * i_, 256).rearrange("p (r q) -> p r q", r=4) for i_ in range(4)]
    o_sb = aw.f32(2048, 512)
    on_b = aw.bf(2560, 256)
    ast = aw.f32(2816, 64)
    ep_ctr = {"n": 0}

    def attn_epilogue(bO, nrows, tok0):
        i = ep_ctr["n"]
        ep_ctr["n"] += 1
        pr = slice(0, nrows)
        sc = ast[:, 16 * (i % 4):16 * (i % 4) + 16]
        ks = aw.k("st", i % 4)
        for g in range(2):
            bk, bkk = bO[g]
            Ov = bk[:, 0:288].rearrange("p (r d) -> p r d", r=4)
            tt("dve", sc[pr, 4 * g:4 * g + 4], Ov[pr, :, 64], sinkexp[pr, 4 * g:4 * g + 4], ALU.add, [bkk, gains.k("sink")], [ks])
        P.dve(lambda e: e.reciprocal(out=sc[pr, 0:8], in_=sc[pr, 0:8]), reads=[ks], writes=[ks])
        for g in range(2):
            bk, bkk = bO[g]
            Ov = bk[:, 0:288].rearrange("p (r d) -> p r d", r=4)
            tt("dve", o_sb[pr, 256 * g:256 * g + 256].rearrange("p (r d) -> p r d", r=4), Ov[pr, :, 0:64],
               sc[pr, 4 * g:4 * g + 4].unsqueeze(2).to_broadcast([nrows, 4, 64]), ALU.mult, [bkk, ks], [aw.k("o")])
        P.act(lambda e: e.activation(out=on_b[pr], in_=o_sb[pr], func=AF.Square, accum_out=sc[pr, 8:9]), reads=[aw.k("o")], writes=[aw.k("on"), ks + ("s",)])
        P.dve(lambda e: e.tensor_scalar(out=sc[pr, 9:10], in0=sc[pr, 8:9], scalar1=1.0 / 512, scalar2=EPS, op0=ALU.mult, op1=ALU.add),
              reads=[ks + ("s",)], writes=[ks + ("a",)])
        P.pool(lambda e: e.tensor_tensor(out=sc[pr, 10:11], in0=sc[pr, 9:10], in1=mhalf[pr], op=ALU.pow), reads=[ks + ("a",), pers.k("mh")], writes=[ks + ("b",)])
        P.dve(lambda e: e.scalar_tensor_tensor(out=on_b[pr], in0=o_sb[pr], scalar=sc[pr, 10:11], in1=g_att[pr], op0=ALU.mult, op1=ALU.mult),
              reads=[aw.k("o"), ks + ("b",), gains.k("att")], writes=[aw.k("on")])
        bk, bkk = PS(6)

        def f(e):
            last = None
            for t in range(4):
                last = e.matmul(bk[:, t * 128:t * 128 + nrows], lhsT=on_b[pr, t * 128:(t + 1) * 128], rhs=ident_b[pr, pr], start=True, stop=True)
            return last
        P.pe(f, reads=[aw.k("on"), pers.k("idb")], writes=[bkk])
        evac(mT_att[:, :, tok0:tok0 + nrows], bk.rearrange("p (t n) -> p t n", t=4)[:, :, 0:nrows], [bkk], [mT_r.k()], eng="act")

    def att_stage_a(b):
        j = b + 1
        ps_ = 4 * (b % 2)
        for g in range(2):
            for slot in range(2):
                kb = j - 1 + slot
                bS, bSk = PS(2 + slot)

                def f(e, bS=bS, g=g, kb=kb, b=b):
                    last = None
                    for r in range(4):
                        last = e.matmul(bS[:, r * 128:(r + 1) * 128], lhsT=kT[:, 2 * g + (r % 2), 128 * kb:128 * kb + 128],
                                        rhs=qT[:, 2 * g + r // 2, 128 * b:128 * b + 128], start=True, stop=True)
                    return last
                P.pe(f, reads=[kT_r.k(), qT_r.k()], writes=[bSk])
                bias = (biasH if b == 0 else biasT[:, 0]) if slot == 0 else biasT[:, 1]
                bkey = (bh_r.k() if b == 0 else pers.k("biasT")) if slot == 0 else pers.k("biasT")
                P.dve(lambda e, bS=bS, bias=bias, slot=slot, g=g: e.scalar_tensor_tensor(
                    out=tS[slot], in0=bS, scalar=0.125, in1=bias[:, 4 * g:4 * g + 4, :].rearrange("p r q -> p (r q)"), op0=ALU.mult, op1=ALU.add),
                    reads=[bSk, bkey], writes=[aw.k("tS", slot)])
                pt = PT[ps_ + 2 * g + slot]
                P.act(lambda e, pt=pt, slot=slot: e.activation(out=pt.rearrange("p r q -> p (r q)"), in_=tS[slot], func=AF.Exp),
                      reads=[aw.k("tS", slot)], writes=[aw.k("PT", ps_ + 2 * g + slot)])

    def att_stage_b(b):
        j = b + 1
        ps_ = 4 * (b % 2)
        bO = [PS(4), PS(5)]
        for g in range(2):
            bk, bkk = bO[g]

            def fpv(e, bk=bk, g=g, j=j, ps_=ps_):
                last = None
                for r in range(4):
                    for slot in range(2):
                        last = e.matmul(bk[:, r * 72:r * 72 + 65], lhsT=PT[ps_ + 2 * g + slot][:, r, :], rhs=v_aug[:, j - 1 + slot, g, 0:65],
                                        start=(slot == 0), stop=(slot == 1))
                return last
            P.pe(fpv, reads=[aw.k("PT", ps_ + 2 * g), aw.k("PT", ps_ + 2 * g + 1), va_r.k()], writes=[bkk])
        attn_epilogue(bO, 128, 128 * b)

    att_stage_a(0)
    for b in range(16):
        if b + 1 < 16:
            att_stage_a(b + 1)
        att_stage_b(b)

    sa = AR.alloc("sa_st", 2048)
    sa2 = AR.alloc("sa_kc", 2048)
    sa3 = AR.alloc("sa_vc", 1152)

    class _SK:
        def k(self, s_):
            return ("sa", s_)
    cst32 = sa.f32(0, 2048).rearrange("p (b d) -> p b d", b=16)
    kc_n = sa2.bf(0, 1024).rearrange("p (b d) -> p b d", b=16)
    kc_s = sa2.bf(1024, 1024).rearrange("p (b d) -> p b d", b=16)
    Vc = sa3.bf(0, 1152).rearrange("p (b g d) -> p b g d", b=16, g=2)
    Kz = [None] * 4
    _unused = [lambda i: sa4.bf(1024 * i, 1024).rearrange("p (b k) -> p b k", b=16) for i in range(4)]
    _sa0 = sa
    sa = type("X", (), {"k": staticmethod(lambda s_: ("sa_" + {"st": "st", "kcn": "kc", "kcs": "kc", "vc": "vc", "kz": "kz", "pfa": "pfa", "tsa": "ts", "tsb": "ts", "pb": "ts"}[s_], s_))})
    P.dma(cst32, ck_d.rearrange("b k d -> k b d"), writes=[sa.k("st")])
    P.pool(lambda e: e.tensor_copy(out=kc_n, in_=cst32), reads=[sa.k("st")], writes=[sa.k("kcn")])
    P.pool(lambda e: e.tensor_copy(out=kc_s[:, :, 0:64], in_=cst32[:, :, 64:128]), reads=[sa.k("st")], writes=[sa.k("kcs")])
    P.pool(lambda e: e.tensor_copy(out=kc_s[:, :, 64:128], in_=cst32[:, :, 0:64]), reads=[sa.k("st")], writes=[sa.k("kcs")])
    P.pool(lambda e: e.memset(Vc, 1.0), writes=[sa.k("vc")])
    P.dma(cst32, cv_d.rearrange("b k d -> k b d"), writes=[sa.k("st")])
    P.pool(lambda e: e.tensor_copy(out=Vc[:, :, :, 0:64], in_=cst32.rearrange("p b (g d) -> p b g d", g=2)), reads=[sa.k("st")], writes=[sa.k("vc")])
    AR.free(_sa0)
    sa4 = AR.alloc("sa_kz", 4096)
    Kz = [sa4.bf(1024 * i, 1024).rearrange("p (b k) -> p b k", b=16) for i in range(4)]
    P.pool(lambda e: e.memset(sa4.f32(), 0.0), writes=[sa.k("kz")])
    for typ, src, (lo_i, hi_i) in ((0, kc_n, (0, 1)), (1, kc_s, (2, 3))):
        for b4 in range(4):
            bk, bkk = PS(2 + b4 % 2)

            def f(e, bk=bk, src=src, b4=b4):
                last = None
                for jj in range(4):
                    last = e.matmul(bk[:, jj * 128:(jj + 1) * 128], lhsT=src[:, 4 * b4 + jj, :], rhs=ident_b, start=True, stop=True)
                return last
            P.pe(f, reads=[sa.k("kcn"), sa.k("kcs"), pers.k("idb")], writes=[bkk])
            bv = bk.rearrange("p (j k) -> p j k", j=4)
            evac(Kz[lo_i][0:64, 4 * b4:4 * b4 + 4, :], bv[0:64], [bkk], [sa.k("kz")], eng="act")
            evac(Kz[hi_i][64:128, 4 * b4:4 * b4 + 4, :], bv[64:128], [bkk], [sa.k("kz")], eng="dve")
    AR.free(sa2)
    sa5 = AR.alloc("sa_ts", 512 + 512 + 256)
    sa6 = AR.alloc("sa_pfa", 2048)
    tSA = sa5.f32(0, 512)
    PfA = sa6.bf(0, 2048)
    tSB = sa5.f32(512, 512)
    PBs = sa5.bf(1024, 256).rearrange("p (h q) -> p h q", h=8)
    P.pool(lambda e: e.memset(sa6.f32(), 0.0), writes=[sa.k("pfa")])
    _sa_regs = [sa3, sa4, sa5, sa6]
    KV = {(0, 0): Kz[0], (0, 1): Kz[3], (1, 0): Kz[2], (1, 1): Kz[1]}
    bA, bAk = PS(2)

    def fqa(e, bA=bA):
        last = None
        for b_ in range(NSQ):
            for g in range(2):
                for r in range(4):
                    c0 = ((b_ * 2 + g) * 4 + r) * 4
                    last = e.matmul(bA[:, c0:c0 + 4], lhsT=KV[(g, r % 2)][:, b_, :], rhs=qT[:, 2 * g + r // 2, NP + 4 * b_:NP + 4 * b_ + 4],
                                    start=True, stop=True)
        return last
    P.pe(fqa, reads=[sa.k("kz"), qT_r.k()], writes=[bAk])
    P.dve(lambda e: e.scalar_tensor_tensor(out=tSA.rearrange("p (b h i) -> p b h i", b=16, h=8), in0=bA.rearrange("p (b h i) -> p b h i", b=16, h=8),
                                           scalar=0.125, in1=biasA.unsqueeze(1).to_broadcast([128, 16, 8, 4]), op0=ALU.mult, op1=ALU.add),
          reads=[bAk, pers.k("biasA")], writes=[sa.k("tsa")])
    bB, bBk = PS(3)

    def fqb(e, bB=bB):
        last = None
        for h in range(8):
            g, r = h // 4, h % 4
            last = e.matmul(bB[0:64, h * 64:(h + 1) * 64], lhsT=kT[:, 2 * g + (r % 2), 128 + NP:128 + NP + 64],
                            rhs=qT[:, 2 * g + r // 2, NP:NP + 64], start=True, stop=True)
        return last
    P.pe(fqb, reads=[kT_r.k(), qT_r.k()], writes=[bBk])
    P.dve(lambda e: e.scalar_tensor_tensor(out=tSB[0:64], in0=bB[0:64], scalar=0.125, in1=biasB[0:64].rearrange("p h q -> p (h q)"),
                                           op0=ALU.mult, op1=ALU.add), reads=[bBk, pers.k("biasB")], writes=[sa.k("tsb")])
    P.act(lambda e: e.activation(out=PBs[0:64].rearrange("p h q -> p (h q)"), in_=tSB[0:64], func=AF.Exp), reads=[sa.k("tsb")], writes=[sa.k("pb")])
    PfAv = PfA.rearrange("p (r b q) -> p r b q", r=4, b=16)
    pfa_out = bass.AP(PfA.tensor, PfA.offset, [list(PfA.ap[0]), [68, 16], [1024, 4], [1, 4]])
    bO = [PS(4), PS(5)]
    for g in range(2):
        bk, bkk = bO[g]
        P.act(lambda e, g=g: e.activation(out=pfa_out, in_=tSA.rearrange("p (b h i) -> p b h i", b=16, h=8)[:, :, 4 * g:4 * g + 4, :], func=AF.Exp),
              reads=[sa.k("tsa"), sa.k("pfa")], writes=[sa.k("pfa")])

        def fpvs(e, bk=bk, g=g):
            last = None
            for r in range(4):
                h = 4 * g + r
                for b_ in range(NSQ):
                    e.matmul(bk[0:64, r * 72:r * 72 + 65], lhsT=PfAv[:, r, b_, :], rhs=Vc[:, b_, g, 0:65], start=(b_ == 0), stop=False)
                last = e.matmul(bk[0:64, r * 72:r * 72 + 65], lhsT=PBs[0:64, h, :], rhs=vs_aug[0:64, g, 0:65], start=False, stop=True)
            return last
        P.pe(fpvs, reads=[sa.k("pfa"), sa.k("pb"), sa.k("vc"), vs_r.k()], writes=[bkk])
    attn_epilogue(bO, 64, NP)
    for _r in _sa_regs:
        AR.free(_r)
    AR.free(aw)
    AR.free(qT_r)
    AR.free(kT_r)
    AR.free(va_r)
    AR.free(vs_r)
    AR.free(bh_r)
    if debug:
        dbg["mTattn"] = (mT_r, [128, 4 * NT], BF16)
        dbg["pers"] = (pers, [128, 3048], F32)

    hn2_rs = [AR.alloc("hnT2a", 8 * 1024 // 2), AR.alloc("hnT2b", 8 * 1024 // 2)]
    hnT2s = [r_.bf().rearrange("p (k n) -> p k n", k=8) for r_ in hn2_rs]
    gb_r = AR.alloc("gbuf", 16 * 2 * 128)
    G = gb_r.f32().rearrange("p (g r k) -> p g r k", g=16, r=2)
    xs_r = AR.alloc("xs2", 3 * 1024)
    utm2_r = AR.alloc("utm8b", 32 * 8 * 16 // 2)
    u_tm8b = utm2_r.bf().rearrange("p (g s c) -> p g s c", g=32, s=8)
    ul_rs = [AR.alloc("Ulitea", 32 * 128 // 2), AR.alloc("Uliteb", 32 * 128 // 2)]
    U_ls = [r_.bf().rearrange("p (g k) -> p g k", g=32) for r_ in ul_rs]
    rt_r = AR.alloc("rottmp", 2048)
    xn_rs = [AR.alloc("xn2a", 512), AR.alloc("xn2b", 512)]
    stat_r = AR.alloc("stats2", 256)
    ss_r = AR.alloc("sscr", 16 * 12)
    SS = [ss_r.f32(16 * i, 16) for i in range(12)]
    kH = ksm("H")
    P.pool(lambda e: e.memset(SS[11], 0.0), reads=[ksm("hre"), ksm("him")], writes=[kH])

    def S_rotate(Uv, Ukey, eng="dve"):
        for q4 in range(4):
            (bre, brek), (bim, bimk) = (PS(4), PS(5)) if q4 % 2 == 0 else (PS(6), PS(7))

            def f(e, bre=bre, bim=bim, q4=q4, Uv=Uv):
                last = None
                for j in range(4):
                    gq = 4 * q4 + j
                    for ri, bank in ((0, bre), (1, bim)):
                        e.matmul(bank[:, j * 128:(j + 1) * 128], lhsT=W_P[:, gq, ri, :], rhs=Uv[:, gq, :], start=True, stop=False)
                        last = e.matmul(bank[:, j * 128:(j + 1) * 128], lhsT=W_P[:, 16 + gq, ri, :], rhs=Uv[:, 16 + gq, :], start=False, stop=True)
                return last
            P.pe(f, reads=[wp.k(), Ukey], writes=[brek, bimk])
            gs = slice(4 * q4, 4 * q4 + 4)
            ck = CK[:, gs, 0:128]
            sk = SK[:, gs, 0:128]
            tk = [tabs.k("ck"), tabs.k("sk")]
            if eng == "dve":
                Sre = bre.rearrange("p (g k) -> p g k", g=4)
                Sim = bim.rearrange("p (g k) -> p g k", g=4)
                tA = rt_r.f32(0, 512).rearrange("p (g k) -> p g k", g=4)
                tB = rt_r.f32(512, 512).rearrange("p (g k) -> p g k", g=4)
                kA = rt_r.k("a"); kB = rt_r.k("b")
                tt("dve", tA, ck, Sre, ALU.mult, tk + [brek], [kA])
                tt("dve", tB, sk, Sim, ALU.mult, tk + [bimk], [kB])
                tt("dve", G[:, gs, 0, :], tA, tB, ALU.add, [kA, kB], [gb_r.k(q4)])
                tt("dve", tA, ck, Sim, ALU.mult, tk + [bimk], [kA])
                tt("dve", tB, sk, Sre, ALU.mult, tk + [brek], [kB])
                tt("dve", G[:, gs, 1, :], tA, tB, ALU.subtract, [kA, kB], [gb_r.k(q4)])
            else:
                gk = gb_r.k(q4)
                evac(G[:, gs, 0, :], bre.rearrange("p (g k) -> p g k", g=4), [brek], [gk], eng="act")
                evac(G[:, gs, 1, :], bim.rearrange("p (g k) -> p g k", g=4), [bimk], [gk], eng="act")
                tmps = [rt_r.f32(512 * i, 512).rearrange("p (g k) -> p g k", g=4) for i in range(4)]
                kt_ = [rt_r.k("t", i) for i in range(4)]
                tt(eng, tmps[0], ck, G[:, gs, 0, :], ALU.mult, tk + [gk], [kt_[0]])
                tt(eng, tmps[1], sk, G[:, gs, 1, :], ALU.mult, tk + [gk], [kt_[1]])
                tt(eng, tmps[2], ck, G[:, gs, 1, :], ALU.mult, tk + [gk], [kt_[2]])
                tt(eng, tmps[3], sk, G[:, gs, 0, :], ALU.mult, tk + [gk], [kt_[3]])
                tt(eng, G[:, gs, 0, :], tmps[0], tmps[1], ALU.add, [kt_[0], kt_[1]], [gk])
                tt(eng, G[:, gs, 1, :], tmps[2], tmps[3], ALU.subtract, [kt_[2], kt_[3]], [gk])

    def scan_all(init, eng="dve"):
        for gq in range(16):
            for ri in range(2):
                ini = 0.0 if init is None else init[ri][:, gq:gq + 1]
                rd = [gb_r.k(gq // 4), ksm("r8")] + ([] if init is None else [ss_r.k("ini")])
                P.op(eng, lambda e, gq=gq, ri=ri, ini=ini: e.tensor_tensor_scan(
                    out=G[:, gq, ri, :], data0=R8[:, gq:gq + 1].to_broadcast([128, 128]), data1=G[:, gq, ri, :],
                    initial=ini, op0=ALU.mult, op1=ALU.add), reads=rd, writes=[gb_r.k(gq // 4)])

    def rot_small(ore, oim, cc, sn, xre, xim, rd, wr, eng="dve"):
        tt(eng, SS[0], cc, xre, ALU.mult, rd, [ss_r.k(0)])
        tt(eng, SS[1], sn, xim, ALU.mult, rd, [ss_r.k(1)])
        tt(eng, SS[2], cc, xim, ALU.mult, rd, [ss_r.k(2)])
        tt(eng, SS[3], sn, xre, ALU.mult, rd, [ss_r.k(3)])
        tt(eng, ore, SS[0], SS[1], ALU.subtract, [ss_r.k(0), ss_r.k(1)], wr)
        tt(eng, oim, SS[2], SS[3], ALU.add, [ss_r.k(2), ss_r.k(3)], wr)

    tabk = [tabs.k("ck"), tabs.k("sk")]
    GK = [gb_r.k(i_) for i_ in range(4)]

    def final_state(ore, oim, wr, eng="dve"):
        rot_small(ore, oim, CK[:, :, 127], SK[:, :, 127], G[:, :, 0, 127], G[:, :, 1, 127], tabk + GK, wr, eng=eng)

    def lite_T_gen(lt):
        pb = lt % 2
        hk2 = hn2_rs[pb].k()
        hnT_ = hnT2s[pb]
        specs = [(xp[lt * ST + tl * 128:lt * ST + tl * 128 + 128, :], hnT_[:, :, tl * 128:(tl + 1) * 128]) for tl in range(8)]
        ctxs = [None] * 8
        ctxs[0] = norm_front(specs[0][0], 128, g_mix, "mix")
        for i in range(8):
            if i + 1 < 8:
                ctxs[i + 1] = norm_front(specs[i + 1][0], 128, g_mix, "mix")
            norm_back(ctxs[i], specs[i][1], hk2)
            yield

    def lite_U_gen(lt):
        pb = lt % 2
        hk2 = hn2_rs[pb].k()
        hnT_ = hnT2s[pb]
        for s_ in range(8):
            bk, bkk = PS(nb())

            def f(e, bk=bk, s_=s_, hnT_=hnT_):
                last = None
                for kt in range(8):
                    last = e.matmul(bk[:, 0:512], lhsT=hnT_[:, kt, s_::8], rhs=W_U[:, kt, :], start=(kt == 0), stop=(kt == 7))
                return last
            P.pe(f, reads=[hk2, wu.k()], writes=[bkk])
            evac(u_tm8b[:, :, s_, :], bk.rearrange("p (g c) -> p g c", g=32), [bkk], [utm2_r.k()])
            yield
        U_dst = U_ls[pb]
        for q4 in range(8):
            bk, bkk = PS(5 + q4 % 2)

            def f(e, bk=bk, q4=q4):
                last = None
                for j in range(4):
                    g = 4 * q4 + j
                    last = e.matmul(bk[:, j * 128:(j + 1) * 128], lhsT=u_tm8b[:, g, :, :].rearrange("p s c -> p (s c)"), rhs=ident_b, start=True, stop=True)
                return last
            P.pe(f, reads=[utm2_r.k(), pers.k("idb")], writes=[bkk])
            evac(U_dst[:, 4 * q4:4 * q4 + 4, :], bk.rearrange("p (j k) -> p j k", j=4), [bkk], [ul_rs[pb].k()])
            yield

    def scan_gen():
        for gq in range(16):
            for ri in range(2):
                P.dve(lambda e, gq=gq, ri=ri: e.tensor_tensor_scan(
                    out=G[:, gq, ri, :], data0=R8[:, gq:gq + 1].to_broadcast([128, 128]), data1=G[:, gq, ri, :],
                    initial=0.0, op0=ALU.mult, op1=ALU.add), reads=[gb_r.k(gq // 4), ksm("r8")], writes=[gb_r.k(gq // 4)])
            yield

    def drain(g):
        for _ in g:
            pass

    def interleave(gens):
        gens = [g for g in gens if g is not None]
        while gens:
            for g in list(gens):
                try:
                    next(g)
                except StopIteration:
                    gens.remove(g)

    drain(lite_T_gen(0))
    interleave([lite_T_gen(1), lite_U_gen(0)])
    interleave([lite_T_gen(2), lite_U_gen(1)])
    for lt in range(NLITE):
        S_rotate(U_ls[lt % 2], ul_rs[lt % 2].k(), eng=("pool" if lt >= 3 else "dve"))
        interleave([lite_T_gen(lt + 3) if lt + 3 < NLITE else None,
                    lite_U_gen(lt + 2) if lt + 2 < NLITE else None,
                    scan_gen()])
        final_state(SS[4], SS[5], [ss_r.k("F")])
        rot_small(SS[6], SS[7], CK[:, :, 128], SK[:, :, 128], HRE, HIM, tabk + [kH], [ss_r.k("LH")])
        tt("dve", HRE, SS[6], R128, ALU.mult, [ss_r.k("LH"), ksm("r128")], [kH])
        tt("dve", HIM, SS[7], R128, ALU.mult, [ss_r.k("LH"), ksm("r128")], [kH])
        tt("dve", HRE, HRE, SS[4], ALU.add, [kH, ss_r.k("F")], [kH])
        tt("dve", HIM, HIM, SS[5], ALU.add, [kH, ss_r.k("F")], [kH])
    for _r in [xs_r, stat_r, utm2_r, rt_r] + xn_rs + hn2_rs + ul_rs:
        AR.free(_r)
    rt_r = AR.alloc("rottmp2", 2 * 2032)
    rt2_r = AR.alloc("rottmp3", 2 * 2032)
    xp_r = AR.alloc("Xprev", 2 * 16 * 2 * 128 // 2)
    Xprev = xp_r.bf().rearrange("p (s g r k) -> p s g r k", s=2, g=16, r=2)

    for st in range(2):
        S_rotate(U_own[:, st], uown_r.k(st))
        rot_small(SS[8], SS[9], CK[:, :, 1], SK[:, :, 1], HRE, HIM, tabk + [kH], [ss_r.k("ini")])
        scan_all((SS[8], SS[9]))
        P.dve(lambda e, st=st: e.tensor_copy(out=Xprev[:, st, :, 0, 0], in_=HRE), reads=[kH], writes=[xp_r.k(st)])
        P.dve(lambda e, st=st: e.tensor_copy(out=Xprev[:, st, :, 1, 0], in_=HIM), reads=[kH], writes=[xp_r.k(st)])
        tA = rt_r.f32(0, 2032).rearrange("p (g k) -> p g k", g=16)
        tB = rt_r.f32(2032, 2032).rearrange("p (g k) -> p g k", g=16)
        kA = rt_r.k("a"); kB = rt_r.k("b")
        ck = CK[:, :, 0:127]; sk = SK[:, :, 0:127]
        tt("dve", tA, ck, G[:, :, 0, 0:127], ALU.mult, tabk + GK, [kA])
        tt("dve", tB, sk, G[:, :, 1, 0:127], ALU.mult, tabk + GK, [kB])
        tt("dve", Xprev[:, st, :, 0, 1:128], tA, tB, ALU.subtract, [kA, kB], [xp_r.k(st)])
        tC = rt2_r.f32(0, 2032).rearrange("p (g k) -> p g k", g=16)
        tD = rt2_r.f32(2032, 2032).rearrange("p (g k) -> p g k", g=16)
        kC = rt2_r.k("c"); kD = rt2_r.k("d")
        tt("pool", tC, ck, G[:, :, 1, 0:127], ALU.mult, tabk + GK, [kC])
        tt("pool", tD, sk, G[:, :, 0, 0:127], ALU.mult, tabk + GK, [kD])
        tt("pool", Xprev[:, st, :, 1, 1:128], tC, tD, ALU.add, [kC, kD], [xp_r.k(st, "im")])
        final_state(HRE, HIM, [kH])
    AR.free(gb_r)
    AR.free(rt_r)
    AR.free(rt2_r)
    AR.free(tabs)
    AR.free(wu)

    h0_r = AR.alloc("h0", 2 * 256 + 256 + 2 * 256 + 512)
    h0n_r = AR.alloc("h0n", 2 * 2048)
    h0n = [h0n_r.f32(0, 2048), h0n_r.f32(2048, 2048)]
    h0T = [h0_r.f32(0, 256).rearrange("p (g b) -> p g b", g=16), h0_r.f32(256, 256).rearrange("p (g b) -> p g b", g=16)]
    h0b = h0_r.bf(512, 256).rearrange("p (g r b) -> p g r b", g=16, r=2)
    XN = [h0_r.f32(768, 256).rearrange("p (g b) -> p g b", g=16), h0_r.f32(1024, 256).rearrange("p (g b) -> p g b", g=16)]
    for pl, src in ((0, sre_d), (1, sim_d)):
        for gh in range(2):
            P.dma(h0n[pl][0:16].rearrange("p (g h q) -> p g h q", g=16, h=2)[:, :, gh, :], dram_ap(src, 1024 * gh, [[2048, 16], [64, 16], [1, 64]]),
                  writes=[h0n_r.k(pl)])
        bk, bkk = PS(4 + pl)

        def f(e, bk=bk, pl=pl):
            last = None
            for gq in range(16):
                last = e.matmul(bk[:, gq * 16:(gq + 1) * 16], lhsT=h0n[pl][0:16, gq * 128:(gq + 1) * 128], rhs=ident_f[0:16, 0:16], start=True, stop=True)
            return last
        P.pe(f, reads=[h0n_r.k(pl), pers.k("idf")], writes=[bkk])
        evac(h0T[pl].rearrange("p g b -> p (g b)"), bk[:, 0:256], [bkk], [h0_r.k("T", pl)], eng="act")
        P.dve(lambda e, pl=pl: e.tensor_copy(out=h0b[:, :, pl, :], in_=h0T[pl]), reads=[h0_r.k("T", pl)], writes=[h0_r.k("b")])
    AR.free(h0n_r)
    xo_r = AR.alloc("xosb", 2048)
    xo_sb = xo_r.f32()
    bS, bSk = PS(6)

    def fss(e, bS=bS):
        last = None
        for gq in range(16):
            for ri in range(2):
                c0 = (gq * 2 + ri) * 16
                e.matmul(bS[:, c0:c0 + 16], lhsT=W_P[:, gq, ri, :], rhs=U_s[:, 0, gq, :], start=True, stop=False)
                last = e.matmul(bS[:, c0:c0 + 16], lhsT=W_P[:, 16 + gq, ri, :], rhs=U_s[:, 0, 16 + gq, :], start=False, stop=True)
        return last
    P.pe(fss, reads=[wp.k(), us_r.k()], writes=[bSk])
    Sv = bS.rearrange("p (g r b) -> p g r b", g=16, r=2)
    l4r = LRE[:, :, 4].unsqueeze(2).to_broadcast([128, 16, 16])
    l4i = LIM[:, :, 4].unsqueeze(2).to_broadcast([128, 16, 16])
    vv1 = h0_r.f32(1280, 256).rearrange("p (g b) -> p g b", g=16)
    vv2 = h0_r.f32(1536, 256).rearrange("p (g b) -> p g b", g=16)
    kv1_ = h0_r.k("w1"); kv2_ = h0_r.k("w2")
    lk = [ksm("lre"), ksm("lim")]
    tt("dve", vv1, l4r, h0T[0], ALU.mult, lk + [h0_r.k("T", 0)], [kv1_])
    tt("dve", vv2, l4i, h0T[1], ALU.mult, lk + [h0_r.k("T", 1)], [kv2_])
    tt("dve", vv1, vv1, vv2, ALU.subtract, [kv1_, kv2_], [kv1_])
    tt("dve", XN[0], vv1, Sv[:, :, 0, :], ALU.add, [kv1_, bSk], [h0_r.k("X", 0)])
    tt("dve", vv1, l4r, h0T[1], ALU.mult, lk + [h0_r.k("T", 1)], [kv1_])
    tt("dve", vv2, l4i, h0T[0], ALU.mult, lk + [h0_r.k("T", 0)], [kv2_])
    tt("dve", vv1, vv1, vv2, ALU.add, [kv1_, kv2_], [kv1_])
    tt("dve", XN[1], vv1, Sv[:, :, 1, :], ALU.add, [kv1_, bSk], [h0_r.k("X", 1)])
    for pl, dst in ((0, ssre_o), (1, ssim_o)):
        for q4 in range(4):
            bk, bkk = PS(4 + q4 % 2)

            def f(e, bk=bk, pl=pl, q4=q4):
                last = None
                for j in range(4):
                    last = e.matmul(bk[0:16, j * 128:(j + 1) * 128], lhsT=XN[pl][:, 4 * q4 + j, :], rhs=ident_f, start=True, stop=True)
                return last
            P.pe(f, reads=[h0_r.k("X", pl), pers.k("idf")], writes=[bkk])
            evac(xo_sb[0:16].rearrange("p (h g q) -> p g h q", h=2, g=16)[:, 4 * q4:4 * q4 + 4, :, :],
                 bk[0:16, :].rearrange("p (g h q) -> p g h q", g=4, h=2), [bkk], [xo_r.k()], eng="act")
        P.dma(dst, xo_sb[0:16], reads=[xo_r.k()])
    AR.free(xo_r)
    AR.free(wp)
    if debug:
        dbg["Xprev"] = (xp_r, [128, 2 * 16 * 2 * 128], BF16)
        dbg["sm2"] = (sm, [128, 2816], F32)

    _s3 = {"cn": AR.alloc("s3cn", 512), "g9r": AR.alloc("s3g9r", 2304), "g9i": AR.alloc("s3g9i", 2304), "u1": AR.alloc("s3u1", 2304), "u2": AR.alloc("s3u2", 2304)}

    class _S3:
        @staticmethod
        def k(x):
            return (_s3["cn" if x.startswith("cn") else x].name, x)
    s3 = _S3
    Cn = _s3["cn"].f32(0, 512).rearrange("p (r s q) -> p r s q", r=2, s=2)
    G9R = _s3["g9r"].f32().rearrange("p (g j c) -> p g j c", g=16, j=9)
    G9I = _s3["g9i"].f32().rearrange("p (g j c) -> p g j c", g=16, j=9)
    u1 = _s3["u1"].f32().rearrange("p (g j c) -> p g j c", g=16, j=9)
    u2 = _s3["u2"].f32().rearrange("p (g j c) -> p g j c", g=16, j=9)
    for gs in range(2):
        P.dma(Cn[:, 0, gs, :].rearrange("p (h q) -> p h q", h=2), dram_ap(cre_d, 8192 * gs, [[64, 128], [16384, 2], [1, 64]]), writes=[s3.k("cn0")])
        P.dma(Cn[:, 1, gs, :].rearrange("p (h q) -> p h q", h=2), dram_ap(cim_d, 8192 * gs, [[64, 128], [16384, 2], [1, 64]]), writes=[s3.k("cn1")])
    for ri, CT in ((0, CTRE), (1, CTIM)):
        bk, bkk = PS(4 + ri)
        for gs in range(2):
            P.pe(lambda e, bk=bk, gs=gs, ri=ri: e.matmul(bk[:, gs * 128:(gs + 1) * 128], lhsT=Cn[:, ri, gs, :], rhs=ident_f, start=True, stop=True),
                 reads=[s3.k("cn%d" % ri), pers.k("idf")], writes=[bkk + (gs,)])
        P.act(lambda e, bk=bk, CT=CT: e.copy(out=CT.rearrange("p g c -> p (g c)"), in_=bk[:, 0:256]),
              reads=[bkk + (0,), bkk + (1,)], writes=[ksm("ct%d" % ri)])
    _STOP = 9
    cj = lambda a: a.unsqueeze(2).to_broadcast([128, 16, 9, 16])
    lj = lambda a: a.unsqueeze(3).to_broadcast([128, 16, 9, 16])
    if _STOP >= 2:
      tt("dve", u1, cj(CTRE), lj(LRE), ALU.mult, [ksm("ct0"), ksm("lre")], [s3.k("u1")])
      tt("dve", u2, cj(CTIM), lj(LIM), ALU.mult, [ksm("ct1"), ksm("lim")], [s3.k("u2")])
      tt("dve", G9R, u1, u2, ALU.subtract, [s3.k("u1"), s3.k("u2")], [s3.k("g9r")])
      tt("dve", u1, cj(CTRE), lj(LIM), ALU.mult, [ksm("ct0"), ksm("lim")], [s3.k("u1")])
      tt("dve", u2, cj(CTIM), lj(LRE), ALU.mult, [ksm("ct1"), ksm("lre")], [s3.k("u2")])
      tt("dve", G9I, u1, u2, ALU.add, [s3.k("u1"), s3.k("u2")], [s3.k("g9i")])

    AR.free(_s3.pop("u1"))
    AR.free(_s3.pop("u2"))
    wq = AR.alloc("wq", 32 * 2 * 128 // 2)
    W_Q = wq.bf().rearrange("p (g r m) -> p g r m", g=32, r=2)
    wm = AR.alloc("wm", 32 * 128 // 2)
    W_M = wm.bf().rearrange("p (g m) -> p g m", g=32)
    P.pool(lambda e: e.memset(wq.f32(), 0.0), writes=[wq.k()])
    _s4 = {"fp0": AR.alloc("s4fp0", 3840), "fp1": AR.alloc("s4fp1", 3840), "zz": AR.alloc("s4zz", 3840), "dc": AR.alloc("s4dc", 32)}

    class _S4:
        @staticmethod
        def k(x=None):
            return "s4all" if x is None else (_s4["dc"].name, x)
    s4 = _S4
    FPl = [_s4["fp0"].bf().rearrange("p (g m) -> p g m", g=32), _s4["fp1"].bf().rearrange("p (g m) -> p g m", g=32)]
    ZZ = _s4["zz"].bf().rearrange("p (r g m) -> p r g m", r=2, g=16)
    DC = _s4["dc"].f32(0, 32)
    S4K = ["s4all"] + [_s4[n].k() for n in ("fp0", "fp1", "zz")]
    for _nm in ("fp0", "fp1", "zz"):
        P.pool(lambda e, _nm=_nm: e.memset(_s4[_nm].f32(), 0.0), writes=S4K)
    for s_ in range(8):
        P.dma(DC[16 * s_:16 * s_ + 16, :], dram_ap(dd_d, 0, [[1, 16], [16, 32]]), writes=[s4.k("dc")], allow_slow_non_contiguous=True)
    for gh in (range(2) if _STOP >= 3 else []):
        hs = slice(64 * gh, 64 * gh + 64)
        gsl = slice(16 * gh, 16 * gh + 16)
        P.act(lambda e, hs=hs, gsl=gsl: e.copy(out=W_Q[hs, gsl, 0, :].rearrange("p g (j c) -> p g j c", j=8), in_=G9R[hs, :, 1:9, :]),
              reads=[s3.k("g9r")], writes=[wq.k()])
        P.act(lambda e, hs=hs, gsl=gsl: e.mul(out=W_Q[hs, gsl, 1, :].rearrange("p g (j c) -> p g j c", j=8), in_=G9I[hs, :, 1:9, :], mul=-1.0),
              reads=[s3.k("g9i")], writes=[wq.k()])
        P.act(lambda e, hs=hs, gsl=gsl: e.copy(out=FPl[0][hs, gsl, 112:240].rearrange("p g (j c) -> p g j c", j=8), in_=G9R[hs, :, 0:8, :]),
              reads=[s3.k("g9r")], writes=S4K)
        P.act(lambda e, hs=hs, gsl=gsl: e.mul(out=FPl[1][hs, gsl, 112:240].rearrange("p g (j c) -> p g j c", j=8), in_=G9I[hs, :, 0:8, :], mul=-1.0),
              reads=[s3.k("g9i")], writes=S4K)
    P.dve(lambda e: e.tensor_copy(out=ZZ[:, 0, :, 112:128], in_=BRE), reads=[ksm("bre")], writes=S4K)
    P.dve(lambda e: e.tensor_copy(out=ZZ[:, 1, :, 112:128], in_=BIM), reads=[ksm("bim")], writes=S4K)
    for g in (range(32) if _STOP >= 4 else []):
        gq = g % 16
        bk, bkk = PS(4 + (g // 4) % 4)
        o_ = bk[:, (g % 4) * 128:(g % 4) * 128 + 128]

        def f(e, o_=o_, g=g, gq=gq):
            last = None
            for s_ in range(8):
                lo = 112 - 16 * s_
                for ri in range(2):
                    last = e.matmul(o_, lhsT=ZZ[:, ri, gq, lo:lo + 128], rhs=FPl[ri][:, g, lo:lo + 128],
                                    start=(s_ == 0 and ri == 0), stop=(s_ == 7 and ri == 1))
            return last
        P.pe(f, reads=S4K, writes=[bkk + (g % 4,)])
        if True:
            P.dve(lambda e, o_=o_, g=g: e.scalar_tensor_tensor(out=W_M[:, g, :], in0=ident_f, scalar=DC[:, g:g + 1], in1=o_,
                                                           op0=ALU.mult, op1=ALU.add),
              reads=[bkk + (g % 4,), s4.k("dc"), pers.k("idf")], writes=[wm.k()])
    for _r in list(_s3.values()) + list(_s4.values()):
        AR.free(_r)
    for gh in range(2):
        P.dma(dram_ap(spre_o, 1024 * gh, [[1, 64], [64, 16]]), HRE[64 * gh:64 * gh + 64, :], reads=[kH], allow_slow_non_contiguous=True)
        P.dma(dram_ap(spim_o, 1024 * gh, [[1, 64], [64, 16]]), HIM[64 * gh:64 * gh + 64, :], reads=[kH], allow_slow_non_contiguous=True)
    if debug:
        dbg["W_Q"] = (wq, [128, 32 * 2 * 128], BF16)
        dbg["W_M"] = (wm, [128, 32 * 128], BF16)

    gl_r = AR.alloc("gl", 2 * 8 * 512 // 2 + 4 * 512 // 2)
    gl_tm8 = gl_r.bf(0, 4096).rearrange("p (s t g c) -> p s t g c", s=2, t=8, g=32)
    gl_s = gl_r.bf(4096, 1024).rearrange("p (t g c) -> p t g c", t=4, g=32)
    yw = AR.alloc("ywork", 4 * 512)
    yA = yw.f32(0, 512); yB = yw.f32(512, 512); yC = yw.f32(1024, 512); yD = yw.f32(1536, 512)
    GC0 = 0.7978845608028654
    GC1 = 0.044715

    def gelu_bank(bk, bkk, nrows, out_ap, out_key):
        pr = slice(0, nrows)
        P.act(lambda e: e.activation(out=yA[pr], in_=bk[pr], func=AF.Copy, scale=0.5), reads=[bkk], writes=[yw.k("a")])
        P.act(lambda e: e.activation(out=yB[pr], in_=bk[pr], func=AF.Square), reads=[bkk], writes=[yw.k("b"), yw.k("b2")])
        P.act(lambda e: e.activation(out=yD[pr], in_=yB[pr], func=AF.Identity, scale=GC1, bias=1.0), reads=[yw.k("b")], writes=[yw.k("d")])
        tt("dve", yB[pr], yD[pr], yA[pr], ALU.mult, [yw.k("d"), yw.k("a")], [yw.k("b2")])
        P.act(lambda e: e.activation(out=yC[pr], in_=yB[pr], func=AF.Tanh, scale=2.0 * GC0), reads=[yw.k("b2")], writes=[yw.k("c")])
        g_ = out_ap.shape[1]
        t_ = out_ap.shape[2]
        P.dve(lambda e: e.tensor_scalar(out=yC[pr], in0=yC[pr], scalar1=1.0, scalar2=None, op0=ALU.add), reads=[yw.k("c")], writes=[yw.k("c")])
        tt("dve", out_ap, yC[pr].rearrange("p (g t c) -> p g t c", g=g_, t=t_), yA[pr].rearrange("p (g t c) -> p g t c", g=g_, t=t_),
           ALU.mult, [yw.k("c"), yw.k("a")], [out_key])

    for st in range(2):
        for q8 in range(8):
            bk, bkk = PS(q8 % 2)

            def f(e, bk=bk, st=st, q8=q8):
                last = None
                for j in range(4):
                    g = 4 * q8 + j
                    gq = g % 16
                    o_ = bk[:, j * 128:(j + 1) * 128]
                    e.matmul(o_, lhsT=U_own[:, st, g, :], rhs=W_M[:, g, :], start=True, stop=False)
                    e.matmul(o_, lhsT=Xprev[:, st, gq, 0, :], rhs=W_Q[:, g, 0, :], start=False, stop=False)
                    last = e.matmul(o_, lhsT=Xprev[:, st, gq, 1, :], rhs=W_Q[:, g, 1, :], start=False, stop=True)
                return last
            P.pe(f, reads=[uown_r.k(st), xp_r.k(st), xp_r.k(st, "im"), wm.k(), wq.k()], writes=[bkk])
            out_ap = gl_tm8[:, st, :, 4 * q8:4 * q8 + 4, :].rearrange("p t g c -> p g t c")
            gelu_bank(bk, bkk, 128, out_ap, gl_r.k(st))
    for q8 in range(4):
        bk, bkk = PS(q8 % 2)

        def f(e, bk=bk, q8=q8):
            last = None
            for j in range(8):
                g = 8 * q8 + j
                gq = g % 16
                o_ = bk[0:16, j * 64:(j + 1) * 64]
                e.matmul(o_, lhsT=U_s[:, 1, g, :], rhs=W_M[:, g, 0:64], start=True, stop=False)
                e.matmul(o_, lhsT=h0b[:, gq, 0, :], rhs=W_Q[:, g, 0, 0:64], start=False, stop=False)
                last = e.matmul(o_, lhsT=h0b[:, gq, 1, :], rhs=W_Q[:, g, 1, 0:64], start=False, stop=True)
            return last
        P.pe(f, reads=[us_r.k(), h0_r.k("b"), wm.k(), wq.k()], writes=[bkk])
        out_ap = gl_s[0:16, :, 8 * q8:8 * q8 + 8, :].rearrange("p t g c -> p g t c")
        gelu_bank(bk, bkk, 16, out_ap, gl_r.k("s"))
    for _r in (sm, wq, wm, xp_r, uown_r, us_r, h0_r, ss_r, yw):
        AR.free(_r)
    if debug:
        dbg["gl"] = (gl_r, [128, 4096 + 1024], BF16)

    x1_r = AR.alloc("x1", 17 * 1024)
    x1 = x1_r.f32().rearrange("p (t d) -> p t d", t=17)
    wgl_r = AR.alloc("wglu", 4 * 512 // 2)
    W_GLU = wgl_r.bf().rearrange("p (k c) -> p k c", k=4)
    wo_r = AR.alloc("wout", 8 * 1024 // 2)
    W_OUT = wo_r.bf().rearrange("p (k c) -> p k c", k=8)
    wst2 = AR.alloc("wstage2", 12 * 1024)
    for kt in range(4):
        P.dma(wst2.f32(1024 * kt, 512), wglu_d[kt * 128:(kt + 1) * 128, :], writes=[wst2.k(kt)])
    for kt in range(8):
        P.dma(wst2.f32(1024 * (4 + kt), 1024), wout_d[kt * 128:(kt + 1) * 128, :], writes=[wst2.k(4 + kt)])
    for kt in range(4):
        evac(W_GLU[:, kt, :], wst2.f32(1024 * kt, 512), [wst2.k(kt)], [wgl_r.k()])
    for kt in range(8):
        evac(W_OUT[:, kt, :], wst2.f32(1024 * (4 + kt), 1024), [wst2.k(4 + kt)], [wo_r.k()])
    AR.free(wst2)
    cw = AR.alloc("cwork", 2048 + 2048 + 512 + 512 + 256 + 1024 + 64 + 2048 + 1024 + 16 + 1024 + 16 + 512)
    glT = cw.bf(0, 2048).rearrange("p (k n) -> p k n", k=4)
    oT = cw.f32(2048, 2048).rearrange("p (k n) -> p k n", k=4)
    ctmp = cw.f32(4096, 512)
    rstd_t = cw.f32(4608, 512)
    osq = cw.bf(5120, 256)
    mTs = cw.bf(5376, 1024).rearrange("p (k n) -> p k n", k=4)
    ones_b = cw.bf(6400, 64)
    xst = cw.f32(6464, 2048)
    osq4 = cw.bf(8512, 1024).rearrange("p (k n) -> p k n", k=4)
    rtok = cw.f32(9536, 16)
    mTs_l = [mTs, cw.bf(9552, 1024).rearrange("p (k n) -> p k n", k=4)]
    rtok_l = [rtok, cw.f32(10576, 16)]
    ctmp_l = [ctmp, cw.f32(10592, 512)]
    P.pool(lambda e: e.memset(ones_b, 1.0), writes=[cw.k("ones")])
    P.dve(lambda e: e.tensor_scalar(out=gsb[:, 4:8], in0=gsb[:, 4:8], scalar1=0.5, scalar2=None, op0=ALU.mult), reads=[gains.k("bg")], writes=[gains.k("bg")])

    def ssm_tail(ntok, tok0, c0, bi=0):
        mTs = mTs_l[bi]
        rtok = rtok_l[bi]
        for ko in range(4):
            bk, bkk = PS(2 + ko % 2)
            ct = ctmp_l[ko % 2]
            kct = cw.k("ctmp", ko % 2)

            def f(e, bk=bk, ko=ko):
                last = None
                for kt in range(4):
                    last = e.matmul(bk[:, 0:ntok], lhsT=W_GLU[:, kt, ko * 128:(ko + 1) * 128], rhs=glT[:, kt, c0:c0 + ntok], start=(kt == 0), stop=(kt == 3))
                return last
            P.pe(f, reads=[wgl_r.k(), cw.k("glT")], writes=[bkk])
            P.act(lambda e, bk=bk, ko=ko, ct=ct: e.activation(out=ct[:, 0:ntok], in_=bk[:, 0:ntok], func=AF.Tanh, scale=0.5, bias=gsb[:, 4 + ko:5 + ko]),
                  reads=[bkk, gains.k("bg")], writes=[kct])
            P.dve(lambda e, ct=ct: e.tensor_scalar(out=ct[:, 0:ntok], in0=ct[:, 0:ntok], scalar1=0.5, scalar2=0.5, op0=ALU.mult, op1=ALU.add),
                  reads=[kct], writes=[kct])
            tt("dve", oT[:, ko, 0:ntok], ct[:, 0:ntok], glT[:, ko, c0:c0 + ntok], ALU.mult, [kct, cw.k("glT")], [cw.k("oT", ko)])
        for ko in range(4):
            P.act(lambda e, ko=ko: e.activation(out=osq4[:, ko, 0:ntok], in_=oT[:, ko, 0:ntok], func=AF.Square), reads=[cw.k("oT", ko)], writes=[cw.k("osq", ko)])
            P.dve(lambda e, ko=ko: e.tensor_scalar(out=mTs[:, ko, 0:ntok], in0=oT[:, ko, 0:ntok], scalar1=gsb[:, ko:ko + 1], scalar2=None, op0=ALU.mult),
                  reads=[cw.k("oT", ko), gains.k("gs")], writes=[cw.k("mTs", bi)])
        bs, bsk = PS(4)
        ntl = (ntok + 127) // 128

        def frs(e, bs=bs):
            last = None
            for tl in range(ntl):
                nr = min(128, ntok - 128 * tl)
                for ko in range(4):
                    last = e.matmul(bs[0:nr, tl:tl + 1], lhsT=osq4[:, ko, 128 * tl:128 * tl + nr], rhs=ones_b[:, 0:1], start=(ko == 0), stop=(ko == 3))
            return last
        P.pe(frs, reads=[cw.k("osq", k_) for k_ in range(4)] + [cw.k("ones")], writes=[bsk])
        nr0 = min(128, ntok)
        P.dve(lambda e: e.tensor_scalar(out=rtok[0:nr0, 0:ntl], in0=bs[0:nr0, 0:ntl], scalar1=1.0 / 512, scalar2=EPS, op0=ALU.mult, op1=ALU.add),
              reads=[bsk], writes=[cw.k("rtok", bi)])
        P.pool(lambda e: e.tensor_tensor(out=rtok[0:nr0, 0:ntl], in0=rtok[0:nr0, 0:ntl], in1=mhalf[0:nr0].to_broadcast([nr0, ntl]), op=ALU.pow),
               reads=[cw.k("rtok", bi), pers.k("mh")], writes=[cw.k("rtok", bi)])

    def out_proj_tile(nrows, tok0, mcol0, tile_idx, bi=0):
        mTs = mTs_l[bi]
        rtok = rtok_l[bi]
        pr = slice(0, nrows)
        xs_ = xst[:, 1024 * (tile_idx % 2):1024 * (tile_idx % 2) + 1024]
        tl = mcol0 // 128
        P.dma(xs_[pr], xo[tok0:tok0 + nrows, :], writes=[cw.k("xst", tile_idx % 2)])
        for hf in range(2):
            bka, bkak = PS(5 + hf)
            bks, bksk = PS(7 if hf == 0 else 1)

            def f(e, bka=bka, bks=bks, hf=hf):
                last = None
                for kt in range(4):
                    e.matmul(bka[pr, :], lhsT=mT_att[:, kt, tok0:tok0 + nrows], rhs=W_OUT[:, kt, 512 * hf:512 * hf + 512], start=(kt == 0), stop=(kt == 3))
                for kt in range(4):
                    last = e.matmul(bks[pr, :], lhsT=mTs[:, kt, mcol0:mcol0 + nrows], rhs=W_OUT[:, 4 + kt, 512 * hf:512 * hf + 512], start=(kt == 0), stop=(kt == 3))
                return last
            P.pe(f, reads=[mT_r.k(), cw.k("mTs", bi), wo_r.k()], writes=[bkak, bksk])
            xo_ = x1[pr, tile_idx, 512 * hf:512 * hf + 512]
            P.dve(lambda e, bks=bks, hf=hf, xo_=xo_: e.scalar_tensor_tensor(out=xo_, in0=bks[pr, :], scalar=rtok[pr, tl:tl + 1], in1=xs_[pr, 512 * hf:512 * hf + 512],
                                                                      op0=ALU.mult, op1=ALU.add),
                  reads=[bksk, cw.k("rtok", bi), cw.k("xst", tile_idx % 2)], writes=[x1_r.k(tile_idx)])
            tt("dve", xo_, bka[pr, :], xo_, ALU.add, [bkak, x1_r.k(tile_idx)], [x1_r.k(tile_idx)])

    for st in range(2):
        for kt in range(4):
            for a in range(2):
                bk, bkk = PS(a)

                def f(e, bk=bk, kt=kt, a=a, st=st):
                    last = None
                    for j in range(4):
                        t = 4 * a + j
                        last = e.matmul(bk[:, j * 128:(j + 1) * 128], lhsT=gl_tm8[:, st, t, 8 * kt:8 * kt + 8, :].rearrange("p g c -> p (g c)"), rhs=ident_b,
                                        start=True, stop=True)
                    return last
                P.pe(f, reads=[gl_r.k(st), pers.k("idb")], writes=[bkk])
                evac(glT[:, kt, :].rearrange("p (k t) -> p t k", t=8)[:, 4 * a:4 * a + 4, :], bk.rearrange("p (t k) -> p t k", t=4), [bkk], [cw.k("glT")])
        for hf in range(2):
            gi_ = 2 * st + hf
            ssm_tail(512, st * ST + hf * 512, hf * 512, gi_ % 2)
            if gi_ > 0:
                pst, phf = (gi_ - 1) // 2, (gi_ - 1) % 2
                for tl in range(4):
                    tok0 = pst * ST + phf * 512 + tl * 128
                    out_proj_tile(128, tok0, tl * 128, tok0 // 128, (gi_ - 1) % 2)
    for tl in range(4):
        tok0 = 1 * ST + 512 + tl * 128
        out_proj_tile(128, tok0, tl * 128, tok0 // 128, 1)
    bk, bkk = PS(0)

    def fsT(e, bk=bk):
        last = None
        for kt in range(4):
            for t in range(4):
                c = (kt * 4 + t) * 16
                last = e.matmul(bk[:, c:c + 16], lhsT=gl_s[0:16, t, 8 * kt:8 * kt + 8, :].rearrange("p g c -> p (g c)"), rhs=ident_b[0:16, 0:16], start=True, stop=True)
        return last
    P.pe(fsT, reads=[gl_r.k("s"), pers.k("idb")], writes=[bkk])
    evac(glT[:, :, 0:64].rearrange("p k (b t) -> p k t b", t=4), bk[:, 0:256].rearrange("p (k t b) -> p k t b", k=4, t=4), [bkk], [cw.k("glT")])
    ssm_tail(64, NP, 0, 0)
    out_proj_tile(64, NP, 0, 16, 0)
    AR.free(cw)
    AR.free(gl_r)
    AR.free(mT_r)
    AR.free(wgl_r)
    AR.free(wo_r)
    if debug:
        dbg["x1"] = (x1_r, [128, 17 * 1024], F32)

    AR.free(gains)
    g2 = AR.alloc("gains2", 2048)
    g_mlp = g2.f32(0, 1024)
    g_fin = g2.f32(1024, 1024)
    P.dma(g_mlp, bc_row(gmlp_d, 1024), writes=[g2.k("mlp")])
    P.dma(g_fin, bc_row(gfin_d, 1024), writes=[g2.k("fin")])
    hm_r = AR.alloc("hmT", 8 * NT // 2)
    hmT = hm_r.bf().rearrange("p (k n) -> p k n", k=8)
    dw = AR.alloc("dwork", 512 + 256)
    hmb = dw.bf(0, 512)
    hmb2_r = AR.alloc("hmb2", 512)
    dst_ = dw.f32(512, 256)

    def rms_stats(tile_idx, nrows, junk_bf, slot):
        pr = slice(0, nrows)
        sc = dst_[:, 4 * (slot % 64):4 * (slot % 64) + 4]
        ks = dw.k("st", slot % 64)
        P.act(lambda e: e.activation(out=junk_bf[pr], in_=x1[pr, tile_idx, :], func=AF.Square, accum_out=sc[pr, 0:1]),
              reads=[x1_r.k(tile_idx)], writes=[dw.k("hmb"), ks])
        P.dve(lambda e: e.tensor_scalar(out=sc[pr, 1:2], in0=sc[pr, 0:1], scalar1=1.0 / 1024, scalar2=EPS, op0=ALU.mult, op1=ALU.add), reads=[ks], writes=[ks + ("a",)])
        P.pool(lambda e: e.tensor_tensor(out=sc[pr, 2:3], in0=sc[pr, 1:2], in1=mhalf[pr], op=ALU.pow), reads=[ks + ("a",), pers.k("mh")], writes=[ks + ("b",)])
        return sc, ks + ("b",)

    hmbs = [dw.bf(0, 512), hmb2_r.bf(0, 512)]

    def hm_front(ti):
        nrows = 128 if ti < 16 else 64
        pr = slice(0, nrows)
        hb_ = hmbs[ti % 2]
        kh_ = ("hmbk", ti % 2)
        sc = dst_[:, 4 * (ti % 64):4 * (ti % 64) + 4]
        ks = dw.k("st", ti % 64)
        P.act(lambda e: e.activation(out=hb_[pr], in_=x1[pr, ti, :], func=AF.Square, accum_out=sc[pr, 0:1]),
              reads=[x1_r.k(ti)], writes=[kh_, ks])
        P.dve(lambda e: e.tensor_scalar(out=sc[pr, 1:2], in0=sc[pr, 0:1], scalar1=1.0 / 1024, scalar2=EPS, op0=ALU.mult, op1=ALU.add), reads=[ks], writes=[ks + ("a",)])
        P.pool(lambda e: e.tensor_tensor(out=sc[pr, 2:3], in0=sc[pr, 1:2], in1=mhalf[pr], op=ALU.pow), reads=[ks + ("a",), pers.k("mh")], writes=[ks + ("b",)])
        P.dve(lambda e: e.scalar_tensor_tensor(out=hb_[pr], in0=x1[pr, ti, :], scalar=sc[pr, 2:3], in1=g_mlp[pr], op0=ALU.mult, op1=ALU.mult),
              reads=[x1_r.k(ti), ks + ("b",), g2.k("mlp")], writes=[kh_])

    def hm_back(ti):
        nrows = 128 if ti < 16 else 64
        pr = slice(0, nrows)
        hb_ = hmbs[ti % 2]
        kh_ = ("hmbk", ti % 2)
        for half in range(2):
            bk, bkk = PS(half)

            def f(e, bk=bk, half=half):
                last = None
                for j in range(4):
                    kt = 4 * half + j
                    last = e.matmul(bk[:, j * 128:j * 128 + nrows], lhsT=hb_[pr, kt * 128:(kt + 1) * 128], rhs=ident_b[pr, pr], start=True, stop=True)
                return last
            P.pe(f, reads=[kh_, pers.k("idb")], writes=[bkk])
            evac(hmT[:, 4 * half:4 * half + 4, 128 * ti:128 * ti + nrows], bk.rearrange("p (j n) -> p j n", j=4)[:, :, 0:nrows], [bkk], [hm_r.k()])

    NCH = 8
    FT = 4
    wup_rs = [AR.alloc("wup%d" % i, 8 * 512 // 2) for i in range(2)]
    wdn_rs = [AR.alloc("wdn%d" % i, FT * 1024 // 2) for i in range(2)]
    wst3 = AR.alloc("wstage3", 3 * 1024)
    hT_rs = [AR.alloc("hTa", FT * 512 // 2), AR.alloc("hTb", FT * 512 // 2)]
    rl_r = AR.alloc("relu", 2 * 512)
    W_UP = [wup_rs[i].bf().rearrange("p (k c) -> p k c", k=8) for i in range(2)]
    W_DN = [wdn_rs[i].bf().rearrange("p (k c) -> p k c", k=FT) for i in range(2)]
    hT = [hT_rs[i].bf().rearrange("p (k n) -> p k n", k=FT) for i in range(2)]

    class _WK:
        def __init__(self, rs):
            self.rs = rs

        def k(self, b):
            return self.rs[b].k()
    wup_r = _WK(wup_rs)
    wdn_r = _WK(wdn_rs)
    sctr = {"n": 0}

    NSTG = 3

    def chunk_jobs(c):
        b = c % 2
        jobs = []
        for kt in range(8):
            st_ = {}

            def d(kt=kt, st_=st_):
                i = sctr["n"] % NSTG
                sctr["n"] += 1
                st_["i"] = i
                P.dma(wst3.f32(1024 * i, 512), wup_d[kt * 128:(kt + 1) * 128, 512 * c:512 * c + 512], writes=[wst3.k(i)])

            def cst(kt=kt, st_=st_):
                i = st_["i"]
                evac(W_UP[b][:, kt, :], wst3.f32(1024 * i, 512), [wst3.k(i)], [wup_r.k(b)], eng="act")
            jobs.append((d, cst))
        for ft in range(FT):
            st_ = {}

            def d(ft=ft, st_=st_):
                i = sctr["n"] % NSTG
                sctr["n"] += 1
                st_["i"] = i
                r0 = 512 * c + 128 * ft
                P.dma(wst3.f32(1024 * i, 1024), wdn_d[r0:r0 + 128, :], writes=[wst3.k(i)])

            def cst(ft=ft, st_=st_):
                i = st_["i"]
                evac(W_DN[b][:, ft, :], wst3.f32(1024 * i, 1024), [wst3.k(i)], [wdn_r.k(b)], eng="act")
            jobs.append((d, cst))
        return jobs

    class JobRunner:
        LAG = 2

        def __init__(self):
            self.q = []
            self.nd = 0
            self.nc_ = 0

        def add(self, jobs):
            self.q += jobs

        def step(self):
            while self.nd < len(self.q) and self.nd < self.nc_ + self.LAG:
                self.q[self.nd][0]()
                self.nd += 1
            if self.nc_ < self.nd:
                self.q[self.nc_][1]()
                self.nc_ += 1

        def pending(self):
            return self.nc_ < len(self.q)

    groups = [(512 * i, 512) for i in range(4)] + [(NP, 64)]
    ys_r = AR.alloc("yst", 2048)
    ysts = [ys_r.f32(0, 1024), ys_r.f32(1024, 1024)]
    junk2_r = AR.alloc("junk2", 512)

    def final_tiles(tis):
        pre = rms_stats(tis[0], 128 if tis[0] < 16 else 64, junk2_r.bf(), 32 + tis[0])
        for n_, ti in enumerate(tis):
            nrows = 128 if ti < 16 else 64
            pr = slice(0, nrows)
            sc, kb_ = pre
            if n_ + 1 < len(tis):
                t2 = tis[n_ + 1]
                pre = rms_stats(t2, 128 if t2 < 16 else 64, junk2_r.bf(), 32 + t2)
            yst = ysts[ti % 2]
            P.dve(lambda e, ti=ti, sc=sc, pr=pr, yst=yst: e.scalar_tensor_tensor(out=yst[pr], in0=x1[pr, ti, :], scalar=sc[pr, 2:3], in1=g_fin[pr], op0=ALU.mult, op1=ALU.mult),
                  reads=[x1_r.k(ti), kb_, g2.k("fin")], writes=[ys_r.k(ti % 2)])
            P.dma(y_o[128 * ti:128 * ti + nrows, :], yst[pr], reads=[ys_r.k(ti % 2)])
    JR = JobRunner()
    JR.add(chunk_jobs(0))
    JR.step()
    hm_front(0)
    for ti in range(17):
        if ti + 1 < 17:
            hm_front(ti + 1)
        hm_back(ti)
        JR.step()
    while JR.pending():
        JR.step()
    gi = 0
    for c in range(NCH):
        if c + 1 < NCH:
            JR.add(chunk_jobs(c + 1))
        b = c % 2
        for (tok0, ntok) in groups:
            hb = gi % 2
            gi += 1
            for ft in range(FT):
                bk, bkk = PS(ft % 2)

                def f(e, bk=bk, ft=ft, b=b, tok0=tok0, ntok=ntok):
                    last = None
                    for kt in range(8):
                        last = e.matmul(bk[:, 0:ntok], lhsT=W_UP[b][:, kt, ft * 128:(ft + 1) * 128], rhs=hmT[:, kt, tok0:tok0 + ntok], start=(kt == 0), stop=(kt == 7))
                    return last
                P.pe(f, reads=[wup_r.k(b), hm_r.k()], writes=[bkk])
                rl = rl_r.f32(512 * (ft % 2), 512)
                P.act(lambda e, bk=bk, rl=rl, ntok=ntok: e.activation(out=rl[:, 0:ntok], in_=bk[:, 0:ntok], func=AF.Relu), reads=[bkk], writes=[rl_r.k(ft % 2)])
                tt("dve", hT[hb][:, ft, 0:ntok], rl[:, 0:ntok], rl[:, 0:ntok], ALU.mult, [rl_r.k(ft % 2)], [hT_rs[hb].k()])
                JR.step()
            ntile = (ntok + 127) // 128
            for tl in range(ntile):
                nrows = min(128, ntok - 128 * tl)
                pr = slice(0, nrows)
                ti = (tok0 + 128 * tl) // 128
                for hf in range(2):
                    bk, bkk = PS(2 + 2 * (tl % 2) + hf)

                    def f(e, bk=bk, hf=hf, hb=hb, b=b, tl=tl, nrows=nrows, pr=pr):
                        last = None
                        for ft in range(FT):
                            last = e.matmul(bk[pr, :], lhsT=hT[hb][:, ft, 128 * tl:128 * tl + nrows], rhs=W_DN[b][:, ft, 512 * hf:512 * hf + 512], start=(ft == 0), stop=(ft == FT - 1))
                        return last
                    P.pe(f, reads=[hT_rs[hb].k(), wdn_r.k(b)], writes=[bkk])
                    tt("dve", x1[pr, ti, 512 * hf:512 * hf + 512], bk[pr, :], x1[pr, ti, 512 * hf:512 * hf + 512], ALU.add, [bkk, x1_r.k(ti)], [x1_r.k(ti)])
            if c == NCH - 1:
                final_tiles([(tok0 + 128 * tl_) // 128 for tl_ in range(ntile)])

    if debug:
        import os
        want = os.environ.get("DBG", "").split(",")
        keep = set(v[0].name for k, v in dbg.items() if k in want) | {"pers"}
        for nm in list(AR.live.keys()):
            if nm not in keep:
                o_, n_ = AR.live[nm]
                AR.free(Region(AR, nm, o_, n_))
        for name, (reg, shape, dt) in dbg.items():
            if name not in want or reg.name not in AR.live:
                continue
            o = nc.dram_tensor("dbg_" + name, list(shape), F32, kind="ExternalOutput").ap()
            if dt == BF16:
                tmp = AR.alloc("dbgtmp_" + name, shape[1])
                P.dve(lambda e, tmp=tmp, reg=reg: e.tensor_copy(out=tmp.f32(), in_=reg.bf()), reads=[reg.k()], writes=[tmp.k()])
                P.dma(o, tmp.f32(), reads=[tmp.k()])
                AR.free(tmp)
            else:
                P.dma(o, reg.f32(0, shape[1]), reads=[reg.k()])
    P.emit()
    return nc


def _bucket(n):
    n = np.maximum(n, 0)
    nf = np.maximum(n, 16).astype(np.float32)
    large = 16 + (np.log(nf / np.float32(16)) / np.float32(math.log(8.0)) * np.float32(16)).astype(np.int32)
    large = np.minimum(large, 31)
    return np.where(n < 16, n, large)


def _static_consts():
    c = np.zeros((128, 160), np.float32)
    c[:, 0:9] = np.arange(9)
    c[:, 16:145] = np.arange(129)
    c[:, 146] = 7 - (np.arange(128) // 16)
    oh = np.zeros((33, 384), np.float32)
    for j in range(384):
        d = j - 127
        if 0 <= d <= 127:
            oh[int(_bucket(np.array(d))), j] = 1.0
        else:
            oh[32, j] = NEG
    mb = np.full((64, 64), NEG, np.float32)
    jm = np.zeros((128, 192), np.float32)
    jm[np.arange(128), 127 - np.arange(128)] = 1.0
    for b in range(16):
        mb[4 * b:4 * b + 4, 4 * b:4 * b + 4] = 0.0
        for t_ in range(4):
            jm[3 - t_, 128 + 4 * b + t_] = 1.0
    return c, oh, mb, jm


def make_core_inputs(inputs, c):
    f = lambda a: np.ascontiguousarray(np.asarray(a), dtype=np.float32)
    seq, m = c // 4, c % 4
    xpr = np.asarray(inputs["x_prompt"])
    xsm = np.asarray(inputs["x_sample"])
    cst, oh, mb, jm = _static_consts()
    d = {}
    d["xo"] = np.concatenate([xpr[seq, 2048 * m:2048 * (m + 1)], xsm[16 * c:16 * c + 16].reshape(64, 1024)], 0)
    xp = np.zeros((NLITE * ST, D), np.float32)
    if m > 0:
        xp[NLITE * ST - 2048 * m:] = xpr[seq, 0:2048 * m]
    d["xp"] = xp
    d["cache_k"] = np.asarray(inputs["cache_k"])[0, 16 * c:16 * c + 16].reshape(16, 128, 128)
    d["cache_v"] = np.asarray(inputs["cache_v"])[0, 16 * c:16 * c + 16].reshape(16, 128, 128)
    d["st_re"] = np.asarray(inputs["state_ssm_re"])[0, 16 * c:16 * c + 16].reshape(16, 2048)
    d["st_im"] = np.asarray(inputs["state_ssm_im"])[0, 16 * c:16 * c + 16].reshape(16, 2048)
    d["rel_bias"] = inputs["rel_bias"]
    d["norm_mix"] = inputs["norm_mix"]
    d["w_in"] = np.asarray(inputs["w_in"])[0]
    d["sinks"] = inputs["attn_sinks"]
    d["a_re"] = np.asarray(inputs["ssm_a_re"])[0]
    d["a_im"] = np.asarray(inputs["ssm_a_im"])[0]
    d["log_step"] = inputs["ssm_log_step"]
    d["b_re"] = np.asarray(inputs["ssm_b_re"])[0]
    d["b_im"] = np.asarray(inputs["ssm_b_im"])[0]
    d["c_re"] = np.asarray(inputs["ssm_c_re"])[0]
    d["c_im"] = np.asarray(inputs["ssm_c_im"])[0]
    d["ssm_d"] = np.asarray(inputs["ssm_d"])[0]
    d["w_glu"] = np.asarray(inputs["w_glu"])[0]
    d["b_glu"] = inputs["b_glu"]
    d["norm_attn"] = inputs["norm_attn_out"]
    d["norm_ssm"] = inputs["norm_ssm_out"]
    d["w_out"] = np.asarray(inputs["w_out"])[0]
    d["norm_mlp"] = inputs["norm_mlp"]
    d["w_up"] = np.asarray(inputs["w_up"])[0]
    d["w_down"] = np.asarray(inputs["w_down"])[0]
    d["norm_final"] = np.asarray(inputs["norm_final"]).reshape(1, D)
    d["consts"] = cst
    d["onehot"] = oh
    d["maskB"] = mb
    d["jmat"] = jm
    d["halo_mask"] = np.full((128, 1), NEG if m == 0 else 0.0, np.float32)
    return {k: f(v) for k, v in d.items()}


_NC_CACHE = {}


def kernel(**inputs):
    if "nc" not in _NC_CACHE:
        _NC_CACHE["nc"] = build_program(debug=False)
    nc = _NC_CACHE["nc"]
    in_maps = [make_core_inputs(inputs, c) for c in range(NCORES)]
    res = run_bass_kernel_spmd(nc, in_maps, core_ids=list(range(NCORES)))
    R = res.results
    y_prompt = np.zeros((2, 8192, D), np.float32)
    y_sample = np.zeros((128, 4, D), np.float32)
    nkp = np.zeros((1, 2, 128, 2, 64), np.float32)
    nvp = np.zeros((1, 2, 128, 2, 64), np.float32)
    srp = np.zeros((1, 2, 32, 64), np.float32)
    sip = np.zeros((1, 2, 32, 64), np.float32)
    nks = np.zeros((1, 128, 128, 2, 64), np.float32)
    nvs = np.zeros((1, 128, 128, 2, 64), np.float32)
    srs = np.zeros((1, 128, 32, 64), np.float32)
    sis = np.zeros((1, 128, 32, 64), np.float32)
    for c in range(NCORES):
        seq, m = c // 4, c % 4
        r = R[c]
        y = np.asarray(r["y"])
        y_prompt[seq, 2048 * m:2048 * (m + 1)] = y[0:2048]
        y_sample[16 * c:16 * c + 16] = y[2048:].reshape(16, 4, D)
        if m == 3:
            nkp[0, seq] = np.asarray(r["nk_p"]).reshape(128, 2, 64)
            nvp[0, seq] = np.asarray(r["nv_p"]).reshape(128, 2, 64)
            srp[0, seq] = np.asarray(r["sp_re"])
            sip[0, seq] = np.asarray(r["sp_im"])
        nks[0, 16 * c:16 * c + 16] = np.asarray(r["nk_s"]).reshape(16, 128, 2, 64)
        nvs[0, 16 * c:16 * c + 16] = np.asarray(r["nv_s"]).reshape(16, 128, 2, 64)
        srs[0, 16 * c:16 * c + 16] = np.asarray(r["ss_re"]).reshape(16, 32, 64)
        sis[0, 16 * c:16 * c + 16] = np.asarray(r["ss_im"]).reshape(16, 32, 64)
    return (y_prompt, y_sample, nkp, nvp, srp, sip, nks, nvs, srs, sis)
```
